# Optimizing a Trainium2 kernel written in Bass

```python
import jax, jax.numpy as jnp
from jax import lax
import numpy as np

D_MODEL = 1024
BATCH = 8
SEQ = 4096
DEPTH = 1

GRID_W = 64
CTX_LEN = 256
D_HGRN = D_MODEL // 2
HGRN_DIM = 128
HGRN_HEADS = D_HGRN // HGRN_DIM
D_FOURIER = D_MODEL - D_HGRN
FOURIER_DIM = 128
FOURIER_GROUPS = D_FOURIER // FOURIER_DIM
D_MIX = D_HGRN + D_FOURIER
N_HGRN_PROJ = 5
D_IN = N_HGRN_PROJ * D_HGRN + D_FOURIER
D_FF = ((8 * D_MODEL // 3 + 127) // 128) * 128
CHUNK = 64
EPS = 1e-6

kernel_name = "hybrid_hgrn2_fnet_convffn_dit_block"


def rms_norm(x, g):
    xf = x.astype(jnp.float32)
    y = xf * lax.rsqrt(jnp.mean(xf * xf, axis=-1, keepdims=True) + EPS)
    return (y * g.astype(jnp.float32)).astype(x.dtype)


def modulate(h, shift, scale):
    return h * (1.0 + scale) + shift


def forget_gate(logits, lb):
    f = lb + (1.0 - lb) * jax.nn.sigmoid(logits)
    return 1.0 - f, jnp.log(f)


def gla_chunked(q, k, v, logf, s0):
    B, L, H, DK = q.shape
    n = L // CHUNK
    r = lambda t: t.reshape(B, n, CHUNK, H, t.shape[-1])
    q, k, v, logf = r(q), r(k), r(v), r(logf)
    b = jnp.cumsum(logf, axis=2)
    b_ref = b[:, :, CHUNK // 2:CHUNK // 2 + 1]
    b_last = b[:, :, -1:]
    q_t = q * jnp.exp(b - b_ref)
    k_t = k * jnp.exp(b_ref - b)
    scores = jnp.einsum('bnthk,bnshk->bnhts', q_t, k_t)
    causal_in_scan = jnp.tril(jnp.ones((CHUNK, CHUNK), dtype=bool))
    scores = jnp.where(causal_in_scan, scores, 0.0)
    o_intra = jnp.einsum('bnhts,bnshv->bnthv', scores, v)
    q_inter = q * jnp.exp(b)
    u = jnp.einsum('bnshk,bnshv->nbhkv', k * jnp.exp(b_last - b), v)
    g = jnp.transpose(jnp.exp(b_last[:, :, 0]), (1, 0, 2, 3))[..., None]

    def step(s, inp):
        g_n, u_n = inp
        return g_n * s + u_n, s

    s_final, s_prev = lax.scan(step, s0, (g, u))
    o_inter = jnp.einsum('bnthk,nbhkv->bnthv', q_inter, s_prev)
    o = (o_intra + o_inter).reshape(B, L, H, v.shape[-1])
    return o, s_final


def token_mixer(h, w_in, lb_f, lb_b, g_norm, s0_f, s0_b):
    B, L, _ = h.shape
    z = h @ w_in
    zq, zf_f, zf_b, zi, zg, zu = jnp.split(z, [D_HGRN * j for j in range(1, N_HGRN_PROJ + 1)], axis=-1)
    heads = lambda t: t.astype(jnp.float32).reshape(B, L, HGRN_HEADS, HGRN_DIM)
    q = heads(jax.nn.silu(zq))
    v = heads(zi)
    k_f, lf_f = forget_gate(heads(zf_f), lb_f.reshape(HGRN_HEADS, HGRN_DIM))
    k_b, lf_b = forget_gate(heads(zf_b), lb_b.reshape(HGRN_HEADS, HGRN_DIM))
    flip = lambda t: jnp.flip(t, axis=1)
    o_f, s_f = gla_chunked(q, k_f, v, lf_f, s0_f)
    o_b, s_b = gla_chunked(flip(q), flip(k_b), flip(v), flip(lf_b), s0_b)
    o = rms_norm(o_f + flip(o_b), g_norm) * jax.nn.silu(heads(zg))
    o = o.reshape(B, L, D_HGRN).astype(h.dtype)
    uf = zu.astype(jnp.float32).reshape(B, L, FOURIER_GROUPS, FOURIER_DIM)
    y_four = jnp.fft.fft2(uf, axes=(1, 3), norm='ortho').real
    y_four = y_four.reshape(B, L, D_FOURIER).astype(h.dtype)
    return jnp.concatenate([o, y_four], axis=-1), s_f, s_b


def conv_ffn(h, w_up, w_dw, b_dw, w_down, rows, cols):
    B, L, _ = h.shape
    a, u = jnp.split(h @ w_up, 2, axis=-1)
    a = lax.conv_general_dilated(
        a.reshape(B, rows, cols, D_FF), w_dw[:, :, None, :].astype(a.dtype),
        window_strides=(1, 1), padding=((1, 1), (1, 1)),
        dimension_numbers=('NHWC', 'HWIO', 'NHWC'), feature_group_count=D_FF)
    a = a.reshape(B, L, D_FF) + b_dw
    return (jax.nn.gelu(a) * u) @ w_down


def setup_inputs(seed: int = 0) -> dict:
    key = jax.random.key(seed)
    ks = jax.random.split(key, 20)
    nrm = lambda k, shape, s: jax.random.normal(k, shape, jnp.float32) * s
    return {
        "x": nrm(ks[0], (BATCH, SEQ, D_MODEL), 1.0),
        "c": nrm(ks[1], (BATCH, D_MODEL), 1.0),
        "ctx": nrm(ks[2], (BATCH, CTX_LEN, D_MODEL), 1.0),
        "c_ctx": nrm(ks[3], (D_MODEL,), 1.0),
        "w_mod": nrm(ks[4], (DEPTH, D_MODEL, 6 * D_MODEL), 0.5 * D_MODEL ** -0.5),
        "b_mod": nrm(ks[5], (DEPTH, 6 * D_MODEL), 0.01),
        "norm1": 1.0 + nrm(ks[6], (DEPTH, D_MODEL), 0.01),
        "norm2": 1.0 + nrm(ks[7], (DEPTH, D_MODEL), 0.01),
        "w_in": nrm(ks[8], (DEPTH, D_MODEL, D_IN), D_MODEL ** -0.5),
        "lb_fwd": nrm(ks[9], (DEPTH + 1, D_HGRN), 0.5),
        "lb_bwd": nrm(ks[10], (DEPTH + 1, D_HGRN), 0.5),
        "hgrn_norm": 1.0 + nrm(ks[11], (DEPTH, HGRN_DIM), 0.01),
        "w_out": nrm(ks[12], (DEPTH, D_MIX, D_MODEL), D_MIX ** -0.5),
        "w_up": nrm(ks[13], (DEPTH, D_MODEL, 2 * D_FF), D_MODEL ** -0.5),
        "w_dw": nrm(ks[14], (DEPTH, 3, 3, D_FF), 1.0 / 3.0),
        "b_dw": nrm(ks[15], (DEPTH, D_FF), 0.01),
        "w_down": nrm(ks[16], (DEPTH, D_FF, D_MODEL), D_FF ** -0.5),
        "norm_f": 1.0 + nrm(ks[17], (D_MODEL,), 0.01),
    }


def reference(x, c, ctx, c_ctx, w_mod, b_mod, norm1, norm2, w_in, lb_fwd, lb_bwd,
              hgrn_norm, w_out, w_up, w_dw, b_dw, w_down, norm_f):
    B, L, D = x.shape
    rows = L // GRID_W
    n_ctx = ctx.shape[1]
    lb_tab_f = jnp.cumsum(jax.nn.softmax(lb_fwd.astype(jnp.float32), axis=0), axis=0)
    lb_tab_b = jnp.cumsum(jax.nn.softmax(lb_bwd.astype(jnp.float32), axis=0), axis=0)
    zero_state = jnp.zeros((B, HGRN_HEADS, HGRN_DIM, HGRN_DIM), jnp.float32)
    for l in range(DEPTH):
        mod = (jax.nn.silu(c) @ w_mod[l] + b_mod[l])[:, None, :]
        sh1, sc1, gt1, sh2, sc2, gt2 = jnp.split(mod, 6, axis=-1)
        mod_c = jax.nn.silu(c_ctx) @ w_mod[l] + b_mod[l]
        sh1c, sc1c, gt1c, sh2c, sc2c, gt2c = jnp.split(mod_c, 6, axis=-1)
        hc = modulate(rms_norm(ctx, norm1[l]), sh1c, sc1c)
        yc, sc_f, sc_b = token_mixer(hc, w_in[l], lb_tab_f[l], lb_tab_b[l], hgrn_norm[l],
                                     zero_state, zero_state)
        h = modulate(rms_norm(x, norm1[l]), sh1, sc1)
        y, _, _ = token_mixer(h, w_in[l], lb_tab_f[l], lb_tab_b[l], hgrn_norm[l], sc_f, sc_b)
        x = x + gt1 * (y @ w_out[l])
        h2 = modulate(rms_norm(x, norm2[l]), sh2, sc2)
        x = x + gt2 * conv_ffn(h2, w_up[l], w_dw[l], b_dw[l], w_down[l], rows, GRID_W)
        if l < DEPTH - 1:
            ctx = ctx + gt1c * (yc @ w_out[l])
            hc2 = modulate(rms_norm(ctx, norm2[l]), sh2c, sc2c)
            ctx = ctx + gt2c * conv_ffn(hc2, w_up[l], w_dw[l], b_dw[l], w_down[l], 1, n_ctx)
    return rms_norm(x, norm_f)
```

```python
import os
import numpy as np
from contextlib import ExitStack
import ml_dtypes
import concourse.bass as bass
import concourse.mybir as mybir
from concourse.bass_utils import run_bass_kernel_spmd

F32 = mybir.dt.float32
BF16 = mybir.dt.bfloat16
AF = mybir.ActivationFunctionType
ALU = mybir.AluOpType
AX = mybir.AxisListType

D = 1024
T = 4096
NT = 32
NTA = 34
KC = 8
DFF = 2816
NFB = 22
EPS = 1e-6
NPBF = ml_dtypes.bfloat16


class _Op:
    __slots__ = ("eng", "fn", "reads", "writes", "dma", "lane", "count", "signal", "waits", "semkey")

    def __init__(self, eng, fn, reads, writes, dma):
        self.eng = eng; self.fn = fn; self.reads = tuple(reads); self.writes = tuple(writes)
        self.dma = dma; self.lane = None; self.count = None; self.signal = dma; self.waits = (); self.semkey = None


class Sched:
    CENG = ("pe", "act", "dve", "pool")
    ENG = ("pe", "act", "dve", "pool", "sp")

    def __init__(self, nc, csem, dsem):
        self.nc = nc
        self.csem = csem
        self.dsem = dsem
        self.ccount = {e: 0 for e in csem}
        self.dcount = {q: [0] * len(l) for q, l in dsem.items()}
        self.drr = {q: 0 for q in dsem}
        self.ops = []
        self.nops = 0

    def op(self, eng, fn, reads=(), writes=()):
        self.ops.append(_Op(eng, fn, reads, writes, False))

    def dma(self, q, fn, reads=(), writes=()):
        self.ops.append(_Op(q, fn, reads, writes, True))

    def _sem(self, key):
        if key[0] == 'c':
            return self.csem[key[1]]
        return self.dsem[key[1]][key[2]]

    def flush(self, final=False):
        ops = self.ops
        self.ops = []
        self.nops += len(ops)
        last_writer = {}
        readers = {}
        need = []
        lane_last = {q: [None] * len(l) for q, l in self.dsem.items()}
        for i, o in enumerate(ops):
            deps = {}
            for k in o.reads:
                w = last_writer.get(k)
                if w is not None:
                    deps[w] = 'raw'
            for k in o.writes:
                w = last_writer.get(k)
                if w is not None and w not in deps:
                    deps[w] = 'waw'
                for r in readers.get(k, ()):
                    if r not in deps:
                        deps[r] = 'war'
            nd = []
            for j, kind in deps.items():
                if j == i:
                    continue
                y = ops[j]
                if y.dma:
                    nd.append(j)
                elif y.eng == o.eng and not o.dma:
                    if o.eng != 'pe':
                        nd.append(j)
                else:
                    nd.append(j)
            if o.dma:
                q = o.eng
                lane = self.drr[q]
                self.drr[q] = (lane + 1) % len(self.dsem[q])
                o.lane = lane
                prev = lane_last[q][lane]
                if prev is not None:
                    nd.append(prev)
                lane_last[q][lane] = i
            best = {}
            nd2 = []
            for j in nd:
                y = ops[j]
                if y.dma:
                    nd2.append(j)
                else:
                    if best.get(y.eng, -1) < j:
                        best[y.eng] = j
            nd = nd2 + list(best.values())
            need.append(nd)
            for j in nd:
                ops[j].signal = True
            for k in o.writes:
                last_writer[k] = i
                readers[k] = []
            for k in o.reads:
                readers.setdefault(k, []).append(i)
        seen = set()
        for o in reversed(ops):
            if not o.dma and o.eng not in seen:
                seen.add(o.eng)
                o.signal = True
        known = {e: {} for e in self.ENG}
        for i, o in enumerate(ops):
            w = {}
            for j in need[i]:
                y = ops[j]
                w[y.semkey] = max(w.get(y.semkey, 0), y.count)
            kn = known[o.eng]
            o.waits = tuple((k, v) for k, v in w.items() if kn.get(k, 0) < v)
            for k, v in o.waits:
                kn[k] = v
            if o.dma:
                self.dcount[o.eng][o.lane] += 16
                o.count = self.dcount[o.eng][o.lane]
                o.semkey = ('d', o.eng, o.lane)
            elif o.signal:
                self.ccount[o.eng] += 1
                o.count = self.ccount[o.eng]
                o.semkey = ('c', o.eng)
        finals = []
        for e in self.CENG:
            finals.append((('c', e), self.ccount[e]))
        for q, l in self.dcount.items():
            for li, v in enumerate(l):
                finals.append((('d', q, li), v))
        per = {e: [o for o in ops if o.eng == e] for e in self.ENG}

        def run(e_name, eng):
            for o in per[e_name]:
                for k, v in o.waits:
                    eng.wait_ge(self._sem(k), v)
                ins = o.fn(eng)
                if o.dma:
                    ins.then_inc(self._sem(o.semkey), 16)
                elif o.signal:
                    ins.then_inc(self._sem(o.semkey), 1)
            kn = known[e_name]
            for k, v in finals:
                if v > 0 and kn.get(k, 0) < v:
                    eng.wait_ge(self._sem(k), v)

        with self.nc.Block() as block:
            @block.tensor
            def _(e):
                run("pe", e)

            @block.scalar
            def _(e):
                run("act", e)

            @block.vector
            def _(e):
                run("dve", e)

            @block.gpsimd
            def _(e):
                run("pool", e)

            @block.sync
            def _(e):
                run("sp", e)


def _consts():
    c = {}
    c["ident"] = np.eye(128, dtype=np.float32).astype(NPBF)
    s = np.arange(128)[:, None]; t = np.arange(128)[None, :]
    same = (s // 64) == (t // 64)
    cs = (t // 64) * 64
    c["mcum_f"] = (same * ((s <= t).astype(np.float32) - (s <= cs + 32).astype(np.float32))).astype(np.float32)
    c["mcum_b"] = (same * ((s >= t).astype(np.float32) - (s >= cs + 31).astype(np.float32))).astype(np.float32)
    sv = np.arange(128)
    self_f = np.zeros((128, 4), np.float32); self_b = np.zeros((128, 4), np.float32)
    for ch in range(2):
        inch = (sv // 64) == ch
        self_f[:, 2 * ch] = inch & (sv - 64 * ch <= 32)
        self_f[:, 2 * ch + 1] = inch
        self_b[:, 2 * ch] = inch & (sv - 64 * ch >= 31)
        self_b[:, 2 * ch + 1] = inch
    c["sel_f"] = self_f; c["sel_b"] = self_b
    s6 = np.arange(64)[:, None]; t6 = np.arange(64)[None, :]
    c["mask_f"] = (same & (s <= t)).astype(np.float32)
    c["mask_b"] = (same & (s >= t)).astype(np.float32)
    t1 = np.arange(64)[:, None, None]; t2 = np.arange(64)[None, :, None]; k1 = np.arange(64)[None, None, :]
    ph = 2 * np.pi * (t1 * k1 / 64.0 + t2 * k1 / 4096.0)
    c["w1"] = (np.concatenate([np.cos(ph), -np.sin(ph)], axis=-1) / 8.0).astype(np.float32).astype(NPBF)
    a = np.arange(64)[:, None]; b = np.arange(64)[None, :]
    th = 2 * np.pi * a * b / 64.0
    w2 = np.zeros((128, 128), np.float64)
    w2[:64, :64] = np.cos(th); w2[64:, :64] = np.sin(th); w2[:64, 64:] = -np.sin(th); w2[64:, 64:] = np.cos(th)
    c["w2"] = (w2 / 8.0).astype(np.float32).astype(NPBF)
    a = np.arange(128)[:, None]; b = np.arange(128)[None, :]
    th = 2 * np.pi * a * b / 128.0
    c["c128"] = (np.cos(th) / np.sqrt(128.0)).astype(np.float32).astype(NPBF)
    c["s128"] = (np.sin(th) / np.sqrt(128.0)).astype(np.float32).astype(NPBF)
    return c


CONST_SPECS = {
    "ident": ([128, 128], BF16), "mcum_f": ([128, 128], F32), "mcum_b": ([128, 128], F32),
    "sel_f": ([128, 4], F32), "sel_b": ([128, 4], F32),
    "mask_f": ([128, 128], F32), "mask_b": ([128, 128], F32),
    "w1": ([64, 64, 128], BF16), "w2": ([128, 128], BF16), "c128": ([128, 128], BF16), "s128": ([128, 128], BF16),
}

SMALL_SPECS = {
    "cc": ([128, 8, 2], F32),
    "bmT": ([128, 48], F32),
    "bm_bc": ([128, 2, 1024], F32),
    "n1T": ([128, 8], F32), "n2T": ([128, 8], F32),
    "nf_bc": ([128, 1024], F32),
    "lbf_bc": ([128, 2, 512], F32), "lbb_bc": ([128, 2, 512], F32),
    "gnT": ([128, 1], F32),
    "wdwT": ([128, 22, 9], F32), "bdwT": ([128, 22], F32),
}


def build(debug=None):
    nc = bass.Bass("TRN2", target_bir_lowering=False)
    dbg = debug or ()

    def din(name, shape, dt):
        return nc.dram_tensor(name, list(shape), dt, kind="ExternalInput").ap()

    def dscr(name, shape, dt):
        kind = "ExternalOutput" if name in dbg else "Internal"
        return nc.dram_tensor(name, list(shape), dt, kind=kind).ap()

    x = din("x", [T, D], F32)
    ctx = din("ctx", [256, D], F32)
    w_mod = din("w_mod", [D, 6 * D], F32)
    w_in = din("w_in", [D, 3072], F32)
    w_out = din("w_out", [D, D], F32)
    w_up = din("w_up", [D, 2 * DFF], F32)
    w_down = din("w_down", [DFF, D], F32)
    CI = {k: din(k, s, dt) for k, (s, dt) in CONST_SPECS.items()}
    SI = {k: din(k, s, dt) for k, (s, dt) in SMALL_SPECS.items()}
    out = nc.dram_tensor("out", [T, D], F32, kind="ExternalOutput").ap()

    QT = {d: dscr("QT" + d, [NTA, 128, 512], BF16) for d in "fb"}
    KT = {d: dscr("KT" + d, [NTA, 128, 512], BF16) for d in "fb"}
    KK = {d: dscr("KK" + d, [NTA, 128, 512], BF16) for d in "fb"}
    VV = dscr("VV", [NTA, 128, 512], BF16)
    GT = dscr("GT", [NTA, 128, 512], BF16)
    ZU = dscr("ZU", [T, 512], BF16)
    dbg_mod = dscr("dbg_mod", [128, 48 * 2 + 2048], F32) if "dbg_mod" in dbg else None
    dbg_cs = dscr("dbg_cs", [128, 2 * NTA * 24], F32) if "dbg_cs" in dbg else None

    es = ExitStack()
    with es:
        sem = lambda n: es.enter_context(nc.semaphore(n))
        csem = {e: sem("s_" + e) for e in Sched.CENG}
        dsem = {"sp": [sem("d_sp%d" % i) for i in range(8)], "pool": [sem("d_pl%d" % i) for i in range(4)]}
        S = Sched(nc, csem, dsem)

        def sb(stack, name, shape, dt):
            return stack.enter_context(nc.sbuf_tensor("sb_" + name, list(shape), dt))

        def ps(stack, name, shape, dt):
            return stack.enter_context(nc.psum_tensor("ps_" + name, list(shape), dt))

        modT = sb(es, "modT", [128, 48, 2], F32)
        gts = sb(es, "gts", [128, 2, 1024], F32)
        A1 = sb(es, "A1", [128, 8, 2], F32)
        A2 = sb(es, "A2", [128, 8], F32)
        ident = sb(es, "ident", [128, 128], BF16)
        CSE = {d: sb(es, "CSE" + d, [128, NTA, 16], F32) for d in "fb"}
        CSS = {d: sb(es, "CSS" + d, [128, NTA, 8], F32) for d in "fb"}

        pwin = ExitStack()
        pwin.__enter__()
        win = sb(pwin, "win", [128, 8, 3072], BF16)
        with ExitStack() as p0:
            cc = sb(p0, "cc", [128, 8, 2], F32)
            for kc in range(8):
                for half in range(2):
                    S.dma("pool", lambda e, kc=kc, half=half: e.dma_start(
                        out=win[:, kc, half * 1536:(half + 1) * 1536],
                        in_=w_in[kc * 128:(kc + 1) * 128, half * 1536:(half + 1) * 1536]),
                        writes=[("win", kc, half)])
            scT = sb(p0, "scT", [128, 8, 2], F32)
            sc_bc = sb(p0, "sc_bc", [128, 8, 128], F32)
            bmT = sb(p0, "bmT", [128, 48], F32)
            bm_bc = sb(p0, "bm_bc", [128, 2, 1024], F32)
            n1T = sb(p0, "n1T", [128, 8], F32)
            n2T = sb(p0, "n2T", [128, 8], F32)
            wm = [sb(p0, "wm%d" % i, [128, 8, 1024], F32) for i in range(2)]
            pm = ps(p0, "pm", [128, 8, 2], F32)
            pg = [ps(p0, "pg%d" % i, [128, 512], F32) for i in range(2)]

            S.dma("sp", lambda e: e.dma_start(out=cc[:], in_=SI["cc"]), writes=["cc"])
            S.dma("sp", lambda e: e.dma_start(out=bmT[:], in_=SI["bmT"]), writes=["bmT"])
            S.dma("sp", lambda e: e.dma_start(out=bm_bc[:], in_=SI["bm_bc"]), writes=["bm_bc"])
            S.dma("sp", lambda e: e.dma_start(out=n1T[:], in_=SI["n1T"]), writes=["n1T"])
            S.dma("sp", lambda e: e.dma_start(out=n2T[:], in_=SI["n2T"]), writes=["n2T"])
            S.dma("sp", lambda e: e.dma_start(out=ident[:], in_=CI["ident"]), writes=["ident"])
            S.op("act", lambda e: e.activation(out=scT[:], in_=cc[:], func=AF.Silu), reads=["cc"], writes=["scT"])
            S.op("dve", lambda e: e.tensor_copy(out=sc_bc[:], in_=scT[:, :, 0:1].to_broadcast([128, 8, 128])),
                 reads=["scT"], writes=["sc_bc"])
            for g in range(6):
                slot = g % 2
                wk = ("wm", slot)
                for kc in range(8):
                    S.dma("sp", lambda e, g=g, kc=kc, slot=slot: e.dma_start(
                        out=wm[slot][:, kc, :], in_=w_mod[kc * 128:(kc + 1) * 128, g * 1024:(g + 1) * 1024]),
                        writes=[(wk, kc)])
                if g in (2, 5):
                    gi = 0 if g == 2 else 1
                    for half in range(2):
                        for kc in range(8):
                            S.op("pe", lambda e, kc=kc, half=half, slot=slot: e.matmul(
                                pg[half][:], lhsT=sc_bc[:, kc, :], rhs=wm[slot][:, kc, half * 512:(half + 1) * 512],
                                start=(kc == 0), stop=(kc == 7)),
                                reads=["sc_bc", (wk, kc)], writes=[("pg", half)])
                        S.op("dve", lambda e, half=half, gi=gi: e.tensor_tensor(
                            out=gts[:, gi, half * 512:(half + 1) * 512], in0=pg[half][:],
                            in1=bm_bc[:, gi, half * 512:(half + 1) * 512], op=ALU.add),
                            reads=[("pg", half), "bm_bc"], writes=[("gts", gi, half)])
                else:
                    for j in range(8):
                        for kc in range(8):
                            S.op("pe", lambda e, kc=kc, j=j, slot=slot: e.matmul(
                                pm[:, j, :], lhsT=wm[slot][:, kc, j * 128:(j + 1) * 128], rhs=scT[:, kc, :],
                                start=(kc == 0), stop=(kc == 7)),
                                reads=["scT", (wk, kc)], writes=["pm"])
                    S.op("dve", lambda e, g=g: e.tensor_tensor(
                        out=modT[:, g * 8:(g + 1) * 8, :], in0=pm[:],
                        in1=bmT[:, g * 8:(g + 1) * 8].unsqueeze(2).to_broadcast([128, 8, 2]), op=ALU.add),
                        reads=["pm", "bmT"], writes=[("modT", g)])
            S.op("dve", lambda e: e.scalar_tensor_tensor(
                out=A1[:], in0=modT[:, 8:16, :], scalar=1.0, in1=n1T[:].unsqueeze(2).to_broadcast([128, 8, 2]),
                op0=ALU.add, op1=ALU.mult), reads=[("modT", 1), "n1T"], writes=["A1"])
            S.op("dve", lambda e: e.scalar_tensor_tensor(
                out=A2[:], in0=modT[:, 32:40, 0], scalar=1.0, in1=n2T[:],
                op0=ALU.add, op1=ALU.mult), reads=[("modT", 4), "n2T"], writes=["A2"])
            if dbg_mod is not None:
                S.dma("sp", lambda e: e.dma_start(out=dbg_mod[:, 0:96], in_=modT[:].rearrange("p a b -> p (a b)")),
                      reads=[("modT", g) for g in (0, 1, 3, 4)])
                S.dma("sp", lambda e: e.dma_start(out=dbg_mod[:, 96:96 + 2048], in_=gts[:].rearrange("p a b -> p (a b)")),
                      reads=[("gts", a, b) for a in range(2) for b in range(2)])
            S.flush()
        if "stop0" in dbg:
            return nc

        with ExitStack() as pa:
            lbraw = {d: sb(pa, "lbraw" + d, [128, 2, 512], F32) for d in "fb"}
            lbd = {d: sb(pa, "lbd" + d, [128, 512], F32) for d in "fb"}
            oml = {d: sb(pa, "oml" + d, [128, 512], F32) for d in "fb"}
            roml = {d: sb(pa, "roml" + d, [128, 512], F32) for d in "fb"}
            mcum = {d: sb(pa, "mcum" + d, [128, 128], F32) for d in "fb"}
            sel = {d: sb(pa, "sel" + d, [128, 4], F32) for d in "fb"}
            NS = 2
            xt = [sb(pa, "xt%d" % i, [128, 1024], F32) for i in range(NS)]
            junk = sb(pa, "junk", [128, 1024], BF16)
            ssq = [sb(pa, "ssq%d" % i, [128, 1], F32) for i in range(NS)]
            lnv = [sb(pa, "lnv%d" % i, [128, 1], F32) for i in range(NS)]
            rstd = [sb(pa, "rstd%d" % i, [128, 1], F32) for i in range(NS)]
            xn = [sb(pa, "xn%d" % i, [128, 1024], BF16) for i in range(NS)]
            hT = [sb(pa, "hT%d" % i, [128, 8, 128], BF16) for i in range(NS)]
            Eq = [sb(pa, "Eq%d" % i, [128, 512], F32) for i in range(NS)]
            Eg = [sb(pa, "Eg%d" % i, [128, 512], F32) for i in range(NS)]
            Ed = {d: [sb(pa, "E%s%d" % (d, i), [128, 512], F32) for i in range(NS)] for d in "fb"}
            q32 = [sb(pa, "q32_%d" % i, [128, 512], F32) for i in range(3)]
            k32 = {d: [sb(pa, "k32%s%d" % (d, i), [128, 512], F32) for i in range(NS)] for d in "fb"}
            lf32 = {d: [sb(pa, "lf32%s%d" % (d, i), [128, 512], F32) for i in range(NS)] for d in "fb"}
            e1 = {d: sb(pa, "e1" + d, [128, 512], F32) for d in "fb"}
            e2 = {d: sb(pa, "e2" + d, [128, 512], F32) for d in "fb"}
            qt = {d: [sb(pa, "qt%s%d" % (d, i), [128, 512], BF16) for i in range(NS)] for d in "fb"}
            kt = {d: [sb(pa, "kt%s%d" % (d, i), [128, 512], BF16) for i in range(NS)] for d in "fb"}
            stg = [sb(pa, "stg%d" % i, [128, 512], BF16) for i in range(4)]
            vb = [sb(pa, "vb%d" % i, [128, 512], BF16) for i in range(NS)]
            gb = [sb(pa, "gb%d" % i, [128, 512], BF16) for i in range(NS)]
            zub = [sb(pa, "zub%d" % i, [128, 512], BF16) for i in range(NS)]
            dif = [sb(pa, "dif%d" % i, [128, 16], F32) for i in range(NS)]
            cssb = [sb(pa, "cssb%d" % i, [128, 32], F32) for i in range(NS)]
            Z = [ps(pa, "Z%d" % i, [128, 512], F32) for i in range(4)]
            Pbank = ps(pa, "Pbank", [128, 512], F32)
            psT = Pbank[:].bitcast(BF16)
            DX = {"f": ps(pa, "DXf", [128, 512], F32), "b": ps(pa, "DXb", [128, 512], F32)}
            Y = ps(pa, "Y", [128, 512], F32)
            tps = Y[:, 0:256].bitcast(BF16)
            csps = Y[:, 256:288]

            for d, nm in (("f", "lbf_bc"), ("b", "lbb_bc")):
                S.dma("sp", lambda e, d=d, nm=nm: e.dma_start(out=lbraw[d][:], in_=SI[nm]), writes=["lbraw" + d])
                S.dma("sp", lambda e, d=d: e.dma_start(out=mcum[d][:], in_=CI["mcum_" + d]), writes=["mcum" + d])
                S.dma("sp", lambda e, d=d: e.dma_start(out=sel[d][:], in_=CI["sel_" + d]), writes=["sel" + d])
                S.op("dve", lambda e, d=d: e.tensor_tensor(out=lbd[d][:], in0=lbraw[d][:, 0, :], in1=lbraw[d][:, 1, :],
                                                          op=ALU.subtract), reads=["lbraw" + d], writes=["lbd" + d])
                S.op("act", lambda e, d=d: e.activation(out=oml[d][:], in_=lbd[d][:], func=AF.Sigmoid, scale=-1.0),
                     reads=["lbd" + d], writes=["oml" + d])

            bia = sb(pa, "bia", [128, 2, 8, 128], F32)
            for col in range(2):
                S.op("dve", lambda e, col=col: e.tensor_copy(out=bia[:, col, :, :], in_=modT[:, 0:8, col:col + 1].to_broadcast([128, 8, 128])),
                     reads=[("modT", 0)], writes=["bia"])
            stg_rr = [0]
            NTILES = int(os.environ.get('PA_TILES', NTA))
            ok = lambda t: 0 <= t < NTILES
            isx = lambda t: t >= 2

            def transposes(src, srckey):
                for h in range(4):
                    S.op("pe", lambda e, h=h: e.transpose(tps[:, h * 128:(h + 1) * 128], src[:, h * 128:(h + 1) * 128], ident[:]),
                         reads=[srckey, "ident"], writes=["Y"])

            def tstore(dst_ap):
                sj = stg_rr[0]; stg_rr[0] = (sj + 1) % 4
                S.op("dve", lambda e, sj=sj: e.tensor_copy(out=stg[sj][:], in_=tps), reads=["Y"], writes=[("stg", sj)])
                S.dma("sp", lambda e, sj=sj: e.dma_start(out=dst_ap, in_=stg[sj][:]), reads=[("stg", sj)])

            def S1(t):
                s = t % NS
                src = x[(t - 2) * 128:(t - 1) * 128, :] if isx(t) else ctx[t * 128:(t + 1) * 128, :]
                S.dma("pool", lambda e: e.dma_start(out=xt[s][:], in_=src), writes=[("xt", s)])
                S.op("act", lambda e: e.activation(out=junk[:], in_=xt[s][:], func=AF.Square, accum_out=ssq[s][:]),
                     reads=[("xt", s)], writes=["junk", ("ssq", s)])
                S.op("act", lambda e: e.activation(out=lnv[s][:], in_=ssq[s][:], func=AF.Ln, scale=1.0 / D, bias=EPS),
                     reads=[("ssq", s)], writes=[("lnv", s)])
                S.op("act", lambda e: e.activation(out=rstd[s][:], in_=lnv[s][:], func=AF.Exp, scale=-0.5),
                     reads=[("lnv", s)], writes=[("rstd", s)])
                S.op("dve", lambda e: e.tensor_scalar(out=xn[s][:], in0=xt[s][:], scalar1=rstd[s][:, 0:1], scalar2=None, op0=ALU.mult),
                     reads=[("xt", s), ("rstd", s)], writes=[("xn", s)])

            def S2(t):
                s = t % NS
                col = 0 if isx(t) else 1
                for kc in range(8):
                    S.op("pe", lambda e, kc=kc: e.transpose(psT[:, kc * 128:(kc + 1) * 128], xn[s][:, kc * 128:(kc + 1) * 128], ident[:]),
                         reads=[("xn", s), "ident"], writes=["psT"])
                for kc in range(8):
                    if True:
                        S.op("act", lambda e, kc=kc: e.activation(
                            out=hT[s][:, kc, :], in_=psT[:, kc * 128:(kc + 1) * 128], func=AF.Identity,
                            scale=A1[:, kc, col:col + 1], bias=modT[:, kc, col:col + 1]),
                            reads=["psT"], writes=[("hT", s, kc)])
                    else:
                        S.op("dve", lambda e, kc=kc: e.scalar_tensor_tensor(
                            out=hT[s][:, kc, :], in0=psT[:, kc * 128:(kc + 1) * 128], scalar=A1[:, kc, col:col + 1],
                            in1=bia[:, col, kc, :], op0=ALU.mult, op1=ALU.add),
                            reads=["psT", "bia"], writes=[("hT", s, kc)])

            def zblock(t, nb, bank):
                s = t % NS
                dst = Pbank if bank == "P" else Z[bank]
                wkey = "psT" if bank == "P" else ("Z", bank)
                for kc in range(8):
                    S.op("pe", lambda e, kc=kc: e.matmul(dst[:], lhsT=hT[s][:, kc, :], rhs=win[:, kc, nb * 512:(nb + 1) * 512],
                                                        start=(kc == 0), stop=(kc == 7)),
                         reads=[("hT", s, kc), ("win", kc, nb // 3)], writes=[wkey])

            def silu_from(t, bank, Ebuf, Ekey, out_ap, outkey):
                S.op("dve", lambda e: e.tensor_tensor(out=out_ap, in0=Ebuf[:], in1=Z[bank][:], op=ALU.mult),
                     reads=[Ekey, ("Z", bank)], writes=[outkey])

            def TT(t, which):
                if not (ok(t) and isx(t)):
                    return
                s = t % NS
                d = "fb"[which // 2]
                if which % 2 == 0:
                    transposes(qt[d][s], ("qt", d, s)); tstore(QT[d][t])
                else:
                    transposes(kt[d][s], ("kt", d, s)); tstore(KT[d][t])

            def S3E(t, tprev):
                s = t % NS
                if ok(t) and isx(t):
                    zblock(t, 0, 0)
                if ok(t):
                    zblock(t, 1, 1)
                    zblock(t, 2, 2)
                    zblock(t, 3, 3)
                    S.op("dve", lambda e: e.tensor_copy(out=vb[s][:], in_=Z[3][:]), reads=[("Z", 3)], writes=[("vb", s)])
                    S.dma("sp", lambda e: e.dma_start(out=VV[t], in_=vb[s][:]), reads=[("vb", s)])
                TT(tprev, 0)
                if ok(t) and isx(t):
                    zblock(t, 4, "P")
                TT(tprev, 1)
                if ok(t) and isx(t):
                    zblock(t, 5, 3)
                    S.op("dve", lambda e: e.tensor_copy(out=zub[s][:], in_=Z[3][:]), reads=[("Z", 3)], writes=[("zub", s)])
                    S.dma("sp", lambda e: e.dma_start(out=ZU[(t - 2) * 128:(t - 1) * 128, :], in_=zub[s][:]), reads=[("zub", s)])
                TT(tprev, 2)

            def Estage(t):
                s = t % NS
                q3 = t % 3
                if not ok(t):
                    return
                if isx(t):
                    S.op("act", lambda e: e.activation(out=Eq[s][:], in_=Z[0][:], func=AF.Sigmoid), reads=[("Z", 0)], writes=[("Eq", s)])
                S.op("act", lambda e: e.activation(out=Ed["f"][s][:], in_=Z[1][:], func=AF.Sigmoid, scale=-1.0), reads=[("Z", 1)], writes=[("Ef", s)])
                S.op("act", lambda e: e.activation(out=Ed["b"][s][:], in_=Z[2][:], func=AF.Sigmoid, scale=-1.0), reads=[("Z", 2)], writes=[("Eb", s)])
                if isx(t):
                    S.op("act", lambda e: e.activation(out=Eg[s][:], in_=Pbank[:], func=AF.Sigmoid), reads=["psT"], writes=[("Eg", s)])
                    S.op("dve", lambda e: e.tensor_tensor(out=q32[q3][:], in0=Eq[s][:], in1=Z[0][:], op=ALU.mult),
                         reads=[("Eq", s), ("Z", 0)], writes=[("q32", q3)])
                    S.op("dve", lambda e: e.tensor_tensor(out=gb[s][:], in0=Eg[s][:], in1=Pbank[:], op=ALU.mult),
                         reads=[("Eg", s), "psT"], writes=[("gb", s)])

            def S4(t):
                s = t % NS
                for d in "fb":
                    S.op("dve", lambda e, d=d: e.tensor_tensor(out=k32[d][s][:], in0=Ed[d][s][:], in1=oml[d][:], op=ALU.mult),
                         reads=[("E" + d, s), "oml" + d], writes=[("k32", d, s)])
                for d in "fb":
                    S.op("act", lambda e, d=d: e.activation(out=lf32[d][s][:], in_=k32[d][s][:], func=AF.Ln, scale=-1.0, bias=1.0),
                         reads=[("k32", d, s)], writes=[("lf32", d, s)])

            def S5(t):
                s = t % NS
                for d in "fb":
                    S.op("pe", lambda e, d=d: e.matmul(DX[d][:], lhsT=mcum[d][:], rhs=lf32[d][s][:], start=True, stop=True),
                         reads=["mcum" + d, ("lf32", d, s)], writes=[("DX", d)])
                for di, d in enumerate("fb"):
                    for h in range(4):
                        S.op("pe", lambda e, d=d, di=di, h=h: e.matmul(
                            csps[:, di * 16 + h * 4:di * 16 + (h + 1) * 4], lhsT=lf32[d][s][:, h * 128:(h + 1) * 128], rhs=sel[d][:],
                            start=True, stop=True), reads=["sel" + d, ("lf32", d, s)], writes=["Y"])
                S.op("dve", lambda e: e.tensor_copy(out=cssb[s][:], in_=csps), reads=["Y"], writes=[("cssb", s)])
                if isx(t):
                    transposes(gb[s], ("gb", s)); tstore(GT[t])

            def S6(t):
                s = t % NS
                q3 = t % 3
                for di, d in enumerate("fb"):
                    S.op("act", lambda e, d=d, di=di: e.activation(out=CSE[d][:, t, :], in_=cssb[s][:, di * 16:(di + 1) * 16], func=AF.Exp),
                         reads=[("cssb", s)], writes=[("CSE", d, t)])
                S.op("dve", lambda e: e.tensor_tensor(
                    out=dif[s][:], in0=cssb[s][:].rearrange("p (a b) -> p a b", b=2)[:, :, 1],
                    in1=cssb[s][:].rearrange("p (a b) -> p a b", b=2)[:, :, 0], op=ALU.subtract),
                    reads=[("cssb", s)], writes=[("dif", s)])
                for di, d in enumerate("fb"):
                    S.op("act", lambda e, d=d, di=di: e.activation(out=CSS[d][:, t, :], in_=dif[s][:, di * 8:(di + 1) * 8], func=AF.Exp),
                         reads=[("dif", s)], writes=[("CSS", d, t)])
                for d in "fb":
                    if isx(t):
                        S.op("act", lambda e, d=d: e.activation(out=e1[d][:], in_=DX[d][:], func=AF.Exp),
                             reads=[("DX", d)], writes=[("e1", d)])
                    S.op("act", lambda e, d=d: e.activation(out=e2[d][:], in_=DX[d][:], func=AF.Exp, scale=-1.0),
                         reads=[("DX", d)], writes=[("e2", d)])
                for d in "fb":
                    if isx(t):
                        S.op("dve", lambda e, d=d: e.tensor_tensor(out=qt[d][s][:], in0=q32[q3][:], in1=e1[d][:], op=ALU.mult),
                             reads=[("q32", q3), ("e1", d)], writes=[("qt", d, s)])
                    S.op("dve", lambda e, d=d: e.tensor_tensor(out=kt[d][s][:], in0=k32[d][s][:], in1=e2[d][:], op=ALU.mult),
                         reads=[("k32", d, s), ("e2", d)], writes=[("kt", d, s)])
                    S.dma("sp", lambda e, d=d: e.dma_start(out=KK[d][t], in_=kt[d][s][:]), reads=[("kt", d, s)])

            for n in range(-2, NTILES + 1):
                if ok(n + 1):
                    S2(n + 1)
                if ok(n - 1):
                    S6(n - 1)
                if ok(n):
                    S4(n)
                if ok(n + 2):
                    S1(n + 2)
                S3E(n + 1, n - 1)
                Estage(n + 1)
                if ok(n):
                    S5(n)
                TT(n - 1, 3)

            if dbg_cs is not None:
                for di, d in enumerate("fb"):
                    S.dma("sp", lambda e, d=d, di=di: e.dma_start(
                        out=dbg_cs[:, di * NTA * 24:di * NTA * 24 + NTA * 16], in_=CSE[d][:].rearrange("p a b -> p (a b)")),
                        reads=[("CSE", d, i) for i in range(NTILES)])
                    S.dma("sp", lambda e, d=d, di=di: e.dma_start(
                        out=dbg_cs[:, di * NTA * 24 + NTA * 16:(di + 1) * NTA * 24], in_=CSS[d][:].rearrange("p a b -> p (a b)")),
                        reads=[("CSS", d, i) for i in range(NTILES)])
            S.flush()
        pwin.close()
        if "stopA" in dbg:
            return nc
        arena2 = sb(es, "arena2", [128, 8 * T], BF16)

        X1D = dscr("X1D", [64, 128, 512], BF16)
        X1S = dscr("X1S", [T, D], F32)
        GG = dscr("GG", [NFB, 128, T], BF16)
        dbg_yt = dscr("dbg_yt", [128, 8 * T], BF16) if "dbg_yt" in dbg else None

        pbd = ExitStack()
        with pbd:
            YT = sb(pbd, "YT", [128, 8, T], BF16)
            wout = sb(pbd, "wout", [128, 8, 1024], BF16)
            with ExitStack() as pb:
                OT = arena2[:].bitcast(F32).rearrange("p (h t) -> p h t", h=4)
                QTt = {d: [sb(pb, "QTt%s%d" % (d, i), [128, 512], BF16) for i in range(2)] for d in "fb"}
                KTt = {d: [sb(pb, "KTt%s%d" % (d, i), [128, 512], BF16) for i in range(2)] for d in "fb"}
                Kc = {d: [sb(pb, "Kc%s%d" % (d, i), [128, 512], BF16) for i in range(2)] for d in "fb"}
                Vc = {d: [sb(pb, "Vc%s%d" % (d, i), [128, 512], BF16) for i in range(2)] for d in "fb"}
                ATm = {d: sb(pb, "ATm" + d, [128, 4, 128], BF16) for d in "fb"}
                St = {d: sb(pb, "St" + d, [128, 4, 128], F32) for d in "fb"}
                Sg = {d: sb(pb, "Sg" + d, [128, 4, 128], F32) for d in "fb"}
                Sp = {d: sb(pb, "Sp" + d, [128, 4, 128], BF16) for d in "fb"}
                mask = {d: sb(pb, "mask" + d, [128, 128], F32) for d in "fb"}
                ones_bf = sb(pb, "ones_bf", [128, 128], BF16)
                gnT = sb(pb, "gnT", [128, 1], F32)
                sq = [sb(pb, "sq%d" % i, [128, 512], BF16) for i in range(2)]
                rs = [sb(pb, "rs%d" % i, [128, 512], F32) for i in range(2)]
                rinv = [sb(pb, "rinv%d" % i, [128, 512], F32) for i in range(2)]
                t1 = [sb(pb, "t1_%d" % i, [128, 512], F32) for i in range(2)]
                GTt = [sb(pb, "GTt%d" % i, [128, 4, 128], BF16) for i in range(2)]
                atp = {d: ps(pb, "atp" + d, [128, 4, 128], F32) for d in "fb"}
                otp = {d: ps(pb, "otp" + d, [128, 4, 128], F32) for d in "fb"}
                upp = {d: ps(pb, "upp" + d, [128, 4, 128], F32) for d in "fb"}
                nps = [ps(pb, "nps%d" % i, [128, 512], F32) for i in range(2)]

                for d in "fb":
                    S.dma("sp", lambda e, d=d: e.dma_start(out=mask[d][:], in_=CI["mask_" + d]), writes=["mask" + d])
                    S.op("dve", lambda e, d=d: e.memset(St[d][:], 0.0), writes=[("St", d, h) for h in range(4)])
                    S.op("dve", lambda e, d=d: e.memset(Sp[d][:], 0.0), writes=[("Sp", d)])
                S.dma("sp", lambda e: e.dma_start(out=gnT[:], in_=SI["gnT"]), writes=["gnT"])
                S.op("dve", lambda e: e.memset(ones_bf[:], 1.0), writes=["ones_bf"])

                tseq = {"f": list(range(NTA)), "b": [1, 0] + list(range(NTA - 1, 1, -1))}
                cseq = {"f": (0, 1), "b": (1, 0)}
                NSTEP = int(os.environ.get("PB_STEPS", NTA))
                for step in range(NSTEP):
                    ts = step % 2
                    for d in "fb":
                        i = tseq[d][step]
                        isx = i >= 2
                        if isx:
                            S.dma("sp", lambda e, ts=ts, d=d, i=i: e.dma_start(out=QTt[d][ts][:], in_=QT[d][i]), writes=[("QTt", d, ts)])
                            S.dma("sp", lambda e, ts=ts, d=d, i=i: e.dma_start(out=KTt[d][ts][:], in_=KT[d][i]), writes=[("KTt", d, ts)])
                        S.dma("sp", lambda e, ts=ts, d=d, i=i: e.dma_start(out=Kc[d][ts][:], in_=KK[d][i]), writes=[("Kc", d, ts)])
                        S.dma("sp", lambda e, ts=ts, d=d, i=i: e.dma_start(out=Vc[d][ts][:], in_=VV[i]), writes=[("Vc", d, ts)])
                        if isx:
                            for h in range(4):
                                S.op("pe", lambda e, ts=ts, d=d, h=h: e.matmul(
                                    atp[d][:, h, :], lhsT=KTt[d][ts][:, h * 128:(h + 1) * 128],
                                    rhs=QTt[d][ts][:, h * 128:(h + 1) * 128], start=True, stop=True),
                                    reads=[("QTt", d, ts), ("KTt", d, ts)], writes=[("atp", d)])
                            S.op("dve", lambda e, ts=ts, d=d: e.tensor_tensor(
                                out=ATm[d][:], in0=atp[d][:], in1=mask[d][:].unsqueeze(1).to_broadcast([128, 4, 128]), op=ALU.mult),
                                reads=[("atp", d), "mask" + d], writes=[("ATm", d)])
                            for h in range(4):
                                S.op("pe", lambda e, ts=ts, d=d, h=h: e.matmul(
                                    otp[d][:, h, :], lhsT=Vc[d][ts][:, h * 128:(h + 1) * 128], rhs=ATm[d][:, h, :],
                                    start=(h == 0), stop=False, skip_group_check=True),
                                    reads=[("Vc", d, ts), ("ATm", d)], writes=[("otp", d)])
                    for ci in range(2):
                        for d in "fb":
                            i = tseq[d][step]
                            isx = i >= 2
                            c = cseq[d][ci]
                            for h in range(4):
                                S.op("pe", lambda e, ts=ts, d=d, h=h, c=c: e.matmul(
                                    upp[d][:, h, :], lhsT=Kc[d][ts][c * 64:(c + 1) * 64, h * 128:(h + 1) * 128],
                                    rhs=Vc[d][ts][c * 64:(c + 1) * 64, h * 128:(h + 1) * 128],
                                    start=True, stop=True), reads=[("Kc", d, ts), ("Vc", d, ts)], writes=[("upp", d)])
                            if isx:
                                for h in range(4):
                                    S.op("pe", lambda e, ts=ts, d=d, h=h, c=c, ci=ci: e.matmul(
                                        otp[d][:, h, c * 64:(c + 1) * 64], lhsT=Sp[d][:, h, :],
                                        rhs=QTt[d][ts][:, h * 128 + c * 64:h * 128 + c * 64 + 64],
                                        start=False, stop=(ci == 1 and h == 3), skip_group_check=True),
                                        reads=[("Sp", d), ("QTt", d, ts)], writes=[("otp", d)])
                            for h in range(4):
                                S.op("act", lambda e, ts=ts, d=d, h=h, i=i, c=c: e.activation(
                                    out=Sg[d][:, h, :], in_=upp[d][:, h, :], func=AF.Copy,
                                    scale=CSS[d][:, i, h * 2 + c:h * 2 + c + 1]),
                                    reads=[("upp", d)], writes=[("Sg", d, h)])
                            for h in range(4):
                                S.op("dve", lambda e, ts=ts, d=d, h=h, i=i, c=c: e.scalar_tensor_tensor(
                                    out=St[d][:, h, :], in0=St[d][:, h, :], scalar=CSE[d][:, i, h * 4 + 2 * c + 1:h * 4 + 2 * c + 2],
                                    in1=Sg[d][:, h, :], op0=ALU.mult, op1=ALU.add),
                                    reads=[("St", d, h), ("Sg", d, h)], writes=[("St", d, h)])
                            if ci == 0:
                                ni, ncn = i, cseq[d][1]
                            elif step + 1 < NTA:
                                ni, ncn = tseq[d][step + 1], cseq[d][0]
                            else:
                                ni = None
                            if ni is not None and ni >= 2:
                                S.op("pool", lambda e, ts=ts, d=d, ni=ni, ncn=ncn: e.tensor_tensor(
                                    out=Sp[d][:], in0=St[d][:],
                                    in1=CSE[d][:, ni, :].rearrange("p (h k) -> p h k", k=4)[:, :, 2 * ncn:2 * ncn + 1].to_broadcast([128, 4, 128]),
                                    op=ALU.mult), reads=[("St", d, h) for h in range(4)], writes=[("Sp", d)])
                    for d in "fb":
                        i = tseq[d][step]
                        if i >= 2:
                            jt = i - 2
                            tok0 = jt * 128
                            first = (d == "f" and jt < 16) or (d == "b" and jt >= 16)
                            okeys = [("OT", 2 * jt), ("OT", 2 * jt + 1)]
                            if first:
                                S.op("act", lambda e, ts=ts, d=d, tok0=tok0: e.activation(
                                    out=OT[:, :, tok0:tok0 + 128], in_=otp[d][:], func=AF.Copy),
                                    reads=[("otp", d)], writes=okeys)
                            else:
                                S.op("dve", lambda e, ts=ts, d=d, tok0=tok0: e.tensor_tensor(
                                    out=OT[:, :, tok0:tok0 + 128], in0=otp[d][:], in1=OT[:, :, tok0:tok0 + 128], op=ALU.add),
                                    reads=[("otp", d)] + okeys, writes=okeys)
                for bi in range(32 if NSTEP == NTA else 0):
                    h = bi // 8
                    tb = bi % 8
                    s2 = bi % 2
                    blk = OT[:, h, tb * 512:(tb + 1) * 512]
                    rk = [("OT", tb * 8 + q) for q in range(8)]
                    S.dma("sp", lambda e, s2=s2, h=h, tb=tb: e.dma_start(
                        out=GTt[s2][:], in_=GT[2 + 4 * tb:2 + 4 * tb + 4, :, h * 128:(h + 1) * 128].rearrange("t p k -> p t k")),
                        writes=[("GTt", s2)])
                    S.op("act", lambda e, s2=s2, blk=blk: e.activation(out=sq[s2][:], in_=blk, func=AF.Square),
                         reads=rk, writes=[("sq", s2)])
                    S.op("pe", lambda e, s2=s2: e.matmul(nps[s2][:], lhsT=ones_bf[:], rhs=sq[s2][:], start=True, stop=True),
                         reads=["ones_bf", ("sq", s2)], writes=[("nps", s2)])
                    S.op("act", lambda e, s2=s2: e.activation(out=rs[s2][:], in_=nps[s2][:], func=AF.Ln, scale=1.0 / 128, bias=EPS),
                         reads=[("nps", s2)], writes=[("rs", s2)])
                    S.op("act", lambda e, s2=s2: e.activation(out=rinv[s2][:], in_=rs[s2][:], func=AF.Exp, scale=-0.5),
                         reads=[("rs", s2)], writes=[("rinv", s2)])
                    S.op("dve", lambda e, s2=s2, blk=blk: e.tensor_tensor(out=t1[s2][:], in0=blk, in1=rinv[s2][:], op=ALU.mult),
                         reads=rk + [("rinv", s2)], writes=[("t1", s2)])
                    S.op("dve", lambda e, s2=s2, h=h, tb=tb: e.scalar_tensor_tensor(
                        out=YT[:, h, tb * 512:(tb + 1) * 512], in0=t1[s2][:], scalar=gnT[:, 0:1],
                        in1=GTt[s2][:].rearrange("p t k -> p (t k)"), op0=ALU.mult, op1=ALU.mult),
                        reads=[("t1", s2), ("GTt", s2), "gnT"], writes=[("YT", h, tb)])
                S.flush()
            if "stopB" in dbg:
                if dbg_yt is not None:
                    S.dma("sp", lambda e: e.dma_start(out=dbg_yt[:, 0:4 * T], in_=YT[:, 0:4, :].rearrange("p a b -> p (a b)")))
                    S.flush()
                return nc

            with ExitStack() as pc1:
                zus = arena2[0:64, :].rearrange("p (b c) -> p b c", c=512)
                w1 = sb(pc1, "w1", [64, 64, 128], BF16)
                x1sb = [sb(pc1, "x1sb%d" % i, [128, 512], BF16) for i in range(2)]
                x1ps = [ps(pc1, "x1ps%d" % i, [128, 512], F32) for i in range(2)]
                for q in range(4):
                    S.dma("sp", lambda e, q=q: e.dma_start(
                        out=zus[:, q * 16:(q + 1) * 16, :],
                        in_=ZU.rearrange("(a b) c -> a b c", b=64)[:, q * 16:(q + 1) * 16, :]), writes=[("zus", q)])
                S.dma("sp", lambda e: e.dma_start(out=w1[:], in_=CI["w1"]), writes=["w1"])
                for t2 in range(64):
                    s2 = t2 % 2
                    S.op("pe", lambda e, t2=t2, s2=s2: e.matmul(x1ps[s2][:], lhsT=w1[:, t2, :], rhs=zus[:, t2, :], start=True, stop=True),
                         reads=["w1", ("zus", t2 // 16)], writes=[("x1ps", s2)])
                    if s2 == 0:
                        S.op("act", lambda e, s2=s2: e.activation(out=x1sb[s2][:], in_=x1ps[s2][:], func=AF.Copy),
                             reads=[("x1ps", s2)], writes=[("x1sb", s2)])
                    else:
                        S.op("dve", lambda e, s2=s2: e.tensor_copy(out=x1sb[s2][:], in_=x1ps[s2][:]),
                             reads=[("x1ps", s2)], writes=[("x1sb", s2)])
                    S.dma("sp", lambda e, t2=t2, s2=s2: e.dma_start(out=X1D[t2], in_=x1sb[s2][:]), reads=[("x1sb", s2)])
                S.flush()
            with ExitStack() as pc2:
                FT = arena2[:].rearrange("p (g r k) -> p g r k", g=4, r=2)
                w2 = sb(pc2, "w2", [128, 128], BF16)
                c128 = sb(pc2, "c128", [128, 128], BF16)
                s128 = sb(pc2, "s128", [128, 128], BF16)
                dd = [sb(pc2, "dd%d" % i, [128, 512], BF16) for i in range(3)]
                fps = [ps(pc2, "fps%d" % i, [128, 4, 128], F32) for i in range(2)]
                yps = [ps(pc2, "yps%d" % i, [128, 512], F32) for i in range(2)]
                S.dma("sp", lambda e: e.dma_start(out=w2[:], in_=CI["w2"]), writes=["w2"])
                S.dma("sp", lambda e: e.dma_start(out=c128[:], in_=CI["c128"]), writes=["c128"])
                S.dma("sp", lambda e: e.dma_start(out=s128[:], in_=CI["s128"]), writes=["s128"])
                for kc in range(8):
                    S.dma("pool", lambda e, kc=kc: e.dma_start(out=wout[:, kc, :], in_=w_out[kc * 128:(kc + 1) * 128, :]),
                          writes=[("wout", kc)])
                for k1 in range(64):
                    s3 = k1 % 3
                    s2 = k1 % 2
                    S.dma("sp", lambda e, k1=k1, s3=s3: e.dma_start(out=dd[s3][0:64, :], in_=X1D[:, k1, :]), writes=[("dd", s3, 0)])
                    S.dma("sp", lambda e, k1=k1, s3=s3: e.dma_start(out=dd[s3][64:128, :], in_=X1D[:, 64 + k1, :]), writes=[("dd", s3, 1)])
                    for g in range(4):
                        S.op("pe", lambda e, g=g, s3=s3, s2=s2: e.matmul(
                            fps[s2][:, g, :], lhsT=dd[s3][:, g * 128:(g + 1) * 128], rhs=w2[:], start=True, stop=True),
                            reads=[("dd", s3, 0), ("dd", s3, 1), "w2"], writes=[("fps", s2)])
                    FTv = FT[:].rearrange("p g r (b a) -> p g r b a", a=64)[:, :, :, :, k1]
                    if s2 == 0:
                        S.op("act", lambda e, s2=s2, FTv=FTv: e.activation(
                            out=FTv, in_=fps[s2][:].rearrange("p g (r b) -> p g r b", r=2), func=AF.Copy),
                            reads=[("fps", s2)], writes=[("FT", k1)])
                    else:
                        S.op("dve", lambda e, s2=s2, FTv=FTv: e.tensor_copy(
                            out=FTv, in_=fps[s2][:].rearrange("p g (r b) -> p g r b", r=2)),
                            reads=[("fps", s2)], writes=[("FT", k1)])
                allft = [("FT", k1) for k1 in range(64)]
                for g in range(4):
                    for tb in range(8):
                        s2 = (g * 8 + tb) % 2
                        S.op("pe", lambda e, g=g, tb=tb, s2=s2: e.matmul(
                            yps[s2][:], lhsT=c128[:], rhs=FT[:, g, 0, tb * 512:(tb + 1) * 512], start=True, stop=False),
                            reads=allft + ["c128"], writes=[("yps", s2)])
                        S.op("pe", lambda e, g=g, tb=tb, s2=s2: e.matmul(
                            yps[s2][:], lhsT=s128[:], rhs=FT[:, g, 1, tb * 512:(tb + 1) * 512], start=False, stop=True),
                            reads=allft + ["s128"], writes=[("yps", s2)])
                        S.op("act", lambda e, g=g, tb=tb, s2=s2: e.activation(
                            out=YT[:, 4 + g, tb * 512:(tb + 1) * 512], in_=yps[s2][:], func=AF.Copy),
                            reads=[("yps", s2)], writes=[("YT", 4 + g, tb)])
                if dbg_yt is not None:
                    S.dma("sp", lambda e: e.dma_start(out=dbg_yt, in_=YT[:].rearrange("p a b -> p (a b)")),
                          reads=[("YT", a, b) for a in range(4, 8) for b in range(8)])
                S.flush()
            if "stopC" in dbg:
                return nc

            h2T = arena2[:].rearrange("p (k t) -> p k t", k=8)
            with ExitStack() as pd:
                xt = [sb(pd, "dxt%d" % i, [128, 1024], F32) for i in range(3)]
                tm = [sb(pd, "dtm%d" % i, [128, 1024], F32) for i in range(2)]
                junk = sb(pd, "djunk", [128, 1024], BF16)
                ssq = [sb(pd, "dssq%d" % i, [128, 1], F32) for i in range(2)]
                rst = [sb(pd, "drst%d" % i, [128, 1], F32) for i in range(2)]
                rstd = [sb(pd, "drstd%d" % i, [128, 1], F32) for i in range(2)]
                xn = [sb(pd, "dxn%d" % i, [128, 1024], BF16) for i in range(2)]
                aps = [ps(pd, "aps%d" % i, [128, 512], F32) for i in range(4)]
                psT = [ps(pd, "dpsT%d" % i, [128, 1024], BF16) for i in range(2)]
                def d_load(i):
                    s3 = i % 3
                    S.dma("pool", lambda e: e.dma_start(out=xt[s3][:], in_=x[i * 128:(i + 1) * 128, :]), writes=[("xt", s3)])

                def d_mm(i):
                    s = i % 2
                    s3 = i % 3
                    for half in range(2):
                        pb_ = aps[s * 2 + half]
                        for kc in range(8):
                            S.op("pe", lambda e, kc=kc, half=half, pb_=pb_: e.matmul(
                                pb_[:], lhsT=YT[:, kc, i * 128:(i + 1) * 128], rhs=wout[:, kc, half * 512:(half + 1) * 512],
                                start=(kc == 0), stop=(kc == 7)), writes=[("aps", s, half)])
                        S.op("dve", lambda e, half=half, pb_=pb_: e.tensor_tensor(
                            out=tm[s][:, half * 512:(half + 1) * 512], in0=pb_[:], in1=gts[:, 0, half * 512:(half + 1) * 512], op=ALU.mult),
                            reads=[("aps", s, half)], writes=[("tm", s, half)])
                        S.op("pool", lambda e, half=half: e.tensor_tensor(
                            out=xt[s3][:, half * 512:(half + 1) * 512], in0=tm[s][:, half * 512:(half + 1) * 512],
                            in1=xt[s3][:, half * 512:(half + 1) * 512], op=ALU.add),
                            reads=[("tm", s, half), ("xt", s3)], writes=[("xt", s3)])
                    S.dma("sp", lambda e: e.dma_start(out=X1S[i * 128:(i + 1) * 128, :], in_=xt[s3][:]), reads=[("xt", s3)])

                def d_post(i):
                    s = i % 2
                    s3 = i % 3
                    S.op("act", lambda e: e.activation(out=junk[:], in_=xt[s3][:], func=AF.Square, accum_out=ssq[s][:]),
                         reads=[("xt", s3)], writes=["junk", ("ssq", s)])
                    S.op("act", lambda e: e.activation(out=rst[s][:], in_=ssq[s][:], func=AF.Sqrt, scale=1.0 / D, bias=EPS),
                         reads=[("ssq", s)], writes=[("rst", s)])
                    S.op("dve", lambda e: e.reciprocal(out=rstd[s][:], in_=rst[s][:]), reads=[("rst", s)], writes=[("rstd", s)])
                    S.op("dve", lambda e: e.tensor_scalar(out=xn[s][:], in0=xt[s3][:], scalar1=rstd[s][:, 0:1], scalar2=None,
                                                         op0=ALU.mult), reads=[("xt", s3), ("rstd", s)], writes=[("xn", s)])
                    for kc in range(8):
                        S.op("pe", lambda e, kc=kc: e.transpose(psT[s][:, kc * 128:(kc + 1) * 128],
                                                               xn[s][:, kc * 128:(kc + 1) * 128], ident[:]),
                             reads=[("xn", s)], writes=[("psT", s)])
                    for kc in range(8):
                        S.op("act", lambda e, kc=kc: e.activation(
                            out=h2T[:, kc, i * 128:(i + 1) * 128], in_=psT[s][:, kc * 128:(kc + 1) * 128], func=AF.Identity,
                            scale=A2[:, kc:kc + 1], bias=modT[:, 24 + kc, 0:1]),
                            reads=[("psT", s)], writes=[("h2T", i, kc)])

                d_load(0)
                d_load(1)
                d_mm(0)
                for i in range(NT):
                    if i + 2 < NT:
                        d_load(i + 2)
                    if i + 1 < NT:
                        d_mm(i + 1)
                    d_post(i)
                S.flush()
            pbd.close()
            if "stopD" in dbg:
                return nc

            pw = ExitStack()
            pw.__enter__()
            wdn = sb(pw, "wdn", [128, NFB, 1024], BF16)
            with ExitStack() as pe1:
                wa = [sb(pe1, "wa%d" % i, [128, 8, 128], BF16) for i in range(2)]
                wu = [sb(pe1, "wu%d" % i, [128, 8, 128], BF16) for i in range(2)]
                dg = [sb(pe1, "dg%d" % i, [128, 9, 128], BF16) for i in range(2)]
                apad = [sb(pe1, "apad%d" % i, [128, 66, 66], BF16) for i in range(2)]
                ggT = [sb(pe1, "ggT%d" % i, [128, T], BF16) for i in range(2)]
                ga = [sb(pe1, "ga%d" % i, [128, 512], F32) for i in range(2)]
                identf = sb(pe1, "identf", [128, 128], F32)
                wdwT = sb(pe1, "wdwT", [128, 22, 9], F32)
                bdwT = sb(pe1, "bdwT", [128, 22], F32)
                a_ps = [ps(pe1, "a_ps%d" % i, [128, 512], F32) for i in range(2)]
                c_ps = [ps(pe1, "c_ps%d" % i, [128, 8, 64], F32) for i in range(2)]
                u_ps = [ps(pe1, "u_ps%d" % i, [128, 512], F32) for i in range(2)]
                S.dma("sp", lambda e: e.dma_start(out=wdwT[:], in_=SI["wdwT"]), writes=["wdwT"])
                S.dma("sp", lambda e: e.dma_start(out=bdwT[:], in_=SI["bdwT"]), writes=["bdwT"])
                S.op("dve", lambda e: e.tensor_copy(out=identf[:], in_=ident[:]), writes=["identf"])
                for b2 in range(2):
                    S.op("pool", lambda e, b2=b2: e.memset(apad[b2][:], 0.0), writes=[("apad", b2, tb) for tb in range(8)])
                GELU = AF.Gelu_apprx_tanh
                for j in range(int(os.environ.get("PE_FB", NFB))):
                    s = j % 2
                    S.dma("pool", lambda e, s=s, j=j: e.dma_start(
                        out=wa[s][:], in_=w_up[:, j * 128:(j + 1) * 128].rearrange("(kc p) n -> p kc n", p=128)), writes=[("wa", s)])
                    S.dma("pool", lambda e, s=s, j=j: e.dma_start(
                        out=wu[s][:], in_=w_up[:, DFF + j * 128:DFF + (j + 1) * 128].rearrange("(kc p) n -> p kc n", p=128)), writes=[("wu", s)])
                    if j == 1:
                        for jj in range(NFB):
                            S.dma("pool", lambda e, jj=jj: e.dma_start(out=wdn[:, jj, :], in_=w_down[jj * 128:(jj + 1) * 128, :]),
                                  writes=[("wdn", jj)])
                    for tap in range(9):
                        S.op("act", lambda e, s=s, j=j, tap=tap: e.activation(
                            out=dg[s][:, tap, :], in_=identf[:], func=AF.Copy, scale=wdwT[:, j, tap:tap + 1]),
                            reads=["identf", "wdwT"], writes=[("dg", s)])
                    for tb in range(8):
                        p2 = tb % 2
                        for kc in range(8):
                            S.op("pe", lambda e, s=s, kc=kc, tb=tb, p2=p2: e.matmul(
                                a_ps[p2][:], lhsT=wa[s][:, kc, :], rhs=h2T[:, kc, tb * 512:(tb + 1) * 512],
                                start=(kc == 0), stop=(kc == 7)), reads=[("wa", s)], writes=[("a_ps", p2)])
                        S.op("act", lambda e, s=s, tb=tb, p2=p2: e.activation(
                            out=apad[s][:, 1 + 8 * tb:9 + 8 * tb, 1:65], in_=a_ps[p2][:].rearrange("p (r c) -> p r c", c=64), func=AF.Copy),
                            reads=[("a_ps", p2)], writes=[("apad", s, tb)])
                    for tb in range(8):
                        p2 = tb % 2
                        rk = [("apad", s, q) for q in (tb - 1, tb, tb + 1) if 0 <= q < 8]
                        for tap in range(9):
                            dr, dc = tap // 3, tap % 3
                            S.op("pe", lambda e, s=s, tb=tb, p2=p2, tap=tap, dr=dr, dc=dc: e.matmul(
                                c_ps[p2][:], lhsT=dg[s][:, tap, :], rhs=apad[s][:, 8 * tb + dr:8 * tb + dr + 8, dc:dc + 64],
                                start=(tap == 0), stop=(tap == 8)), reads=rk + [("dg", s)], writes=[("c_ps", p2)])
                        S.op("act", lambda e, s=s, j=j, p2=p2: e.activation(
                            out=ga[p2][:], in_=c_ps[p2][:].rearrange("p r c -> p (r c)"), func=GELU, bias=bdwT[:, j:j + 1]),
                            reads=[("c_ps", p2), "bdwT"], writes=[("ga", p2)])
                        for kc in range(8):
                            S.op("pe", lambda e, s=s, kc=kc, tb=tb, p2=p2: e.matmul(
                                u_ps[p2][:], lhsT=wu[s][:, kc, :], rhs=h2T[:, kc, tb * 512:(tb + 1) * 512],
                                start=(kc == 0), stop=(kc == 7)), reads=[("wu", s)], writes=[("u_ps", p2)])
                        S.op("dve", lambda e, s=s, tb=tb, p2=p2: e.tensor_tensor(
                            out=ggT[s][:, tb * 512:(tb + 1) * 512], in0=ga[p2][:], in1=u_ps[p2][:], op=ALU.mult),
                            reads=[("ga", p2), ("u_ps", p2)], writes=[("ggT", s, tb)])
                    S.dma("sp", lambda e, s=s, j=j: e.dma_start(out=GG[j], in_=ggT[s][:]), reads=[("ggT", s, tb) for tb in range(8)])
                S.flush()

            with ExitStack() as pe2:
                nf_bc = sb(pe2, "nf_bc", [128, 1024], F32)
                ggt = [sb(pe2, "ggt%d" % i, [128, NFB, 512], BF16) for i in range(2)]
                xt = [sb(pe2, "ext%d" % i, [128, 1024], F32) for i in range(2)]
                tm = [sb(pe2, "etm%d" % i, [128, 1024], F32) for i in range(2)]
                junk = sb(pe2, "ejunk", [128, 1024], BF16)
                ssq = [sb(pe2, "essq%d" % i, [128, 1], F32) for i in range(2)]
                rst = [sb(pe2, "erst%d" % i, [128, 1], F32) for i in range(2)]
                rstd = [sb(pe2, "erstd%d" % i, [128, 1], F32) for i in range(2)]
                ot = [sb(pe2, "eot%d" % i, [128, 1024], F32) for i in range(2)]
                dps = [ps(pe2, "dps%d" % i, [128, 512], F32) for i in range(4)]
                S.dma("sp", lambda e: e.dma_start(out=nf_bc[:], in_=SI["nf_bc"]), writes=["nf_bc"])
                def ggload(tb):
                    gs = tb % 2
                    for jq in range(2):
                        S.dma("pool", lambda e, gs=gs, tb=tb, jq=jq: e.dma_start(
                            out=ggt[gs][:, jq * 11:(jq + 1) * 11, :],
                            in_=GG[jq * 11:(jq + 1) * 11, :, tb * 512:(tb + 1) * 512].rearrange("j p t -> p j t")),
                            writes=[("ggt", gs, jq)])
                ggload(0)
                S.dma("pool", lambda e: e.dma_start(out=xt[0][:], in_=X1S[0:128, :]), writes=[("xt", 0)])
                for i in range(NT):
                    s = i % 2
                    tb = i // 4
                    gs = tb % 2
                    if i % 4 == 0 and tb + 1 < 8:
                        ggload(tb + 1)
                    if i + 1 < NT:
                        S.dma("pool", lambda e, s=s, i=i: e.dma_start(out=xt[1 - s][:], in_=X1S[(i + 1) * 128:(i + 2) * 128, :]),
                              writes=[("xt", 1 - s)])
                    to = (i % 4) * 128
                    for half in range(2):
                        pb_ = dps[s * 2 + half]
                        for j in range(NFB):
                            S.op("pe", lambda e, gs=gs, j=j, half=half, pb_=pb_, to=to: e.matmul(
                                pb_[:], lhsT=ggt[gs][:, j, to:to + 128], rhs=wdn[:, j, half * 512:(half + 1) * 512],
                                start=(j == 0), stop=(j == NFB - 1)),
                                reads=[("ggt", gs, j // 11)], writes=[("dps", s, half)])
                        S.op("dve", lambda e, s=s, half=half, pb_=pb_: e.tensor_tensor(
                            out=tm[s][:, half * 512:(half + 1) * 512], in0=pb_[:], in1=gts[:, 1, half * 512:(half + 1) * 512], op=ALU.mult),
                            reads=[("dps", s, half)], writes=[("tm", s, half)])
                        S.op("pool", lambda e, s=s, half=half: e.tensor_tensor(
                            out=xt[s][:, half * 512:(half + 1) * 512], in0=tm[s][:, half * 512:(half + 1) * 512],
                            in1=xt[s][:, half * 512:(half + 1) * 512], op=ALU.add),
                            reads=[("tm", s, half), ("xt", s)], writes=[("xt", s)])
                    S.op("act", lambda e, s=s: e.activation(out=junk[:], in_=xt[s][:], func=AF.Square, accum_out=ssq[s][:]),
                         reads=[("xt", s)], writes=["junk", ("ssq", s)])
                    S.op("act", lambda e, s=s: e.activation(out=rst[s][:], in_=ssq[s][:], func=AF.Sqrt, scale=1.0 / D, bias=EPS),
                         reads=[("ssq", s)], writes=[("rst", s)])
                    S.op("dve", lambda e, s=s: e.reciprocal(out=rstd[s][:], in_=rst[s][:]), reads=[("rst", s)], writes=[("rstd", s)])
                    S.op("dve", lambda e, s=s: e.scalar_tensor_tensor(
                        out=ot[s][:], in0=xt[s][:], scalar=rstd[s][:, 0:1], in1=nf_bc[:], op0=ALU.mult, op1=ALU.mult),
                        reads=[("xt", s), ("rstd", s), "nf_bc"], writes=[("ot", s)])
                    S.dma("sp", lambda e, s=s, i=i: e.dma_start(out=out[i * 128:(i + 1) * 128, :], in_=ot[s][:]), reads=[("ot", s)])
                S.flush()
            pw.close()
    return nc


def _host_inputs(inputs, b, consts):
    f = np.float32
    m = {}
    m["x"] = np.ascontiguousarray(inputs["x"][b], f)
    m["ctx"] = np.ascontiguousarray(inputs["ctx"][b], f)
    m["w_mod"] = np.ascontiguousarray(inputs["w_mod"][0], f)
    m["w_in"] = np.ascontiguousarray(inputs["w_in"][0], f)
    m["w_out"] = np.ascontiguousarray(inputs["w_out"][0], f)
    m["w_up"] = np.ascontiguousarray(inputs["w_up"][0], f)
    m["w_down"] = np.ascontiguousarray(inputs["w_down"][0], f)
    fm = lambda v: np.ascontiguousarray(np.asarray(v, f).reshape(-1, 128).T)
    m["cc"] = np.ascontiguousarray(np.stack([fm(inputs["c"][b]), fm(inputs["c_ctx"])], axis=-1))
    m["bmT"] = fm(inputs["b_mod"][0])
    bm = np.asarray(inputs["b_mod"][0], f)
    m["bm_bc"] = np.ascontiguousarray(np.broadcast_to(
        np.stack([bm[2048:3072], bm[5120:6144]])[None], (128, 2, 1024)))
    m["n1T"] = fm(inputs["norm1"][0]); m["n2T"] = fm(inputs["norm2"][0])
    m["nf_bc"] = np.ascontiguousarray(np.broadcast_to(np.asarray(inputs["norm_f"], f)[None], (128, 1024)))
    m["lbf_bc"] = np.ascontiguousarray(np.broadcast_to(np.asarray(inputs["lb_fwd"], f)[None], (128, 2, 512)))
    m["lbb_bc"] = np.ascontiguousarray(np.broadcast_to(np.asarray(inputs["lb_bwd"], f)[None], (128, 2, 512)))
    m["gnT"] = np.ascontiguousarray(np.asarray(inputs["hgrn_norm"][0], f).reshape(128, 1))
    wdw = np.asarray(inputs["w_dw"][0], f).reshape(9, NFB, 128)
    m["wdwT"] = np.ascontiguousarray(wdw.transpose(2, 1, 0))
    m["bdwT"] = fm(inputs["b_dw"][0])
    m.update(consts)
    return m


def kernel(**inputs):
    consts = _consts()
    nc = build()
    in_maps = [_host_inputs(inputs, b, consts) for b in range(8)]
    res = run_bass_kernel_spmd(nc, in_maps, core_ids=list(range(8)))
    return np.stack([np.asarray(r["out"], np.float32) for r in res.results], axis=0)
```

```python
import os
import numpy as np
from contextlib import ExitStack
import ml_dtypes
import concourse.bass as bass
import concourse.mybir as mybir
from concourse.bass_utils import run_bass_kernel_spmd

F32 = mybir.dt.float32
BF16 = mybir.dt.bfloat16
AF = mybir.ActivationFunctionType
ALU = mybir.AluOpType
AX = mybir.AxisListType

D = 1024
T = 4096
NT = 32
NTA = 34
KC = 8
DFF = 2816
NFB = 22
EPS = 1e-6
NPBF = ml_dtypes.bfloat16


class _Op:
    __slots__ = ("eng", "fn", "reads", "writes", "dma", "lane", "count", "signal", "waits", "semkey")

    def __init__(self, eng, fn, reads, writes, dma):
        self.eng = eng; self.fn = fn; self.reads = tuple(reads); self.writes = tuple(writes)
        self.dma = dma; self.lane = None; self.count = None; self.signal = dma; self.waits = (); self.semkey = None


class Sched:
    CENG = ("pe", "act", "dve", "pool")
    ENG = ("pe", "act", "dve", "pool", "sp")

    def __init__(self, nc, csem, dsem):
        self.nc = nc
        self.csem = csem
        self.dsem = dsem
        self.ccount = {e: 0 for e in csem}
        self.dcount = {q: [0] * len(l) for q, l in dsem.items()}
        self.drr = {q: 0 for q in dsem}
        self.ops = []
        self.nops = 0

    def op(self, eng, fn, reads=(), writes=()):
        self.ops.append(_Op(eng, fn, reads, writes, False))

    def dma(self, q, fn, reads=(), writes=()):
        self.ops.append(_Op(q, fn, reads, writes, True))

    def _sem(self, key):
        if key[0] == 'c':
            return self.csem[key[1]]
        return self.dsem[key[1]][key[2]]

    def flush(self, final=False):
        ops = self.ops
        self.ops = []
        self.nops += len(ops)
        last_writer = {}
        readers = {}
        need = []
        lane_last = {q: [None] * len(l) for q, l in self.dsem.items()}
        for i, o in enumerate(ops):
            deps = {}
            for k in o.reads:
                w = last_writer.get(k)
                if w is not None:
                    deps[w] = 'raw'
            for k in o.writes:
                w = last_writer.get(k)
                if w is not None and w not in deps:
                    deps[w] = 'waw'
                for r in readers.get(k, ()):
                    if r not in deps:
                        deps[r] = 'war'
            nd = []
            for j, kind in deps.items():
                if j == i:
                    continue
                y = ops[j]
                if y.dma:
                    nd.append(j)
                elif y.eng == o.eng and not o.dma:
                    if o.eng != 'pe':
                        nd.append(j)
                else:
                    nd.append(j)
            if o.dma:
                q = o.eng
                lane = self.drr[q]
                self.drr[q] = (lane + 1) % len(self.dsem[q])
                o.lane = lane
                prev = lane_last[q][lane]
                if prev is not None:
                    nd.append(prev)
                lane_last[q][lane] = i
            best = {}
            nd2 = []
            for j in nd:
                y = ops[j]
                if y.dma:
                    nd2.append(j)
                else:
                    if best.get(y.eng, -1) < j:
                        best[y.eng] = j
            nd = nd2 + list(best.values())
            need.append(nd)
            for j in nd:
                ops[j].signal = True
            for k in o.writes:
                last_writer[k] = i
                readers[k] = []
            for k in o.reads:
                readers.setdefault(k, []).append(i)
        seen = set()
        for o in reversed(ops):
            if not o.dma and o.eng not in seen:
                seen.add(o.eng)
                o.signal = True
        known = {e: {} for e in self.ENG}
        for i, o in enumerate(ops):
            w = {}
            for j in need[i]:
                y = ops[j]
                w[y.semkey] = max(w.get(y.semkey, 0), y.count)
            kn = known[o.eng]
            o.waits = tuple((k, v) for k, v in w.items() if kn.get(k, 0) < v)
            for k, v in o.waits:
                kn[k] = v
            if o.dma:
                self.dcount[o.eng][o.lane] += 16
                o.count = self.dcount[o.eng][o.lane]
                o.semkey = ('d', o.eng, o.lane)
            elif o.signal:
                self.ccount[o.eng] += 1
                o.count = self.ccount[o.eng]
                o.semkey = ('c', o.eng)
        finals = []
        for e in self.CENG:
            finals.append((('c', e), self.ccount[e]))
        for q, l in self.dcount.items():
            for li, v in enumerate(l):
                finals.append((('d', q, li), v))
        per = {e: [o for o in ops if o.eng == e] for e in self.ENG}

        def run(e_name, eng):
            for o in per[e_name]:
                for k, v in o.waits:
                    eng.wait_ge(self._sem(k), v)
                ins = o.fn(eng)
                if o.dma:
                    ins.then_inc(self._sem(o.semkey), 16)
                elif o.signal:
                    ins.then_inc(self._sem(o.semkey), 1)
            kn = known[e_name]
            for k, v in finals:
                if v > 0 and kn.get(k, 0) < v:
                    eng.wait_ge(self._sem(k), v)

        with self.nc.Block() as block:
            @block.tensor
            def _(e):
                run("pe", e)

            @block.scalar
            def _(e):
                run("act", e)

            @block.vector
            def _(e):
                run("dve", e)

            @block.gpsimd
            def _(e):
                run("pool", e)

            @block.sync
            def _(e):
                run("sp", e)


def _consts():
    c = {}
    c["ident"] = np.eye(128, dtype=np.float32).astype(NPBF)
    s = np.arange(128)[:, None]; t = np.arange(128)[None, :]
    same = (s // 64) == (t // 64)
    cs = (t // 64) * 64
    c["mcum_f"] = (same * ((s <= t).astype(np.float32) - (s <= cs + 32).astype(np.float32))).astype(np.float32)
    c["mcum_b"] = (same * ((s >= t).astype(np.float32) - (s >= cs + 31).astype(np.float32))).astype(np.float32)
    sv = np.arange(128)
    self_f = np.zeros((128, 4), np.float32); self_b = np.zeros((128, 4), np.float32)
    for ch in range(2):
        inch = (sv // 64) == ch
        self_f[:, 2 * ch] = inch & (sv - 64 * ch <= 32)
        self_f[:, 2 * ch + 1] = inch
        self_b[:, 2 * ch] = inch & (sv - 64 * ch >= 31)
        self_b[:, 2 * ch + 1] = inch
    c["sel_f"] = self_f; c["sel_b"] = self_b
    s6 = np.arange(64)[:, None]; t6 = np.arange(64)[None, :]
    c["mask_f"] = (same & (s <= t)).astype(np.float32)
    c["mask_b"] = (same & (s >= t)).astype(np.float32)
    t1 = np.arange(64)[:, None, None]; t2 = np.arange(64)[None, :, None]; k1 = np.arange(64)[None, None, :]
    ph = 2 * np.pi * (t1 * k1 / 64.0 + t2 * k1 / 4096.0)
    c["w1"] = (np.concatenate([np.cos(ph), -np.sin(ph)], axis=-1) / 8.0).astype(np.float32).astype(NPBF)
    a = np.arange(64)[:, None]; b = np.arange(64)[None, :]
    th = 2 * np.pi * a * b / 64.0
    w2 = np.zeros((128, 128), np.float64)
    w2[:64, :64] = np.cos(th); w2[64:, :64] = np.sin(th); w2[:64, 64:] = -np.sin(th); w2[64:, 64:] = np.cos(th)
    c["w2"] = (w2 / 8.0).astype(np.float32).astype(NPBF)
    a = np.arange(128)[:, None]; b = np.arange(128)[None, :]
    th = 2 * np.pi * a * b / 128.0
    c["c128"] = (np.cos(th) / np.sqrt(128.0)).astype(np.float32).astype(NPBF)
    c["s128"] = (np.sin(th) / np.sqrt(128.0)).astype(np.float32).astype(NPBF)
    return c


CONST_SPECS = {
    "ident": ([128, 128], BF16), "mcum_f": ([128, 128], F32), "mcum_b": ([128, 128], F32),
    "sel_f": ([128, 4], F32), "sel_b": ([128, 4], F32),
    "mask_f": ([128, 128], F32), "mask_b": ([128, 128], F32),
    "w1": ([64, 64, 128], BF16), "w2": ([128, 128], BF16), "c128": ([128, 128], BF16), "s128": ([128, 128], BF16),
}

SMALL_SPECS = {
    "cc": ([128, 8, 2], F32),
    "bmT": ([128, 48], F32),
    "bm_bc": ([128, 2, 1024], F32),
    "n1T": ([128, 8], F32), "n2T": ([128, 8], F32),
    "nf_bc": ([128, 1024], F32),
    "lbf_bc": ([128, 2, 512], F32), "lbb_bc": ([128, 2, 512], F32),
    "gnT": ([128, 1], F32),
    "wdwT": ([128, 22, 9], F32), "bdwT": ([128, 22], F32),
}


def build(debug=None):
    nc = bass.Bass("TRN2", target_bir_lowering=False)
    dbg = debug or ()

    def din(name, shape, dt):
        return nc.dram_tensor(name, list(shape), dt, kind="ExternalInput").ap()

    def dscr(name, shape, dt):
        kind = "ExternalOutput" if name in dbg else "Internal"
        return nc.dram_tensor(name, list(shape), dt, kind=kind).ap()

    x = din("x", [T, D], F32)
    ctx = din("ctx", [256, D], F32)
    w_mod = din("w_mod", [D, 6 * D], F32)
    w_in = din("w_in", [D, 3072], F32)
    w_out = din("w_out", [D, D], F32)
    w_up = din("w_up", [D, 2 * DFF], F32)
    w_down = din("w_down", [DFF, D], F32)
    CI = {k: din(k, s, dt) for k, (s, dt) in CONST_SPECS.items()}
    SI = {k: din(k, s, dt) for k, (s, dt) in SMALL_SPECS.items()}
    out = nc.dram_tensor("out", [T, D], F32, kind="ExternalOutput").ap()

    QT = {d: dscr("QT" + d, [NTA, 128, 512], BF16) for d in "fb"}
    KT = {d: dscr("KT" + d, [NTA, 128, 512], BF16) for d in "fb"}
    KK = {d: dscr("KK" + d, [NTA, 128, 512], BF16) for d in "fb"}
    VV = dscr("VV", [NTA, 128, 512], BF16)
    GT = dscr("GT", [NTA, 128, 512], BF16)
    ZU = dscr("ZU", [T, 512], BF16)
    dbg_mod = dscr("dbg_mod", [128, 48 * 2 + 2048], F32) if "dbg_mod" in dbg else None
    dbg_cs = dscr("dbg_cs", [128, 2 * NTA * 24], F32) if "dbg_cs" in dbg else None

    es = ExitStack()
    with es:
        sem = lambda n: es.enter_context(nc.semaphore(n))
        csem = {e: sem("s_" + e) for e in Sched.CENG}
        dsem = {"sp": [sem("d_sp%d" % i) for i in range(8)], "pool": [sem("d_pl%d" % i) for i in range(4)]}
        S = Sched(nc, csem, dsem)

        def sb(stack, name, shape, dt):
            return stack.enter_context(nc.sbuf_tensor("sb_" + name, list(shape), dt))

        def ps(stack, name, shape, dt):
            return stack.enter_context(nc.psum_tensor("ps_" + name, list(shape), dt))

        modT = sb(es, "modT", [128, 48, 2], F32)
        gts = sb(es, "gts", [128, 2, 1024], F32)
        A1 = sb(es, "A1", [128, 8, 2], F32)
        A2 = sb(es, "A2", [128, 8], F32)
        ident = sb(es, "ident", [128, 128], BF16)
        CSE = {d: sb(es, "CSE" + d, [128, NTA, 16], F32) for d in "fb"}
        CSS = {d: sb(es, "CSS" + d, [128, NTA, 8], F32) for d in "fb"}

        pwin = ExitStack()
        pwin.__enter__()
        win = sb(pwin, "win", [128, 8, 3072], BF16)
        with ExitStack() as p0:
            cc = sb(p0, "cc", [128, 8, 2], F32)
            for kc in range(8):
                for half in range(2):
                    S.dma("pool", lambda e, kc=kc, half=half: e.dma_start(
                        out=win[:, kc, half * 1536:(half + 1) * 1536],
                        in_=w_in[kc * 128:(kc + 1) * 128, half * 1536:(half + 1) * 1536]),
                        writes=[("win", kc, half)])
            scT = sb(p0, "scT", [128, 8, 2], F32)
            sc_bc = sb(p0, "sc_bc", [128, 8, 128], F32)
            bmT = sb(p0, "bmT", [128, 48], F32)
            bm_bc = sb(p0, "bm_bc", [128, 2, 1024], F32)
            n1T = sb(p0, "n1T", [128, 8], F32)
            n2T = sb(p0, "n2T", [128, 8], F32)
            wm = [sb(p0, "wm%d" % i, [128, 8, 1024], F32) for i in range(2)]
            pm = ps(p0, "pm", [128, 8, 2], F32)
            pg = [ps(p0, "pg%d" % i, [128, 512], F32) for i in range(2)]

            S.dma("sp", lambda e: e.dma_start(out=cc[:], in_=SI["cc"]), writes=["cc"])
            S.dma("sp", lambda e: e.dma_start(out=bmT[:], in_=SI["bmT"]), writes=["bmT"])
            S.dma("sp", lambda e: e.dma_start(out=bm_bc[:], in_=SI["bm_bc"]), writes=["bm_bc"])
            S.dma("sp", lambda e: e.dma_start(out=n1T[:], in_=SI["n1T"]), writes=["n1T"])
            S.dma("sp", lambda e: e.dma_start(out=n2T[:], in_=SI["n2T"]), writes=["n2T"])
            S.dma("sp", lambda e: e.dma_start(out=ident[:], in_=CI["ident"]), writes=["ident"])
            S.op("act", lambda e: e.activation(out=scT[:], in_=cc[:], func=AF.Silu), reads=["cc"], writes=["scT"])
            S.op("dve", lambda e: e.tensor_copy(out=sc_bc[:], in_=scT[:, :, 0:1].to_broadcast([128, 8, 128])),
                 reads=["scT"], writes=["sc_bc"])
            for g in range(6):
                slot = g % 2
                wk = ("wm", slot)
                for kc in range(8):
                    S.dma("sp", lambda e, g=g, kc=kc, slot=slot: e.dma_start(
                        out=wm[slot][:, kc, :], in_=w_mod[kc * 128:(kc + 1) * 128, g * 1024:(g + 1) * 1024]),
                        writes=[(wk, kc)])
                if g in (2, 5):
                    gi = 0 if g == 2 else 1
                    for half in range(2):
                        for kc in range(8):
                            S.op("pe", lambda e, kc=kc, half=half, slot=slot: e.matmul(
                                pg[half][:], lhsT=sc_bc[:, kc, :], rhs=wm[slot][:, kc, half * 512:(half + 1) * 512],
                                start=(kc == 0), stop=(kc == 7)),
                                reads=["sc_bc", (wk, kc)], writes=[("pg", half)])
                        S.op("dve", lambda e, half=half, gi=gi: e.tensor_tensor(
                            out=gts[:, gi, half * 512:(half + 1) * 512], in0=pg[half][:],
                            in1=bm_bc[:, gi, half * 512:(half + 1) * 512], op=ALU.add),
                            reads=[("pg", half), "bm_bc"], writes=[("gts", gi, half)])
                else:
                    for j in range(8):
                        for kc in range(8):
                            S.op("pe", lambda e, kc=kc, j=j, slot=slot: e.matmul(
                                pm[:, j, :], lhsT=wm[slot][:, kc, j * 128:(j + 1) * 128], rhs=scT[:, kc, :],
                                start=(kc == 0), stop=(kc == 7)),
                                reads=["scT", (wk, kc)], writes=["pm"])
                    S.op("dve", lambda e, g=g: e.tensor_tensor(
                        out=modT[:, g * 8:(g + 1) * 8, :], in0=pm[:],
                        in1=bmT[:, g * 8:(g + 1) * 8].unsqueeze(2).to_broadcast([128, 8, 2]), op=ALU.add),
                        reads=["pm", "bmT"], writes=[("modT", g)])
            S.op("dve", lambda e: e.scalar_tensor_tensor(
                out=A1[:], in0=modT[:, 8:16, :], scalar=1.0, in1=n1T[:].unsqueeze(2).to_broadcast([128, 8, 2]),
                op0=ALU.add, op1=ALU.mult), reads=[("modT", 1), "n1T"], writes=["A1"])
            S.op("dve", lambda e: e.scalar_tensor_tensor(
                out=A2[:], in0=modT[:, 32:40, 0], scalar=1.0, in1=n2T[:],
                op0=ALU.add, op1=ALU.mult), reads=[("modT", 4), "n2T"], writes=["A2"])
            if dbg_mod is not None:
                S.dma("sp", lambda e: e.dma_start(out=dbg_mod[:, 0:96], in_=modT[:].rearrange("p a b -> p (a b)")),
                      reads=[("modT", g) for g in (0, 1, 3, 4)])
                S.dma("sp", lambda e: e.dma_start(out=dbg_mod[:, 96:96 + 2048], in_=gts[:].rearrange("p a b -> p (a b)")),
                      reads=[("gts", a, b) for a in range(2) for b in range(2)])
            S.flush()
        if "stop0" in dbg:
            return nc

        with ExitStack() as pa:
            lbraw = {d: sb(pa, "lbraw" + d, [128, 2, 512], F32) for d in "fb"}
            lbd = {d: sb(pa, "lbd" + d, [128, 512], F32) for d in "fb"}
            oml = {d: sb(pa, "oml" + d, [128, 512], F32) for d in "fb"}
            roml = {d: sb(pa, "roml" + d, [128, 512], F32) for d in "fb"}
            mcum = {d: sb(pa, "mcum" + d, [128, 128], F32) for d in "fb"}
            sel = {d: sb(pa, "sel" + d, [128, 4], F32) for d in "fb"}
            NS = 2
            xt = [sb(pa, "xt%d" % i, [128, 1024], F32) for i in range(NS)]
            junk = sb(pa, "junk", [128, 1024], BF16)
            ssq = [sb(pa, "ssq%d" % i, [128, 1], F32) for i in range(NS)]
            lnv = [sb(pa, "lnv%d" % i, [128, 1], F32) for i in range(NS)]
            rstd = [sb(pa, "rstd%d" % i, [128, 1], F32) for i in range(NS)]
            xn = [sb(pa, "xn%d" % i, [128, 1024], BF16) for i in range(NS)]
            hT = [sb(pa, "hT%d" % i, [128, 8, 128], BF16) for i in range(NS)]
            Eq = [sb(pa, "Eq%d" % i, [128, 512], F32) for i in range(NS)]
            Eg = [sb(pa, "Eg%d" % i, [128, 512], F32) for i in range(NS)]
            Ed = {d: [sb(pa, "E%s%d" % (d, i), [128, 512], F32) for i in range(NS)] for d in "fb"}
            q32 = [sb(pa, "q32_%d" % i, [128, 512], F32) for i in range(3)]
            k32 = {d: [sb(pa, "k32%s%d" % (d, i), [128, 512], F32) for i in range(NS)] for d in "fb"}
            lf32 = {d: [sb(pa, "lf32%s%d" % (d, i), [128, 512], F32) for i in range(NS)] for d in "fb"}
            e1 = {d: sb(pa, "e1" + d, [128, 512], F32) for d in "fb"}
            e2 = {d: sb(pa, "e2" + d, [128, 512], F32) for d in "fb"}
            qt = {d: [sb(pa, "qt%s%d" % (d, i), [128, 512], BF16) for i in range(NS)] for d in "fb"}
            kt = {d: [sb(pa, "kt%s%d" % (d, i), [128, 512], BF16) for i in range(NS)] for d in "fb"}
            stg = [sb(pa, "stg%d" % i, [128, 512], BF16) for i in range(4)]
            vb = [sb(pa, "vb%d" % i, [128, 512], BF16) for i in range(NS)]
            gb = [sb(pa, "gb%d" % i, [128, 512], BF16) for i in range(NS)]
            zub = [sb(pa, "zub%d" % i, [128, 512], BF16) for i in range(NS)]
            dif = [sb(pa, "dif%d" % i, [128, 16], F32) for i in range(NS)]
            cssb = [sb(pa, "cssb%d" % i, [128, 32], F32) for i in range(NS)]
            Z = [ps(pa, "Z%d" % i, [128, 512], F32) for i in range(4)]
            Pbank = ps(pa, "Pbank", [128, 512], F32)
            psT = Pbank[:].bitcast(BF16)
            DX = {"f": ps(pa, "DXf", [128, 512], F32), "b": ps(pa, "DXb", [128, 512], F32)}
            Y = ps(pa, "Y", [128, 512], F32)
            tps = Y[:, 0:256].bitcast(BF16)
            csps = Y[:, 256:288]

            for d, nm in (("f", "lbf_bc"), ("b", "lbb_bc")):
                S.dma("sp", lambda e, d=d, nm=nm: e.dma_start(out=lbraw[d][:], in_=SI[nm]), writes=["lbraw" + d])
                S.dma("sp", lambda e, d=d: e.dma_start(out=mcum[d][:], in_=CI["mcum_" + d]), writes=["mcum" + d])
                S.dma("sp", lambda e, d=d: e.dma_start(out=sel[d][:], in_=CI["sel_" + d]), writes=["sel" + d])
                S.op("dve", lambda e, d=d: e.tensor_tensor(out=lbd[d][:], in0=lbraw[d][:, 0, :], in1=lbraw[d][:, 1, :],
                                                          op=ALU.subtract), reads=["lbraw" + d], writes=["lbd" + d])
                S.op("act", lambda e, d=d: e.activation(out=oml[d][:], in_=lbd[d][:], func=AF.Sigmoid, scale=-1.0),
                     reads=["lbd" + d], writes=["oml" + d])

            bia = sb(pa, "bia", [128, 2, 8, 128], F32)
            for col in range(2):
                S.op("dve", lambda e, col=col: e.tensor_copy(out=bia[:, col, :, :], in_=modT[:, 0:8, col:col + 1].to_broadcast([128, 8, 128])),
                     reads=[("modT", 0)], writes=["bia"])
            stg_rr = [0]
            NTILES = int(os.environ.get('PA_TILES', NTA))
            ok = lambda t: 0 <= t < NTILES
            isx = lambda t: t >= 2

            def transposes(src, srckey):
                for h in range(4):
                    S.op("pe", lambda e, h=h: e.transpose(tps[:, h * 128:(h + 1) * 128], src[:, h * 128:(h + 1) * 128], ident[:]),
                         reads=[srckey, "ident"], writes=["Y"])

            def tstore(dst_ap):
                sj = stg_rr[0]; stg_rr[0] = (sj + 1) % 4
                S.op("dve", lambda e, sj=sj: e.tensor_copy(out=stg[sj][:], in_=tps), reads=["Y"], writes=[("stg", sj)])
                S.dma("sp", lambda e, sj=sj: e.dma_start(out=dst_ap, in_=stg[sj][:]), reads=[("stg", sj)])

            def S1(t):
                s = t % NS
                src = x[(t - 2) * 128:(t - 1) * 128, :] if isx(t) else ctx[t * 128:(t + 1) * 128, :]
                S.dma("pool", lambda e: e.dma_start(out=xt[s][:], in_=src), writes=[("xt", s)])
                S.op("act", lambda e: e.activation(out=junk[:], in_=xt[s][:], func=AF.Square, accum_out=ssq[s][:]),
                     reads=[("xt", s)], writes=["junk", ("ssq", s)])
                S.op("act", lambda e: e.activation(out=lnv[s][:], in_=ssq[s][:], func=AF.Ln, scale=1.0 / D, bias=EPS),
                     reads=[("ssq", s)], writes=[("lnv", s)])
                S.op("act", lambda e: e.activation(out=rstd[s][:], in_=lnv[s][:], func=AF.Exp, scale=-0.5),
                     reads=[("lnv", s)], writes=[("rstd", s)])
                S.op("dve", lambda e: e.tensor_scalar(out=xn[s][:], in0=xt[s][:], scalar1=rstd[s][:, 0:1], scalar2=None, op0=ALU.mult),
                     reads=[("xt", s), ("rstd", s)], writes=[("xn", s)])

            def S2(t):
                s = t % NS
                col = 0 if isx(t) else 1
                for kc in range(8):
                    S.op("pe", lambda e, kc=kc: e.transpose(psT[:, kc * 128:(kc + 1) * 128], xn[s][:, kc * 128:(kc + 1) * 128], ident[:]),
                         reads=[("xn", s), "ident"], writes=["psT"])
                for kc in range(8):
                    if True:
                        S.op("act", lambda e, kc=kc: e.activation(
                            out=hT[s][:, kc, :], in_=psT[:, kc * 128:(kc + 1) * 128], func=AF.Identity,
                            scale=A1[:, kc, col:col + 1], bias=modT[:, kc, col:col + 1]),
                            reads=["psT"], writes=[("hT", s, kc)])
                    else:
                        S.op("dve", lambda e, kc=kc: e.scalar_tensor_tensor(
                            out=hT[s][:, kc, :], in0=psT[:, kc * 128:(kc + 1) * 128], scalar=A1[:, kc, col:col + 1],
                            in1=bia[:, col, kc, :], op0=ALU.mult, op1=ALU.add),
                            reads=["psT", "bia"], writes=[("hT", s, kc)])

            def zblock(t, nb, bank):
                s = t % NS
                dst = Pbank if bank == "P" else Z[bank]
                wkey = "psT" if bank == "P" else ("Z", bank)
                for kc in range(8):
                    S.op("pe", lambda e, kc=kc: e.matmul(dst[:], lhsT=hT[s][:, kc, :], rhs=win[:, kc, nb * 512:(nb + 1) * 512],
                                                        start=(kc == 0), stop=(kc == 7)),
                         reads=[("hT", s, kc), ("win", kc, nb // 3)], writes=[wkey])

            def silu_from(t, bank, Ebuf, Ekey, out_ap, outkey):
                S.op("dve", lambda e: e.tensor_tensor(out=out_ap, in0=Ebuf[:], in1=Z[bank][:], op=ALU.mult),
                     reads=[Ekey, ("Z", bank)], writes=[outkey])

            def TT(t, which):
                if not (ok(t) and isx(t)):
                    return
                s = t % NS
                d = "fb"[which // 2]
                if which % 2 == 0:
                    transposes(qt[d][s], ("qt", d, s)); tstore(QT[d][t])
                else:
                    transposes(kt[d][s], ("kt", d, s)); tstore(KT[d][t])

            def S3E(t, tprev):
                s = t % NS
                if ok(t) and isx(t):
                    zblock(t, 0, 0)
                if ok(t):
                    zblock(t, 1, 1)
                    zblock(t, 2, 2)
                    zblock(t, 3, 3)
                    S.op("dve", lambda e: e.tensor_copy(out=vb[s][:], in_=Z[3][:]), reads=[("Z", 3)], writes=[("vb", s)])
                    S.dma("sp", lambda e: e.dma_start(out=VV[t], in_=vb[s][:]), reads=[("vb", s)])
                TT(tprev, 0)
                if ok(t) and isx(t):
                    zblock(t, 4, "P")
                TT(tprev, 1)
                if ok(t) and isx(t):
                    zblock(t, 5, 3)
                    S.op("dve", lambda e: e.tensor_copy(out=zub[s][:], in_=Z[3][:]), reads=[("Z", 3)], writes=[("zub", s)])
                    S.dma("sp", lambda e: e.dma_start(out=ZU[(t - 2) * 128:(t - 1) * 128, :], in_=zub[s][:]), reads=[("zub", s)])
                TT(tprev, 2)

            def Estage(t):
                s = t % NS
                q3 = t % 3
                if not ok(t):
                    return
                if isx(t):
                    S.op("act", lambda e: e.activation(out=Eq[s][:], in_=Z[0][:], func=AF.Sigmoid), reads=[("Z", 0)], writes=[("Eq", s)])
                S.op("act", lambda e: e.activation(out=Ed["f"][s][:], in_=Z[1][:], func=AF.Sigmoid, scale=-1.0), reads=[("Z", 1)], writes=[("Ef", s)])
                S.op("act", lambda e: e.activation(out=Ed["b"][s][:], in_=Z[2][:], func=AF.Sigmoid, scale=-1.0), reads=[("Z", 2)], writes=[("Eb", s)])
                if isx(t):
                    S.op("act", lambda e: e.activation(out=Eg[s][:], in_=Pbank[:], func=AF.Sigmoid), reads=["psT"], writes=[("Eg", s)])
                    S.op("dve", lambda e: e.tensor_tensor(out=q32[q3][:], in0=Eq[s][:], in1=Z[0][:], op=ALU.mult),
                         reads=[("Eq", s), ("Z", 0)], writes=[("q32", q3)])
                    S.op("dve", lambda e: e.tensor_tensor(out=gb[s][:], in0=Eg[s][:], in1=Pbank[:], op=ALU.mult),
                         reads=[("Eg", s), "psT"], writes=[("gb", s)])

            def S4(t):
                s = t % NS
                for d in "fb":
                    S.op("dve", lambda e, d=d: e.tensor_tensor(out=k32[d][s][:], in0=Ed[d][s][:], in1=oml[d][:], op=ALU.mult),
                         reads=[("E" + d, s), "oml" + d], writes=[("k32", d, s)])
                for d in "fb":
                    S.op("act", lambda e, d=d: e.activation(out=lf32[d][s][:], in_=k32[d][s][:], func=AF.Ln, scale=-1.0, bias=1.0),
                         reads=[("k32", d, s)], writes=[("lf32", d, s)])

            def S5(t):
                s = t % NS
                for d in "fb":
                    S.op("pe", lambda e, d=d: e.matmul(DX[d][:], lhsT=mcum[d][:], rhs=lf32[d][s][:], start=True, stop=True),
                         reads=["mcum" + d, ("lf32", d, s)], writes=[("DX", d)])
                for di, d in enumerate("fb"):
                    for h in range(4):
                        S.op("pe", lambda e, d=d, di=di, h=h: e.matmul(
                            csps[:, di * 16 + h * 4:di * 16 + (h + 1) * 4], lhsT=lf32[d][s][:, h * 128:(h + 1) * 128], rhs=sel[d][:],
                            start=True, stop=True), reads=["sel" + d, ("lf32", d, s)], writes=["Y"])
                S.op("dve", lambda e: e.tensor_copy(out=cssb[s][:], in_=csps), reads=["Y"], writes=[("cssb", s)])
                if isx(t):
                    transposes(gb[s], ("gb", s)); tstore(GT[t])

            def S6(t):
                s = t % NS
                q3 = t % 3
                for di, d in enumerate("fb"):
                    S.op("act", lambda e, d=d, di=di: e.activation(out=CSE[d][:, t, :], in_=cssb[s][:, di * 16:(di + 1) * 16], func=AF.Exp),
                         reads=[("cssb", s)], writes=[("CSE", d, t)])
                S.op("dve", lambda e: e.tensor_tensor(
                    out=dif[s][:], in0=cssb[s][:].rearrange("p (a b) -> p a b", b=2)[:, :, 1],
                    in1=cssb[s][:].rearrange("p (a b) -> p a b", b=2)[:, :, 0], op=ALU.subtract),
                    reads=[("cssb", s)], writes=[("dif", s)])
                for di, d in enumerate("fb"):
                    S.op("act", lambda e, d=d, di=di: e.activation(out=CSS[d][:, t, :], in_=dif[s][:, di * 8:(di + 1) * 8], func=AF.Exp),
                         reads=[("dif", s)], writes=[("CSS", d, t)])
                for d in "fb":
                    if isx(t):
                        S.op("act", lambda e, d=d: e.activation(out=e1[d][:], in_=DX[d][:], func=AF.Exp),
                             reads=[("DX", d)], writes=[("e1", d)])
                    S.op("act", lambda e, d=d: e.activation(out=e2[d][:], in_=DX[d][:], func=AF.Exp, scale=-1.0),
                         reads=[("DX", d)], writes=[("e2", d)])
                for d in "fb":
                    if isx(t):
                        S.op("dve", lambda e, d=d: e.tensor_tensor(out=qt[d][s][:], in0=q32[q3][:], in1=e1[d][:], op=ALU.mult),
                             reads=[("q32", q3), ("e1", d)], writes=[("qt", d, s)])
                    S.op("dve", lambda e, d=d: e.tensor_tensor(out=kt[d][s][:], in0=k32[d][s][:], in1=e2[d][:], op=ALU.mult),
                         reads=[("k32", d, s), ("e2", d)], writes=[("kt", d, s)])
                    S.dma("sp", lambda e, d=d: e.dma_start(out=KK[d][t], in_=kt[d][s][:]), reads=[("kt", d, s)])

            for n in range(-2, NTILES + 1):
                if ok(n + 1):
                    S2(n + 1)
                if ok(n - 1):
                    S6(n - 1)
                if ok(n):
                    S4(n)
                if ok(n + 2):
                    S1(n + 2)
                S3E(n + 1, n - 1)
                Estage(n + 1)
                if ok(n):
                    S5(n)
                TT(n - 1, 3)

            if dbg_cs is not None:
                for di, d in enumerate("fb"):
                    S.dma("sp", lambda e, d=d, di=di: e.dma_start(
                        out=dbg_cs[:, di * NTA * 24:di * NTA * 24 + NTA * 16], in_=CSE[d][:].rearrange("p a b -> p (a b)")),
                        reads=[("CSE", d, i) for i in range(NTILES)])
                    S.dma("sp", lambda e, d=d, di=di: e.dma_start(
                        out=dbg_cs[:, di * NTA * 24 + NTA * 16:(di + 1) * NTA * 24], in_=CSS[d][:].rearrange("p a b -> p (a b)")),
                        reads=[("CSS", d, i) for i in range(NTILES)])
            S.flush()
        pwin.close()
        if "stopA" in dbg:
            return nc
        arena2 = sb(es, "arena2", [128, 8 * T], BF16)

        X1D = dscr("X1D", [64, 128, 512], BF16)
        X1S = dscr("X1S", [T, D], F32)
        GG = dscr("GG", [NFB, 128, T], BF16)
        dbg_yt = dscr("dbg_yt", [128, 8 * T], BF16) if "dbg_yt" in dbg else None

        pbd = ExitStack()
        with pbd:
            YT = sb(pbd, "YT", [128, 8, T], BF16)
            wout = sb(pbd, "wout", [128, 8, 1024], BF16)
            with ExitStack() as pb:
                OT = arena2[:].bitcast(F32).rearrange("p (h t) -> p h t", h=4)
                QTt = {d: [sb(pb, "QTt%s%d" % (d, i), [128, 512], BF16) for i in range(2)] for d in "fb"}
                KTt = {d: [sb(pb, "KTt%s%d" % (d, i), [128, 512], BF16) for i in range(2)] for d in "fb"}
                Kc = {d: [sb(pb, "Kc%s%d" % (d, i), [128, 512], BF16) for i in range(2)] for d in "fb"}
                Vc = {d: [sb(pb, "Vc%s%d" % (d, i), [128, 512], BF16) for i in range(2)] for d in "fb"}
                ATm = {d: sb(pb, "ATm" + d, [128, 4, 128], BF16) for d in "fb"}
                St = {d: sb(pb, "St" + d, [128, 4, 128], F32) for d in "fb"}
                Sg = {d: sb(pb, "Sg" + d, [128, 4, 128], F32) for d in "fb"}
                Sp = {d: sb(pb, "Sp" + d, [128, 4, 128], BF16) for d in "fb"}
                mask = {d: sb(pb, "mask" + d, [128, 128], F32) for d in "fb"}
                ones_bf = sb(pb, "ones_bf", [128, 128], BF16)
                gnT = sb(pb, "gnT", [128, 1], F32)
                sq = [sb(pb, "sq%d" % i, [128, 512], BF16) for i in range(2)]
                rs = [sb(pb, "rs%d" % i, [128, 512], F32) for i in range(2)]
                rinv = [sb(pb, "rinv%d" % i, [128, 512], F32) for i in range(2)]
                t1 = [sb(pb, "t1_%d" % i, [128, 512], F32) for i in range(2)]
                GTt = [sb(pb, "GTt%d" % i, [128, 4, 128], BF16) for i in range(2)]
                atp = {d: ps(pb, "atp" + d, [128, 4, 128], F32) for d in "fb"}
                otp = {d: ps(pb, "otp" + d, [128, 4, 128], F32) for d in "fb"}
                upp = {d: ps(pb, "upp" + d, [128, 4, 128], F32) for d in "fb"}
                nps = [ps(pb, "nps%d" % i, [128, 512], F32) for i in range(2)]

                for d in "fb":
                    S.dma("sp", lambda e, d=d: e.dma_start(out=mask[d][:], in_=CI["mask_" + d]), writes=["mask" + d])
                    S.op("dve", lambda e, d=d: e.memset(St[d][:], 0.0), writes=[("St", d, h) for h in range(4)])
                    S.op("dve", lambda e, d=d: e.memset(Sp[d][:], 0.0), writes=[("Sp", d)])
                S.dma("sp", lambda e: e.dma_start(out=gnT[:], in_=SI["gnT"]), writes=["gnT"])
                S.op("dve", lambda e: e.memset(ones_bf[:], 1.0), writes=["ones_bf"])

                tseq = {"f": list(range(NTA)), "b": [1, 0] + list(range(NTA - 1, 1, -1))}
                cseq = {"f": (0, 1), "b": (1, 0)}
                NSTEP = int(os.environ.get("PB_STEPS", NTA))
                for step in range(NSTEP):
                    ts = step % 2
                    for d in "fb":
                        i = tseq[d][step]
                        isx = i >= 2
                        if isx:
                            S.dma("sp", lambda e, ts=ts, d=d, i=i: e.dma_start(out=QTt[d][ts][:], in_=QT[d][i]), writes=[("QTt", d, ts)])
                            S.dma("sp", lambda e, ts=ts, d=d, i=i: e.dma_start(out=KTt[d][ts][:], in_=KT[d][i]), writes=[("KTt", d, ts)])
                        S.dma("sp", lambda e, ts=ts, d=d, i=i: e.dma_start(out=Kc[d][ts][:], in_=KK[d][i]), writes=[("Kc", d, ts)])
                        S.dma("sp", lambda e, ts=ts, d=d, i=i: e.dma_start(out=Vc[d][ts][:], in_=VV[i]), writes=[("Vc", d, ts)])
                        if isx:
                            for h in range(4):
                                S.op("pe", lambda e, ts=ts, d=d, h=h: e.matmul(
                                    atp[d][:, h, :], lhsT=KTt[d][ts][:, h * 128:(h + 1) * 128],
                                    rhs=QTt[d][ts][:, h * 128:(h + 1) * 128], start=True, stop=True),
                                    reads=[("QTt", d, ts), ("KTt", d, ts)], writes=[("atp", d)])
                            S.op("dve", lambda e, ts=ts, d=d: e.tensor_tensor(
                                out=ATm[d][:], in0=atp[d][:], in1=mask[d][:].unsqueeze(1).to_broadcast([128, 4, 128]), op=ALU.mult),
                                reads=[("atp", d), "mask" + d], writes=[("ATm", d)])
                            for h in range(4):
                                S.op("pe", lambda e, ts=ts, d=d, h=h: e.matmul(
                                    otp[d][:, h, :], lhsT=Vc[d][ts][:, h * 128:(h + 1) * 128], rhs=ATm[d][:, h, :],
                                    start=(h == 0), stop=False, skip_group_check=True),
                                    reads=[("Vc", d, ts), ("ATm", d)], writes=[("otp", d)])
                    for ci in range(2):
                        for d in "fb":
                            i = tseq[d][step]
                            isx = i >= 2
                            c = cseq[d][ci]
                            for h in range(4):
                                S.op("pe", lambda e, ts=ts, d=d, h=h, c=c: e.matmul(
                                    upp[d][:, h, :], lhsT=Kc[d][ts][c * 64:(c + 1) * 64, h * 128:(h + 1) * 128],
                                    rhs=Vc[d][ts][c * 64:(c + 1) * 64, h * 128:(h + 1) * 128],
                                    start=True, stop=True), reads=[("Kc", d, ts), ("Vc", d, ts)], writes=[("upp", d)])
                            if isx:
                                for h in range(4):
                                    S.op("pe", lambda e, ts=ts, d=d, h=h, c=c, ci=ci: e.matmul(
                                        otp[d][:, h, c * 64:(c + 1) * 64], lhsT=Sp[d][:, h, :],
                                        rhs=QTt[d][ts][:, h * 128 + c * 64:h * 128 + c * 64 + 64],
                                        start=False, stop=(ci == 1 and h == 3), skip_group_check=True),
                                        reads=[("Sp", d), ("QTt", d, ts)], writes=[("otp", d)])
                            for h in range(4):
                                S.op("act", lambda e, ts=ts, d=d, h=h, i=i, c=c: e.activation(
                                    out=Sg[d][:, h, :], in_=upp[d][:, h, :], func=AF.Copy,
                                    scale=CSS[d][:, i, h * 2 + c:h * 2 + c + 1]),
                                    reads=[("upp", d)], writes=[("Sg", d, h)])
                            for h in range(4):
                                S.op("dve", lambda e, ts=ts, d=d, h=h, i=i, c=c: e.scalar_tensor_tensor(
                                    out=St[d][:, h, :], in0=St[d][:, h, :], scalar=CSE[d][:, i, h * 4 + 2 * c + 1:h * 4 + 2 * c + 2],
                                    in1=Sg[d][:, h, :], op0=ALU.mult, op1=ALU.add),
                                    reads=[("St", d, h), ("Sg", d, h)], writes=[("St", d, h)])
                            if ci == 0:
                                ni, ncn = i, cseq[d][1]
                            elif step + 1 < NTA:
                                ni, ncn = tseq[d][step + 1], cseq[d][0]
                            else:
                                ni = None
                            if ni is not None and ni >= 2:
                                S.op("pool", lambda e, ts=ts, d=d, ni=ni, ncn=ncn: e.tensor_tensor(
                                    out=Sp[d][:], in0=St[d][:],
                                    in1=CSE[d][:, ni, :].rearrange("p (h k) -> p h k", k=4)[:, :, 2 * ncn:2 * ncn + 1].to_broadcast([128, 4, 128]),
                                    op=ALU.mult), reads=[("St", d, h) for h in range(4)], writes=[("Sp", d)])
                    for d in "fb":
                        i = tseq[d][step]
                        if i >= 2:
                            jt = i - 2
                            tok0 = jt * 128
                            first = (d == "f" and jt < 16) or (d == "b" and jt >= 16)
                            okeys = [("OT", 2 * jt), ("OT", 2 * jt + 1)]
                            if first:
                                S.op("act", lambda e, ts=ts, d=d, tok0=tok0: e.activation(
                                    out=OT[:, :, tok0:tok0 + 128], in_=otp[d][:], func=AF.Copy),
                                    reads=[("otp", d)], writes=okeys)
                            else:
                                S.op("dve", lambda e, ts=ts, d=d, tok0=tok0: e.tensor_tensor(
                                    out=OT[:, :, tok0:tok0 + 128], in0=otp[d][:], in1=OT[:, :, tok0:tok0 + 128], op=ALU.add),
                                    reads=[("otp", d)] + okeys, writes=okeys)
                for bi in range(32 if NSTEP == NTA else 0):
                    h = bi // 8
                    tb = bi % 8
                    s2 = bi % 2
                    blk = OT[:, h, tb * 512:(tb + 1) * 512]
                    rk = [("OT", tb * 8 + q) for q in range(8)]
                    S.dma("sp", lambda e, s2=s2, h=h, tb=tb: e.dma_start(
                        out=GTt[s2][:], in_=GT[2 + 4 * tb:2 + 4 * tb + 4, :, h * 128:(h + 1) * 128].rearrange("t p k -> p t k")),
                        writes=[("GTt", s2)])
                    S.op("act", lambda e, s2=s2, blk=blk: e.activation(out=sq[s2][:], in_=blk, func=AF.Square),
                         reads=rk, writes=[("sq", s2)])
                    S.op("pe", lambda e, s2=s2: e.matmul(nps[s2][:], lhsT=ones_bf[:], rhs=sq[s2][:], start=True, stop=True),
                         reads=["ones_bf", ("sq", s2)], writes=[("nps", s2)])
                    S.op("act", lambda e, s2=s2: e.activation(out=rs[s2][:], in_=nps[s2][:], func=AF.Ln, scale=1.0 / 128, bias=EPS),
                         reads=[("nps", s2)], writes=[("rs", s2)])
                    S.op("act", lambda e, s2=s2: e.activation(out=rinv[s2][:], in_=rs[s2][:], func=AF.Exp, scale=-0.5),
                         reads=[("rs", s2)], writes=[("rinv", s2)])
                    S.op("dve", lambda e, s2=s2, blk=blk: e.tensor_tensor(out=t1[s2][:], in0=blk, in1=rinv[s2][:], op=ALU.mult),
                         reads=rk + [("rinv", s2)], writes=[("t1", s2)])
                    S.op("dve", lambda e, s2=s2, h=h, tb=tb: e.scalar_tensor_tensor(
                        out=YT[:, h, tb * 512:(tb + 1) * 512], in0=t1[s2][:], scalar=gnT[:, 0:1],
                        in1=GTt[s2][:].rearrange("p t k -> p (t k)"), op0=ALU.mult, op1=ALU.mult),
                        reads=[("t1", s2), ("GTt", s2), "gnT"], writes=[("YT", h, tb)])
                S.flush()
            if "stopB" in dbg:
                if dbg_yt is not None:
                    S.dma("sp", lambda e: e.dma_start(out=dbg_yt[:, 0:4 * T], in_=YT[:, 0:4, :].rearrange("p a b -> p (a b)")))
                    S.flush()
                return nc

            with ExitStack() as pc1:
                zus = arena2[0:64, :].rearrange("p (b c) -> p b c", c=512)
                w1 = sb(pc1, "w1", [64, 64, 128], BF16)
                x1sb = [sb(pc1, "x1sb%d" % i, [128, 512], BF16) for i in range(2)]
                x1ps = [ps(pc1, "x1ps%d" % i, [128, 512], F32) for i in range(2)]
                for q in range(4):
                    S.dma("sp", lambda e, q=q: e.dma_start(
                        out=zus[:, q * 16:(q + 1) * 16, :],
                        in_=ZU.rearrange("(a b) c -> a b c", b=64)[:, q * 16:(q + 1) * 16, :]), writes=[("zus", q)])
                S.dma("sp", lambda e: e.dma_start(out=w1[:], in_=CI["w1"]), writes=["w1"])
                for t2 in range(64):
                    s2 = t2 % 2
                    S.op("pe", lambda e, t2=t2, s2=s2: e.matmul(x1ps[s2][:], lhsT=w1[:, t2, :], rhs=zus[:, t2, :], start=True, stop=True),
                         reads=["w1", ("zus", t2 // 16)], writes=[("x1ps", s2)])
                    if s2 == 0:
                        S.op("act", lambda e, s2=s2: e.activation(out=x1sb[s2][:], in_=x1ps[s2][:], func=AF.Copy),
                             reads=[("x1ps", s2)], writes=[("x1sb", s2)])
                    else:
                        S.op("dve", lambda e, s2=s2: e.tensor_copy(out=x1sb[s2][:], in_=x1ps[s2][:]),
                             reads=[("x1ps", s2)], writes=[("x1sb", s2)])
                    S.dma("sp", lambda e, t2=t2, s2=s2: e.dma_start(out=X1D[t2], in_=x1sb[s2][:]), reads=[("x1sb", s2)])
                S.flush()
            with ExitStack() as pc2:
                FT = arena2[:].rearrange("p (g r k) -> p g r k", g=4, r=2)
                w2 = sb(pc2, "w2", [128, 128], BF16)
                c128 = sb(pc2, "c128", [128, 128], BF16)
                s128 = sb(pc2, "s128", [128, 128], BF16)
                dd = [sb(pc2, "dd%d" % i, [128, 512], BF16) for i in range(3)]
                fps = [ps(pc2, "fps%d" % i, [128, 4, 128], F32) for i in range(2)]
                yps = [ps(pc2, "yps%d" % i, [128, 512], F32) for i in range(2)]
                S.dma("sp", lambda e: e.dma_start(out=w2[:], in_=CI["w2"]), writes=["w2"])
                S.dma("sp", lambda e: e.dma_start(out=c128[:], in_=CI["c128"]), writes=["c128"])
                S.dma("sp", lambda e: e.dma_start(out=s128[:], in_=CI["s128"]), writes=["s128"])
                for kc in range(8):
                    S.dma("pool", lambda e, kc=kc: e.dma_start(out=wout[:, kc, :], in_=w_out[kc * 128:(kc + 1) * 128, :]),
                          writes=[("wout", kc)])
                for k1 in range(64):
                    s3 = k1 % 3
                    s2 = k1 % 2
                    S.dma("sp", lambda e, k1=k1, s3=s3: e.dma_start(out=dd[s3][0:64, :], in_=X1D[:, k1, :]), writes=[("dd", s3, 0)])
                    S.dma("sp", lambda e, k1=k1, s3=s3: e.dma_start(out=dd[s3][64:128, :], in_=X1D[:, 64 + k1, :]), writes=[("dd", s3, 1)])
                    for g in range(4):
                        S.op("pe", lambda e, g=g, s3=s3, s2=s2: e.matmul(
                            fps[s2][:, g, :], lhsT=dd[s3][:, g * 128:(g + 1) * 128], rhs=w2[:], start=True, stop=True),
                            reads=[("dd", s3, 0), ("dd", s3, 1), "w2"], writes=[("fps", s2)])
                    FTv = FT[:].rearrange("p g r (b a) -> p g r b a", a=64)[:, :, :, :, k1]
                    if s2 == 0:
                        S.op("act", lambda e, s2=s2, FTv=FTv: e.activation(
                            out=FTv, in_=fps[s2][:].rearrange("p g (r b) -> p g r b", r=2), func=AF.Copy),
                            reads=[("fps", s2)], writes=[("FT", k1)])
                    else:
                        S.op("dve", lambda e, s2=s2, FTv=FTv: e.tensor_copy(
                            out=FTv, in_=fps[s2][:].rearrange("p g (r b) -> p g r b", r=2)),
                            reads=[("fps", s2)], writes=[("FT", k1)])
                allft = [("FT", k1) for k1 in range(64)]
                for g in range(4):
                    for tb in range(8):
                        s2 = (g * 8 + tb) % 2
                        S.op("pe", lambda e, g=g, tb=tb, s2=s2: e.matmul(
                            yps[s2][:], lhsT=c128[:], rhs=FT[:, g, 0, tb * 512:(tb + 1) * 512], start=True, stop=False),
                            reads=allft + ["c128"], writes=[("yps", s2)])
                        S.op("pe", lambda e, g=g, tb=tb, s2=s2: e.matmul(
                            yps[s2][:], lhsT=s128[:], rhs=FT[:, g, 1, tb * 512:(tb + 1) * 512], start=False, stop=True),
                            reads=allft + ["s128"], writes=[("yps", s2)])
                        S.op("act", lambda e, g=g, tb=tb, s2=s2: e.activation(
                            out=YT[:, 4 + g, tb * 512:(tb + 1) * 512], in_=yps[s2][:], func=AF.Copy),
                            reads=[("yps", s2)], writes=[("YT", 4 + g, tb)])
                if dbg_yt is not None:
                    S.dma("sp", lambda e: e.dma_start(out=dbg_yt, in_=YT[:].rearrange("p a b -> p (a b)")),
                          reads=[("YT", a, b) for a in range(4, 8) for b in range(8)])
                S.flush()
            if "stopC" in dbg:
                return nc

            h2T = arena2[:].rearrange("p (k t) -> p k t", k=8)
            with ExitStack() as pd:
                xt = [sb(pd, "dxt%d" % i, [128, 1024], F32) for i in range(3)]
                tm = [sb(pd, "dtm%d" % i, [128, 1024], F32) for i in range(2)]
                junk = sb(pd, "djunk", [128, 1024], BF16)
                ssq = [sb(pd, "dssq%d" % i, [128, 1], F32) for i in range(2)]
                rst = [sb(pd, "drst%d" % i, [128, 1], F32) for i in range(2)]
                rstd = [sb(pd, "drstd%d" % i, [128, 1], F32) for i in range(2)]
                xn = [sb(pd, "dxn%d" % i, [128, 1024], BF16) for i in range(2)]
                aps = [ps(pd, "aps%d" % i, [128, 512], F32) for i in range(4)]
                psT = [ps(pd, "dpsT%d" % i, [128, 1024], BF16) for i in range(2)]
                dmy = ps(pd, "dmy", [128, 512], F32)
                def d_load(i):
                    s3 = i % 3
                    S.dma("pool", lambda e: e.dma_start(out=xt[s3][:], in_=x[i * 128:(i + 1) * 128, :]), writes=[("xt", s3)])

                def d_mm(i):
                    s = i % 2
                    s3 = i % 3
                    for half in range(2):
                        pb_ = aps[s * 2 + half]
                        for kc in range(8):
                            S.op("pe", lambda e, kc=kc, half=half, pb_=pb_: e.matmul(
                                pb_[:], lhsT=YT[:, kc, i * 128:(i + 1) * 128], rhs=wout[:, kc, half * 512:(half + 1) * 512],
                                start=(kc == 0), stop=(kc == 7)), writes=[("aps", s, half)])
                        S.op("dve", lambda e, half=half, pb_=pb_: e.tensor_tensor(
                            out=tm[s][:, half * 512:(half + 1) * 512], in0=pb_[:], in1=gts[:, 0, half * 512:(half + 1) * 512], op=ALU.mult),
                            reads=[("aps", s, half)], writes=[("tm", s, half)])
                        S.op("pool", lambda e, half=half: e.tensor_tensor(
                            out=xt[s3][:, half * 512:(half + 1) * 512], in0=tm[s][:, half * 512:(half + 1) * 512],
                            in1=xt[s3][:, half * 512:(half + 1) * 512], op=ALU.add),
                            reads=[("tm", s, half), ("xt", s3)], writes=[("xt", s3)])
                    S.dma("sp", lambda e: e.dma_start(out=X1S[i * 128:(i + 1) * 128, :], in_=xt[s3][:]), reads=[("xt", s3)])

                def d_post(i):
                    s = i % 2
                    s3 = i % 3
                    S.op("act", lambda e: e.activation(out=junk[:], in_=xt[s3][:], func=AF.Square, accum_out=ssq[s][:]),
                         reads=[("xt", s3)], writes=["junk", ("ssq", s)])
                    S.op("act", lambda e: e.activation(out=rst[s][:], in_=ssq[s][:], func=AF.Sqrt, scale=1.0 / D, bias=EPS),
                         reads=[("ssq", s)], writes=[("rst", s)])
                    S.op("dve", lambda e: e.reciprocal(out=rstd[s][:], in_=rst[s][:]), reads=[("rst", s)], writes=[("rstd", s)])
                    S.op("dve", lambda e: e.tensor_scalar(out=xn[s][:], in0=xt[s3][:], scalar1=rstd[s][:, 0:1], scalar2=None,
                                                         op0=ALU.mult), reads=[("xt", s3), ("rstd", s)], writes=[("xn", s)])
                    for q in range(8):
                        S.op("pe", lambda e, q=q: e.matmul(dmy[:], lhsT=YT[:, q, 0:128], rhs=wout[:, q, 0:512],
                                                          start=True, stop=True), writes=["dmy"])
                    for kc in range(8):
                        S.op("pe", lambda e, kc=kc: e.transpose(psT[s][:, kc * 128:(kc + 1) * 128],
                                                               xn[s][:, kc * 128:(kc + 1) * 128], ident[:]),
                             reads=[("xn", s)], writes=[("psT", s)])
                    for kc in range(8):
                        S.op("act", lambda e, kc=kc: e.activation(
                            out=h2T[:, kc, i * 128:(i + 1) * 128], in_=psT[s][:, kc * 128:(kc + 1) * 128], func=AF.Identity,
                            scale=A2[:, kc:kc + 1], bias=modT[:, 24 + kc, 0:1]),
                            reads=[("psT", s)], writes=[("h2T", i, kc)])

                d_load(0)
                d_load(1)
                d_mm(0)
                for i in range(NT):
                    if i + 2 < NT:
                        d_load(i + 2)
                    if i + 1 < NT:
                        d_mm(i + 1)
                    d_post(i)
                S.flush()
            pbd.close()
            if "stopD" in dbg:
                return nc

            pw = ExitStack()
            pw.__enter__()
            wdn = sb(pw, "wdn", [128, NFB, 1024], BF16)
            with ExitStack() as pe1:
                wa = [sb(pe1, "wa%d" % i, [128, 8, 128], BF16) for i in range(2)]
                wu = [sb(pe1, "wu%d" % i, [128, 8, 128], BF16) for i in range(2)]
                dg = [sb(pe1, "dg%d" % i, [128, 9, 128], BF16) for i in range(2)]
                apad = [sb(pe1, "apad%d" % i, [128, 66, 66], BF16) for i in range(2)]
                ggT = [sb(pe1, "ggT%d" % i, [128, T], BF16) for i in range(2)]
                ga = [sb(pe1, "ga%d" % i, [128, 512], F32) for i in range(2)]
                identf = sb(pe1, "identf", [128, 128], F32)
                wdwT = sb(pe1, "wdwT", [128, 22, 9], F32)
                bdwT = sb(pe1, "bdwT", [128, 22], F32)
                a_ps = [ps(pe1, "a_ps%d" % i, [128, 512], F32) for i in range(2)]
                c_ps = [ps(pe1, "c_ps%d" % i, [128, 8, 64], F32) for i in range(2)]
                u_ps = [ps(pe1, "u_ps%d" % i, [128, 512], F32) for i in range(2)]
                S.dma("sp", lambda e: e.dma_start(out=wdwT[:], in_=SI["wdwT"]), writes=["wdwT"])
                S.dma("sp", lambda e: e.dma_start(out=bdwT[:], in_=SI["bdwT"]), writes=["bdwT"])
                S.op("dve", lambda e: e.tensor_copy(out=identf[:], in_=ident[:]), writes=["identf"])
                for b2 in range(2):
                    S.op("pool", lambda e, b2=b2: e.memset(apad[b2][:], 0.0), writes=[("apad", b2, tb) for tb in range(8)])
                GELU = AF.Gelu_apprx_tanh
                for j in range(int(os.environ.get("PE_FB", NFB))):
                    s = j % 2
                    S.dma("pool", lambda e, s=s, j=j: e.dma_start(
                        out=wa[s][:], in_=w_up[:, j * 128:(j + 1) * 128].rearrange("(kc p) n -> p kc n", p=128)), writes=[("wa", s)])
                    S.dma("pool", lambda e, s=s, j=j: e.dma_start(
                        out=wu[s][:], in_=w_up[:, DFF + j * 128:DFF + (j + 1) * 128].rearrange("(kc p) n -> p kc n", p=128)), writes=[("wu", s)])
                    if j == 1:
                        for jj in range(NFB):
                            S.dma("pool", lambda e, jj=jj: e.dma_start(out=wdn[:, jj, :], in_=w_down[jj * 128:(jj + 1) * 128, :]),
                                  writes=[("wdn", jj)])
                    for tap in range(9):
                        S.op("act", lambda e, s=s, j=j, tap=tap: e.activation(
                            out=dg[s][:, tap, :], in_=identf[:], func=AF.Copy, scale=wdwT[:, j, tap:tap + 1]),
                            reads=["identf", "wdwT"], writes=[("dg", s)])
                    for tb in range(8):
                        p2 = tb % 2
                        for kc in range(8):
                            S.op("pe", lambda e, s=s, kc=kc, tb=tb, p2=p2: e.matmul(
                                a_ps[p2][:], lhsT=wa[s][:, kc, :], rhs=h2T[:, kc, tb * 512:(tb + 1) * 512],
                                start=(kc == 0), stop=(kc == 7)), reads=[("wa", s)], writes=[("a_ps", p2)])
                        S.op("act", lambda e, s=s, tb=tb, p2=p2: e.activation(
                            out=apad[s][:, 1 + 8 * tb:9 + 8 * tb, 1:65], in_=a_ps[p2][:].rearrange("p (r c) -> p r c", c=64), func=AF.Copy),
                            reads=[("a_ps", p2)], writes=[("apad", s, tb)])
                    for tb in range(8):
                        p2 = tb % 2
                        rk = [("apad", s, q) for q in (tb - 1, tb, tb + 1) if 0 <= q < 8]
                        for tap in range(9):
                            dr, dc = tap // 3, tap % 3
                            S.op("pe", lambda e, s=s, tb=tb, p2=p2, tap=tap, dr=dr, dc=dc: e.matmul(
                                c_ps[p2][:], lhsT=dg[s][:, tap, :], rhs=apad[s][:, 8 * tb + dr:8 * tb + dr + 8, dc:dc + 64],
                                start=(tap == 0), stop=(tap == 8)), reads=rk + [("dg", s)], writes=[("c_ps", p2)])
                        S.op("act", lambda e, s=s, j=j, p2=p2: e.activation(
                            out=ga[p2][:], in_=c_ps[p2][:].rearrange("p r c -> p (r c)"), func=GELU, bias=bdwT[:, j:j + 1]),
                            reads=[("c_ps", p2), "bdwT"], writes=[("ga", p2)])
                        for kc in range(8):
                            S.op("pe", lambda e, s=s, kc=kc, tb=tb, p2=p2: e.matmul(
                                u_ps[p2][:], lhsT=wu[s][:, kc, :], rhs=h2T[:, kc, tb * 512:(tb + 1) * 512],
                                start=(kc == 0), stop=(kc == 7)), reads=[("wu", s)], writes=[("u_ps", p2)])
                        S.op("dve", lambda e, s=s, tb=tb, p2=p2: e.tensor_tensor(
                            out=ggT[s][:, tb * 512:(tb + 1) * 512], in0=ga[p2][:], in1=u_ps[p2][:], op=ALU.mult),
                            reads=[("ga", p2), ("u_ps", p2)], writes=[("ggT", s, tb)])
                    S.dma("sp", lambda e, s=s, j=j: e.dma_start(out=GG[j], in_=ggT[s][:]), reads=[("ggT", s, tb) for tb in range(8)])
                S.flush()

            with ExitStack() as pe2:
                nf_bc = sb(pe2, "nf_bc", [128, 1024], F32)
                ggt = [sb(pe2, "ggt%d" % i, [128, NFB, 512], BF16) for i in range(2)]
                xt = [sb(pe2, "ext%d" % i, [128, 1024], F32) for i in range(2)]
                tm = [sb(pe2, "etm%d" % i, [128, 1024], F32) for i in range(2)]
                junk = sb(pe2, "ejunk", [128, 1024], BF16)
                ssq = [sb(pe2, "essq%d" % i, [128, 1], F32) for i in range(2)]
                rst = [sb(pe2, "erst%d" % i, [128, 1], F32) for i in range(2)]
                rstd = [sb(pe2, "erstd%d" % i, [128, 1], F32) for i in range(2)]
                ot = [sb(pe2, "eot%d" % i, [128, 1024], F32) for i in range(2)]
                dps = [ps(pe2, "dps%d" % i, [128, 512], F32) for i in range(4)]
                S.dma("sp", lambda e: e.dma_start(out=nf_bc[:], in_=SI["nf_bc"]), writes=["nf_bc"])
                def ggload(tb):
                    gs = tb % 2
                    for jq in range(2):
                        S.dma("pool", lambda e, gs=gs, tb=tb, jq=jq: e.dma_start(
                            out=ggt[gs][:, jq * 11:(jq + 1) * 11, :],
                            in_=GG[jq * 11:(jq + 1) * 11, :, tb * 512:(tb + 1) * 512].rearrange("j p t -> p j t")),
                            writes=[("ggt", gs, jq)])
                ggload(0)
                S.dma("pool", lambda e: e.dma_start(out=xt[0][:], in_=X1S[0:128, :]), writes=[("xt", 0)])
                for i in range(NT):
                    s = i % 2
                    tb = i // 4
                    gs = tb % 2
                    if i % 4 == 0 and tb + 1 < 8:
                        ggload(tb + 1)
                    if i + 1 < NT:
                        S.dma("pool", lambda e, s=s, i=i: e.dma_start(out=xt[1 - s][:], in_=X1S[(i + 1) * 128:(i + 2) * 128, :]),
                              writes=[("xt", 1 - s)])
                    to = (i % 4) * 128
                    for half in range(2):
                        pb_ = dps[s * 2 + half]
                        for j in range(NFB):
                            S.op("pe", lambda e, gs=gs, j=j, half=half, pb_=pb_, to=to: e.matmul(
                                pb_[:], lhsT=ggt[gs][:, j, to:to + 128], rhs=wdn[:, j, half * 512:(half + 1) * 512],
                                start=(j == 0), stop=(j == NFB - 1)),
                                reads=[("ggt", gs, j // 11)], writes=[("dps", s, half)])
                        S.op("dve", lambda e, s=s, half=half, pb_=pb_: e.tensor_tensor(
                            out=tm[s][:, half * 512:(half + 1) * 512], in0=pb_[:], in1=gts[:, 1, half * 512:(half + 1) * 512], op=ALU.mult),
                            reads=[("dps", s, half)], writes=[("tm", s, half)])
                        S.op("pool", lambda e, s=s, half=half: e.tensor_tensor(
                            out=xt[s][:, half * 512:(half + 1) * 512], in0=tm[s][:, half * 512:(half + 1) * 512],
                            in1=xt[s][:, half * 512:(half + 1) * 512], op=ALU.add),
                            reads=[("tm", s, half), ("xt", s)], writes=[("xt", s)])
                    S.op("act", lambda e, s=s: e.activation(out=junk[:], in_=xt[s][:], func=AF.Square, accum_out=ssq[s][:]),
                         reads=[("xt", s)], writes=["junk", ("ssq", s)])
                    S.op("act", lambda e, s=s: e.activation(out=rst[s][:], in_=ssq[s][:], func=AF.Sqrt, scale=1.0 / D, bias=EPS),
                         reads=[("ssq", s)], writes=[("rst", s)])
                    S.op("dve", lambda e, s=s: e.reciprocal(out=rstd[s][:], in_=rst[s][:]), reads=[("rst", s)], writes=[("rstd", s)])
                    S.op("dve", lambda e, s=s: e.scalar_tensor_tensor(
                        out=ot[s][:], in0=xt[s][:], scalar=rstd[s][:, 0:1], in1=nf_bc[:], op0=ALU.mult, op1=ALU.mult),
                        reads=[("xt", s), ("rstd", s), "nf_bc"], writes=[("ot", s)])
                    S.dma("sp", lambda e, s=s, i=i: e.dma_start(out=out[i * 128:(i + 1) * 128, :], in_=ot[s][:]), reads=[("ot", s)])
                S.flush()
            pw.close()
    return nc


def _host_inputs(inputs, b, consts):
    f = np.float32
    m = {}
    m["x"] = np.ascontiguousarray(inputs["x"][b], f)
    m["ctx"] = np.ascontiguousarray(inputs["ctx"][b], f)
    m["w_mod"] = np.ascontiguousarray(inputs["w_mod"][0], f)
    m["w_in"] = np.ascontiguousarray(inputs["w_in"][0], f)
    m["w_out"] = np.ascontiguousarray(inputs["w_out"][0], f)
    m["w_up"] = np.ascontiguousarray(inputs["w_up"][0], f)
    m["w_down"] = np.ascontiguousarray(inputs["w_down"][0], f)
    fm = lambda v: np.ascontiguousarray(np.asarray(v, f).reshape(-1, 128).T)
    m["cc"] = np.ascontiguousarray(np.stack([fm(inputs["c"][b]), fm(inputs["c_ctx"])], axis=-1))
    m["bmT"] = fm(inputs["b_mod"][0])
    bm = np.asarray(inputs["b_mod"][0], f)
    m["bm_bc"] = np.ascontiguousarray(np.broadcast_to(
        np.stack([bm[2048:3072], bm[5120:6144]])[None], (128, 2, 1024)))
    m["n1T"] = fm(inputs["norm1"][0]); m["n2T"] = fm(inputs["norm2"][0])
    m["nf_bc"] = np.ascontiguousarray(np.broadcast_to(np.asarray(inputs["norm_f"], f)[None], (128, 1024)))
    m["lbf_bc"] = np.ascontiguousarray(np.broadcast_to(np.asarray(inputs["lb_fwd"], f)[None], (128, 2, 512)))
    m["lbb_bc"] = np.ascontiguousarray(np.broadcast_to(np.asarray(inputs["lb_bwd"], f)[None], (128, 2, 512)))
    m["gnT"] = np.ascontiguousarray(np.asarray(inputs["hgrn_norm"][0], f).reshape(128, 1))
    wdw = np.asarray(inputs["w_dw"][0], f).reshape(9, NFB, 128)
    m["wdwT"] = np.ascontiguousarray(wdw.transpose(2, 1, 0))
    m["bdwT"] = fm(inputs["b_dw"][0])
    m.update(consts)
    return m


def kernel(**inputs):
    consts = _consts()
    nc = build()
    in_maps = [_host_inputs(inputs, b, consts) for b in range(8)]
    res = run_bass_kernel_spmd(nc, in_maps, core_ids=list(range(8)))
    return np.stack([np.asarray(r["out"], np.float32) for r in res.results], axis=0)
```

```python
import os
import numpy as np
from contextlib import ExitStack
import ml_dtypes
import concourse.bass as bass
import concourse.mybir as mybir
from concourse.bass_utils import run_bass_kernel_spmd

F32 = mybir.dt.float32
BF16 = mybir.dt.bfloat16
AF = mybir.ActivationFunctionType
ALU = mybir.AluOpType
AX = mybir.AxisListType

D = 1024
T = 4096
NT = 32
NTA = 34
KC = 8
DFF = 2816
NFB = 22
EPS = 1e-6
NPBF = ml_dtypes.bfloat16


class _Op:
    __slots__ = ("eng", "fn", "reads", "writes", "dma", "lane", "count", "signal", "waits", "semkey")

    def __init__(self, eng, fn, reads, writes, dma):
        self.eng = eng; self.fn = fn; self.reads = tuple(reads); self.writes = tuple(writes)
        self.dma = dma; self.lane = None; self.count = None; self.signal = dma; self.waits = (); self.semkey = None


class Sched:
    CENG = ("pe", "act", "dve", "pool")
    ENG = ("pe", "act", "dve", "pool", "sp")

    def __init__(self, nc, csem, dsem):
        self.nc = nc
        self.csem = csem
        self.dsem = dsem
        self.ccount = {e: 0 for e in csem}
        self.dcount = {q: [0] * len(l) for q, l in dsem.items()}
        self.drr = {q: 0 for q in dsem}
        self.ops = []
        self.nops = 0

    def op(self, eng, fn, reads=(), writes=()):
        self.ops.append(_Op(eng, fn, reads, writes, False))

    def dma(self, q, fn, reads=(), writes=()):
        self.ops.append(_Op(q, fn, reads, writes, True))

    def _sem(self, key):
        if key[0] == 'c':
            return self.csem[key[1]]
        return self.dsem[key[1]][key[2]]

    def flush(self, final=False):
        ops = self.ops
        self.ops = []
        self.nops += len(ops)
        last_writer = {}
        readers = {}
        need = []
        lane_last = {q: [None] * len(l) for q, l in self.dsem.items()}
        for i, o in enumerate(ops):
            deps = {}
            for k in o.reads:
                w = last_writer.get(k)
                if w is not None:
                    deps[w] = 'raw'
            for k in o.writes:
                w = last_writer.get(k)
                if w is not None and w not in deps:
                    deps[w] = 'waw'
                for r in readers.get(k, ()):
                    if r not in deps:
                        deps[r] = 'war'
            nd = []
            for j, kind in deps.items():
                if j == i:
                    continue
                y = ops[j]
                if y.dma:
                    nd.append(j)
                elif y.eng == o.eng and not o.dma:
                    if o.eng != 'pe':
                        nd.append(j)
                else:
                    nd.append(j)
            if o.dma:
                q = o.eng
                lane = self.drr[q]
                self.drr[q] = (lane + 1) % len(self.dsem[q])
                o.lane = lane
                prev = lane_last[q][lane]
                if prev is not None:
                    nd.append(prev)
                lane_last[q][lane] = i
            best = {}
            nd2 = []
            for j in nd:
                y = ops[j]
                if y.dma:
                    nd2.append(j)
                else:
                    if best.get(y.eng, -1) < j:
                        best[y.eng] = j
            nd = nd2 + list(best.values())
            need.append(nd)
            for j in nd:
                ops[j].signal = True
            for k in o.writes:
                last_writer[k] = i
                readers[k] = []
            for k in o.reads:
                readers.setdefault(k, []).append(i)
        seen = set()
        for o in reversed(ops):
            if not o.dma and o.eng not in seen:
                seen.add(o.eng)
                o.signal = True
        known = {e: {} for e in self.ENG}
        for i, o in enumerate(ops):
            w = {}
            for j in need[i]:
                y = ops[j]
                w[y.semkey] = max(w.get(y.semkey, 0), y.count)
            kn = known[o.eng]
            o.waits = tuple((k, v) for k, v in w.items() if kn.get(k, 0) < v)
            for k, v in o.waits:
                kn[k] = v
            if o.dma:
                self.dcount[o.eng][o.lane] += 16
                o.count = self.dcount[o.eng][o.lane]
                o.semkey = ('d', o.eng, o.lane)
            elif o.signal:
                self.ccount[o.eng] += 1
                o.count = self.ccount[o.eng]
                o.semkey = ('c', o.eng)
        finals = []
        for e in self.CENG:
            finals.append((('c', e), self.ccount[e]))
        for q, l in self.dcount.items():
            for li, v in enumerate(l):
                finals.append((('d', q, li), v))
        per = {e: [o for o in ops if o.eng == e] for e in self.ENG}

        def run(e_name, eng):
            for o in per[e_name]:
                for k, v in o.waits:
                    eng.wait_ge(self._sem(k), v)
                ins = o.fn(eng)
                if o.dma:
                    ins.then_inc(self._sem(o.semkey), 16)
                elif o.signal:
                    ins.then_inc(self._sem(o.semkey), 1)
            kn = known[e_name]
            for k, v in finals:
                if v > 0 and kn.get(k, 0) < v:
                    eng.wait_ge(self._sem(k), v)

        with self.nc.Block() as block:
            @block.tensor
            def _(e):
                run("pe", e)

            @block.scalar
            def _(e):
                run("act", e)

            @block.vector
            def _(e):
                run("dve", e)

            @block.gpsimd
            def _(e):
                run("pool", e)

            @block.sync
            def _(e):
                run("sp", e)


def _consts():
    c = {}
    c["ident"] = np.eye(128, dtype=np.float32).astype(NPBF)
    s = np.arange(128)[:, None]; t = np.arange(128)[None, :]
    same = (s // 64) == (t // 64)
    cs = (t // 64) * 64
    c["mcum_f"] = (same * ((s <= t).astype(np.float32) - (s <= cs + 32).astype(np.float32))).astype(np.float32)
    c["mcum_b"] = (same * ((s >= t).astype(np.float32) - (s >= cs + 31).astype(np.float32))).astype(np.float32)
    sv = np.arange(128)
    self_f = np.zeros((128, 4), np.float32); self_b = np.zeros((128, 4), np.float32)
    for ch in range(2):
        inch = (sv // 64) == ch
        self_f[:, 2 * ch] = inch & (sv - 64 * ch <= 32)
        self_f[:, 2 * ch + 1] = inch
        self_b[:, 2 * ch] = inch & (sv - 64 * ch >= 31)
        self_b[:, 2 * ch + 1] = inch
    c["sel_f"] = self_f; c["sel_b"] = self_b
    s6 = np.arange(64)[:, None]; t6 = np.arange(64)[None, :]
    c["mask_f"] = (same & (s <= t)).astype(np.float32)
    c["mask_b"] = (same & (s >= t)).astype(np.float32)
    t1 = np.arange(64)[:, None, None]; t2 = np.arange(64)[None, :, None]; k1 = np.arange(64)[None, None, :]
    ph = 2 * np.pi * (t1 * k1 / 64.0 + t2 * k1 / 4096.0)
    c["w1"] = (np.concatenate([np.cos(ph), -np.sin(ph)], axis=-1) / 8.0).astype(np.float32).astype(NPBF)
    a = np.arange(64)[:, None]; b = np.arange(64)[None, :]
    th = 2 * np.pi * a * b / 64.0
    w2 = np.zeros((128, 128), np.float64)
    w2[:64, :64] = np.cos(th); w2[64:, :64] = np.sin(th); w2[:64, 64:] = -np.sin(th); w2[64:, 64:] = np.cos(th)
    c["w2"] = (w2 / 8.0).astype(np.float32).astype(NPBF)
    a = np.arange(128)[:, None]; b = np.arange(128)[None, :]
    th = 2 * np.pi * a * b / 128.0
    c["c128"] = (np.cos(th) / np.sqrt(128.0)).astype(np.float32).astype(NPBF)
    c["s128"] = (np.sin(th) / np.sqrt(128.0)).astype(np.float32).astype(NPBF)
    return c


CONST_SPECS = {
    "ident": ([128, 128], BF16), "mcum_f": ([128, 128], F32), "mcum_b": ([128, 128], F32),
    "sel_f": ([128, 4], F32), "sel_b": ([128, 4], F32),
    "mask_f": ([128, 128], F32), "mask_b": ([128, 128], F32),
    "w1": ([64, 64, 128], BF16), "w2": ([128, 128], BF16), "c128": ([128, 128], BF16), "s128": ([128, 128], BF16),
}

SMALL_SPECS = {
    "cc": ([128, 8, 2], F32),
    "bmT": ([128, 48], F32),
    "bm_bc": ([128, 2, 1024], F32),
    "n1T": ([128, 8], F32), "n2T": ([128, 8], F32),
    "nf_bc": ([128, 1024], F32),
    "lbf_bc": ([128, 2, 512], F32), "lbb_bc": ([128, 2, 512], F32),
    "gnT": ([128, 1], F32),
    "wdwT": ([128, 22, 9], F32), "bdwT": ([128, 22], F32),
}


def build(debug=None):
    nc = bass.Bass("TRN2", target_bir_lowering=False)
    dbg = debug or ()

    def din(name, shape, dt):
        return nc.dram_tensor(name, list(shape), dt, kind="ExternalInput").ap()

    def dscr(name, shape, dt):
        kind = "ExternalOutput" if name in dbg else "Internal"
        return nc.dram_tensor(name, list(shape), dt, kind=kind).ap()

    x = din("x", [T, D], F32)
    ctx = din("ctx", [256, D], F32)
    w_mod = din("w_mod", [D, 6 * D], F32)
    w_in = din("w_in", [D, 3072], F32)
    w_out = din("w_out", [D, D], F32)
    w_up = din("w_up", [D, 2 * DFF], F32)
    w_down = din("w_down", [DFF, D], F32)
    CI = {k: din(k, s, dt) for k, (s, dt) in CONST_SPECS.items()}
    SI = {k: din(k, s, dt) for k, (s, dt) in SMALL_SPECS.items()}
    out = nc.dram_tensor("out", [T, D], F32, kind="ExternalOutput").ap()

    QT = {d: dscr("QT" + d, [NTA, 128, 512], BF16) for d in "fb"}
    KT = {d: dscr("KT" + d, [NTA, 128, 512], BF16) for d in "fb"}
    KK = {d: dscr("KK" + d, [NTA, 128, 512], BF16) for d in "fb"}
    VV = dscr("VV", [NTA, 128, 512], BF16)
    GT = dscr("GT", [NTA, 128, 512], BF16)
    ZU = dscr("ZU", [T, 512], BF16)
    dbg_mod = dscr("dbg_mod", [128, 48 * 2 + 2048], F32) if "dbg_mod" in dbg else None
    dbg_cs = dscr("dbg_cs", [128, 2 * NTA * 24], F32) if "dbg_cs" in dbg else None

    es = ExitStack()
    with es:
        sem = lambda n: es.enter_context(nc.semaphore(n))
        csem = {e: sem("s_" + e) for e in Sched.CENG}
        dsem = {"sp": [sem("d_sp%d" % i) for i in range(8)], "pool": [sem("d_pl%d" % i) for i in range(4)]}
        S = Sched(nc, csem, dsem)

        def sb(stack, name, shape, dt):
            return stack.enter_context(nc.sbuf_tensor("sb_" + name, list(shape), dt))

        def ps(stack, name, shape, dt):
            return stack.enter_context(nc.psum_tensor("ps_" + name, list(shape), dt))

        modT = sb(es, "modT", [128, 48, 2], F32)
        gts = sb(es, "gts", [128, 2, 1024], F32)
        A1 = sb(es, "A1", [128, 8, 2], F32)
        A2 = sb(es, "A2", [128, 8], F32)
        ident = sb(es, "ident", [128, 128], BF16)
        CSE = {d: sb(es, "CSE" + d, [128, NTA, 16], F32) for d in "fb"}
        CSS = {d: sb(es, "CSS" + d, [128, NTA, 8], F32) for d in "fb"}

        pwin = ExitStack()
        pwin.__enter__()
        win = sb(pwin, "win", [128, 8, 3072], BF16)
        with ExitStack() as p0:
            cc = sb(p0, "cc", [128, 8, 2], F32)
            for kc in range(8):
                for half in range(2):
                    S.dma("pool", lambda e, kc=kc, half=half: e.dma_start(
                        out=win[:, kc, half * 1536:(half + 1) * 1536],
                        in_=w_in[kc * 128:(kc + 1) * 128, half * 1536:(half + 1) * 1536]),
                        writes=[("win", kc, half)])
            scT = sb(p0, "scT", [128, 8, 2], F32)
            sc_bc = sb(p0, "sc_bc", [128, 8, 128], F32)
            bmT = sb(p0, "bmT", [128, 48], F32)
            bm_bc = sb(p0, "bm_bc", [128, 2, 1024], F32)
            n1T = sb(p0, "n1T", [128, 8], F32)
            n2T = sb(p0, "n2T", [128, 8], F32)
            wm = [sb(p0, "wm%d" % i, [128, 8, 1024], F32) for i in range(2)]
            pm = ps(p0, "pm", [128, 8, 2], F32)
            pg = [ps(p0, "pg%d" % i, [128, 512], F32) for i in range(2)]

            S.dma("sp", lambda e: e.dma_start(out=cc[:], in_=SI["cc"]), writes=["cc"])
            S.dma("sp", lambda e: e.dma_start(out=bmT[:], in_=SI["bmT"]), writes=["bmT"])
            S.dma("sp", lambda e: e.dma_start(out=bm_bc[:], in_=SI["bm_bc"]), writes=["bm_bc"])
            S.dma("sp", lambda e: e.dma_start(out=n1T[:], in_=SI["n1T"]), writes=["n1T"])
            S.dma("sp", lambda e: e.dma_start(out=n2T[:], in_=SI["n2T"]), writes=["n2T"])
            S.dma("sp", lambda e: e.dma_start(out=ident[:], in_=CI["ident"]), writes=["ident"])
            S.op("act", lambda e: e.activation(out=scT[:], in_=cc[:], func=AF.Silu), reads=["cc"], writes=["scT"])
            S.op("dve", lambda e: e.tensor_copy(out=sc_bc[:], in_=scT[:, :, 0:1].to_broadcast([128, 8, 128])),
                 reads=["scT"], writes=["sc_bc"])
            for g in range(6):
                slot = g % 2
                wk = ("wm", slot)
                for kc in range(8):
                    S.dma("sp", lambda e, g=g, kc=kc, slot=slot: e.dma_start(
                        out=wm[slot][:, kc, :], in_=w_mod[kc * 128:(kc + 1) * 128, g * 1024:(g + 1) * 1024]),
                        writes=[(wk, kc)])
                if g in (2, 5):
                    gi = 0 if g == 2 else 1
                    for half in range(2):
                        for kc in range(8):
                            S.op("pe", lambda e, kc=kc, half=half, slot=slot: e.matmul(
                                pg[half][:], lhsT=sc_bc[:, kc, :], rhs=wm[slot][:, kc, half * 512:(half + 1) * 512],
                                start=(kc == 0), stop=(kc == 7)),
                                reads=["sc_bc", (wk, kc)], writes=[("pg", half)])
                        S.op("dve", lambda e, half=half, gi=gi: e.tensor_tensor(
                            out=gts[:, gi, half * 512:(half + 1) * 512], in0=pg[half][:],
                            in1=bm_bc[:, gi, half * 512:(half + 1) * 512], op=ALU.add),
                            reads=[("pg", half), "bm_bc"], writes=[("gts", gi, half)])
                else:
                    for j in range(8):
                        for kc in range(8):
                            S.op("pe", lambda e, kc=kc, j=j, slot=slot: e.matmul(
                                pm[:, j, :], lhsT=wm[slot][:, kc, j * 128:(j + 1) * 128], rhs=scT[:, kc, :],
                                start=(kc == 0), stop=(kc == 7)),
                                reads=["scT", (wk, kc)], writes=["pm"])
                    S.op("dve", lambda e, g=g: e.tensor_tensor(
                        out=modT[:, g * 8:(g + 1) * 8, :], in0=pm[:],
                        in1=bmT[:, g * 8:(g + 1) * 8].unsqueeze(2).to_broadcast([128, 8, 2]), op=ALU.add),
                        reads=["pm", "bmT"], writes=[("modT", g)])
            S.op("dve", lambda e: e.scalar_tensor_tensor(
                out=A1[:], in0=modT[:, 8:16, :], scalar=1.0, in1=n1T[:].unsqueeze(2).to_broadcast([128, 8, 2]),
                op0=ALU.add, op1=ALU.mult), reads=[("modT", 1), "n1T"], writes=["A1"])
            S.op("dve", lambda e: e.scalar_tensor_tensor(
                out=A2[:], in0=modT[:, 32:40, 0], scalar=1.0, in1=n2T[:],
                op0=ALU.add, op1=ALU.mult), reads=[("modT", 4), "n2T"], writes=["A2"])
            if dbg_mod is not None:
                S.dma("sp", lambda e: e.dma_start(out=dbg_mod[:, 0:96], in_=modT[:].rearrange("p a b -> p (a b)")),
                      reads=[("modT", g) for g in (0, 1, 3, 4)])
                S.dma("sp", lambda e: e.dma_start(out=dbg_mod[:, 96:96 + 2048], in_=gts[:].rearrange("p a b -> p (a b)")),
                      reads=[("gts", a, b) for a in range(2) for b in range(2)])
            S.flush()
        if "stop0" in dbg:
            return nc

        with ExitStack() as pa:
            lbraw = {d: sb(pa, "lbraw" + d, [128, 2, 512], F32) for d in "fb"}
            lbd = {d: sb(pa, "lbd" + d, [128, 512], F32) for d in "fb"}
            oml = {d: sb(pa, "oml" + d, [128, 512], F32) for d in "fb"}
            roml = {d: sb(pa, "roml" + d, [128, 512], F32) for d in "fb"}
            mcum = {d: sb(pa, "mcum" + d, [128, 128], F32) for d in "fb"}
            sel = {d: sb(pa, "sel" + d, [128, 4], F32) for d in "fb"}
            NS = 2
            xt = [sb(pa, "xt%d" % i, [128, 1024], F32) for i in range(NS)]
            junk = sb(pa, "junk", [128, 1024], BF16)
            ssq = [sb(pa, "ssq%d" % i, [128, 1], F32) for i in range(NS)]
            lnv = [sb(pa, "lnv%d" % i, [128, 1], F32) for i in range(NS)]
            rstd = [sb(pa, "rstd%d" % i, [128, 1], F32) for i in range(NS)]
            xn = [sb(pa, "xn%d" % i, [128, 1024], BF16) for i in range(NS)]
            hT = [sb(pa, "hT%d" % i, [128, 8, 128], BF16) for i in range(NS)]
            Eq = [sb(pa, "Eq%d" % i, [128, 512], F32) for i in range(NS)]
            Eg = [sb(pa, "Eg%d" % i, [128, 512], F32) for i in range(NS)]
            Ed = {d: [sb(pa, "E%s%d" % (d, i), [128, 512], F32) for i in range(NS)] for d in "fb"}
            q32 = [sb(pa, "q32_%d" % i, [128, 512], F32) for i in range(3)]
            k32 = {d: [sb(pa, "k32%s%d" % (d, i), [128, 512], F32) for i in range(NS)] for d in "fb"}
            lf32 = {d: [sb(pa, "lf32%s%d" % (d, i), [128, 512], F32) for i in range(NS)] for d in "fb"}
            e1 = {d: sb(pa, "e1" + d, [128, 512], F32) for d in "fb"}
            e2 = {d: sb(pa, "e2" + d, [128, 512], F32) for d in "fb"}
            qt = {d: [sb(pa, "qt%s%d" % (d, i), [128, 512], BF16) for i in range(NS)] for d in "fb"}
            kt = {d: [sb(pa, "kt%s%d" % (d, i), [128, 512], BF16) for i in range(NS)] for d in "fb"}
            stg = [sb(pa, "stg%d" % i, [128, 512], BF16) for i in range(4)]
            vb = [sb(pa, "vb%d" % i, [128, 512], BF16) for i in range(NS)]
            gb = [sb(pa, "gb%d" % i, [128, 512], BF16) for i in range(NS)]
            zub = [sb(pa, "zub%d" % i, [128, 512], BF16) for i in range(NS)]
            dif = [sb(pa, "dif%d" % i, [128, 16], F32) for i in range(NS)]
            cssb = [sb(pa, "cssb%d" % i, [128, 32], F32) for i in range(NS)]
            Z = [ps(pa, "Z%d" % i, [128, 512], F32) for i in range(4)]
            Pbank = ps(pa, "Pbank", [128, 512], F32)
            psT = Pbank[:].bitcast(BF16)
            DX = {"f": ps(pa, "DXf", [128, 512], F32), "b": ps(pa, "DXb", [128, 512], F32)}
            Y = ps(pa, "Y", [128, 512], F32)
            tps = Y[:, 0:256].bitcast(BF16)
            csps = Y[:, 256:288]

            for d, nm in (("f", "lbf_bc"), ("b", "lbb_bc")):
                S.dma("sp", lambda e, d=d, nm=nm: e.dma_start(out=lbraw[d][:], in_=SI[nm]), writes=["lbraw" + d])
                S.dma("sp", lambda e, d=d: e.dma_start(out=mcum[d][:], in_=CI["mcum_" + d]), writes=["mcum" + d])
                S.dma("sp", lambda e, d=d: e.dma_start(out=sel[d][:], in_=CI["sel_" + d]), writes=["sel" + d])
                S.op("dve", lambda e, d=d: e.tensor_tensor(out=lbd[d][:], in0=lbraw[d][:, 0, :], in1=lbraw[d][:, 1, :],
                                                          op=ALU.subtract), reads=["lbraw" + d], writes=["lbd" + d])
                S.op("act", lambda e, d=d: e.activation(out=oml[d][:], in_=lbd[d][:], func=AF.Sigmoid, scale=-1.0),
                     reads=["lbd" + d], writes=["oml" + d])

            bia = sb(pa, "bia", [128, 2, 8, 128], F32)
            for col in range(2):
                S.op("dve", lambda e, col=col: e.tensor_copy(out=bia[:, col, :, :], in_=modT[:, 0:8, col:col + 1].to_broadcast([128, 8, 128])),
                     reads=[("modT", 0)], writes=["bia"])
            stg_rr = [0]
            NTILES = int(os.environ.get('PA_TILES', NTA))
            ok = lambda t: 0 <= t < NTILES
            isx = lambda t: t >= 2

            def transposes(src, srckey):
                for h in range(4):
                    S.op("pe", lambda e, h=h: e.transpose(tps[:, h * 128:(h + 1) * 128], src[:, h * 128:(h + 1) * 128], ident[:]),
                         reads=[srckey, "ident"], writes=["Y"])

            def tstore(dst_ap):
                sj = stg_rr[0]; stg_rr[0] = (sj + 1) % 4
                S.op("dve", lambda e, sj=sj: e.tensor_copy(out=stg[sj][:], in_=tps), reads=["Y"], writes=[("stg", sj)])
                S.dma("sp", lambda e, sj=sj: e.dma_start(out=dst_ap, in_=stg[sj][:]), reads=[("stg", sj)])

            def S1(t):
                s = t % NS
                src = x[(t - 2) * 128:(t - 1) * 128, :] if isx(t) else ctx[t * 128:(t + 1) * 128, :]
                S.dma("pool", lambda e: e.dma_start(out=xt[s][:], in_=src), writes=[("xt", s)])
                S.op("act", lambda e: e.activation(out=junk[:], in_=xt[s][:], func=AF.Square, accum_out=ssq[s][:]),
                     reads=[("xt", s)], writes=["junk", ("ssq", s)])
                S.op("act", lambda e: e.activation(out=lnv[s][:], in_=ssq[s][:], func=AF.Ln, scale=1.0 / D, bias=EPS),
                     reads=[("ssq", s)], writes=[("lnv", s)])
                S.op("act", lambda e: e.activation(out=rstd[s][:], in_=lnv[s][:], func=AF.Exp, scale=-0.5),
                     reads=[("lnv", s)], writes=[("rstd", s)])
                S.op("dve", lambda e: e.tensor_scalar(out=xn[s][:], in0=xt[s][:], scalar1=rstd[s][:, 0:1], scalar2=None, op0=ALU.mult),
                     reads=[("xt", s), ("rstd", s)], writes=[("xn", s)])

            def S2(t):
                s = t % NS
                col = 0 if isx(t) else 1
                for kc in range(8):
                    S.op("pe", lambda e, kc=kc: e.transpose(psT[:, kc * 128:(kc + 1) * 128], xn[s][:, kc * 128:(kc + 1) * 128], ident[:]),
                         reads=[("xn", s), "ident"], writes=["psT"])
                for kc in range(8):
                    if True:
                        S.op("act", lambda e, kc=kc: e.activation(
                            out=hT[s][:, kc, :], in_=psT[:, kc * 128:(kc + 1) * 128], func=AF.Identity,
                            scale=A1[:, kc, col:col + 1], bias=modT[:, kc, col:col + 1]),
                            reads=["psT"], writes=[("hT", s, kc)])
                    else:
                        S.op("dve", lambda e, kc=kc: e.scalar_tensor_tensor(
                            out=hT[s][:, kc, :], in0=psT[:, kc * 128:(kc + 1) * 128], scalar=A1[:, kc, col:col + 1],
                            in1=bia[:, col, kc, :], op0=ALU.mult, op1=ALU.add),
                            reads=["psT", "bia"], writes=[("hT", s, kc)])

            def zblock(t, nb, bank):
                s = t % NS
                dst = Pbank if bank == "P" else Z[bank]
                wkey = "psT" if bank == "P" else ("Z", bank)
                for kc in range(8):
                    S.op("pe", lambda e, kc=kc: e.matmul(dst[:], lhsT=hT[s][:, kc, :], rhs=win[:, kc, nb * 512:(nb + 1) * 512],
                                                        start=(kc == 0), stop=(kc == 7)),
                         reads=[("hT", s, kc), ("win", kc, nb // 3)], writes=[wkey])

            def silu_from(t, bank, Ebuf, Ekey, out_ap, outkey):
                S.op("dve", lambda e: e.tensor_tensor(out=out_ap, in0=Ebuf[:], in1=Z[bank][:], op=ALU.mult),
                     reads=[Ekey, ("Z", bank)], writes=[outkey])

            def TT(t, which):
                if not (ok(t) and isx(t)):
                    return
                s = t % NS
                d = "fb"[which // 2]
                if which % 2 == 0:
                    transposes(qt[d][s], ("qt", d, s)); tstore(QT[d][t])
                else:
                    transposes(kt[d][s], ("kt", d, s)); tstore(KT[d][t])

            def S3E(t, tprev):
                s = t % NS
                if ok(t) and isx(t):
                    zblock(t, 0, 0)
                if ok(t):
                    zblock(t, 1, 1)
                    zblock(t, 2, 2)
                    zblock(t, 3, 3)
                    S.op("dve", lambda e: e.tensor_copy(out=vb[s][:], in_=Z[3][:]), reads=[("Z", 3)], writes=[("vb", s)])
                    S.dma("sp", lambda e: e.dma_start(out=VV[t], in_=vb[s][:]), reads=[("vb", s)])
                TT(tprev, 0)
                if ok(t) and isx(t):
                    zblock(t, 4, "P")
                TT(tprev, 1)
                if ok(t) and isx(t):
                    zblock(t, 5, 3)
                    S.op("dve", lambda e: e.tensor_copy(out=zub[s][:], in_=Z[3][:]), reads=[("Z", 3)], writes=[("zub", s)])
                    S.dma("sp", lambda e: e.dma_start(out=ZU[(t - 2) * 128:(t - 1) * 128, :], in_=zub[s][:]), reads=[("zub", s)])
                TT(tprev, 2)

            def Estage(t):
                s = t % NS
                q3 = t % 3
                if not ok(t):
                    return
                if isx(t):
                    S.op("act", lambda e: e.activation(out=Eq[s][:], in_=Z[0][:], func=AF.Sigmoid), reads=[("Z", 0)], writes=[("Eq", s)])
                S.op("act", lambda e: e.activation(out=Ed["f"][s][:], in_=Z[1][:], func=AF.Sigmoid, scale=-1.0), reads=[("Z", 1)], writes=[("Ef", s)])
                S.op("act", lambda e: e.activation(out=Ed["b"][s][:], in_=Z[2][:], func=AF.Sigmoid, scale=-1.0), reads=[("Z", 2)], writes=[("Eb", s)])
                if isx(t):
                    S.op("act", lambda e: e.activation(out=Eg[s][:], in_=Pbank[:], func=AF.Sigmoid), reads=["psT"], writes=[("Eg", s)])
                    S.op("dve", lambda e: e.tensor_tensor(out=q32[q3][:], in0=Eq[s][:], in1=Z[0][:], op=ALU.mult),
                         reads=[("Eq", s), ("Z", 0)], writes=[("q32", q3)])
                    S.op("dve", lambda e: e.tensor_tensor(out=gb[s][:], in0=Eg[s][:], in1=Pbank[:], op=ALU.mult),
                         reads=[("Eg", s), "psT"], writes=[("gb", s)])

            def S4(t):
                s = t % NS
                for d in "fb":
                    S.op("dve", lambda e, d=d: e.tensor_tensor(out=k32[d][s][:], in0=Ed[d][s][:], in1=oml[d][:], op=ALU.mult),
                         reads=[("E" + d, s), "oml" + d], writes=[("k32", d, s)])
                for d in "fb":
                    S.op("act", lambda e, d=d: e.activation(out=lf32[d][s][:], in_=k32[d][s][:], func=AF.Ln, scale=-1.0, bias=1.0),
                         reads=[("k32", d, s)], writes=[("lf32", d, s)])

            def S5(t):
                s = t % NS
                for d in "fb":
                    S.op("pe", lambda e, d=d: e.matmul(DX[d][:], lhsT=mcum[d][:], rhs=lf32[d][s][:], start=True, stop=True),
                         reads=["mcum" + d, ("lf32", d, s)], writes=[("DX", d)])
                for di, d in enumerate("fb"):
                    for h in range(4):
                        S.op("pe", lambda e, d=d, di=di, h=h: e.matmul(
                            csps[:, di * 16 + h * 4:di * 16 + (h + 1) * 4], lhsT=lf32[d][s][:, h * 128:(h + 1) * 128], rhs=sel[d][:],
                            start=True, stop=True), reads=["sel" + d, ("lf32", d, s)], writes=["Y"])
                S.op("dve", lambda e: e.tensor_copy(out=cssb[s][:], in_=csps), reads=["Y"], writes=[("cssb", s)])
                if isx(t):
                    transposes(gb[s], ("gb", s)); tstore(GT[t])

            def S6(t):
                s = t % NS
                q3 = t % 3
                for di, d in enumerate("fb"):
                    S.op("act", lambda e, d=d, di=di: e.activation(out=CSE[d][:, t, :], in_=cssb[s][:, di * 16:(di + 1) * 16], func=AF.Exp),
                         reads=[("cssb", s)], writes=[("CSE", d, t)])
                S.op("dve", lambda e: e.tensor_tensor(
                    out=dif[s][:], in0=cssb[s][:].rearrange("p (a b) -> p a b", b=2)[:, :, 1],
                    in1=cssb[s][:].rearrange("p (a b) -> p a b", b=2)[:, :, 0], op=ALU.subtract),
                    reads=[("cssb", s)], writes=[("dif", s)])
                for di, d in enumerate("fb"):
                    S.op("act", lambda e, d=d, di=di: e.activation(out=CSS[d][:, t, :], in_=dif[s][:, di * 8:(di + 1) * 8], func=AF.Exp),
                         reads=[("dif", s)], writes=[("CSS", d, t)])
                for d in "fb":
                    if isx(t):
                        S.op("act", lambda e, d=d: e.activation(out=e1[d][:], in_=DX[d][:], func=AF.Exp),
                             reads=[("DX", d)], writes=[("e1", d)])
                    S.op("act", lambda e, d=d: e.activation(out=e2[d][:], in_=DX[d][:], func=AF.Exp, scale=-1.0),
                         reads=[("DX", d)], writes=[("e2", d)])
                for d in "fb":
                    if isx(t):
                        S.op("dve", lambda e, d=d: e.tensor_tensor(out=qt[d][s][:], in0=q32[q3][:], in1=e1[d][:], op=ALU.mult),
                             reads=[("q32", q3), ("e1", d)], writes=[("qt", d, s)])
                    S.op("dve", lambda e, d=d: e.tensor_tensor(out=kt[d][s][:], in0=k32[d][s][:], in1=e2[d][:], op=ALU.mult),
                         reads=[("k32", d, s), ("e2", d)], writes=[("kt", d, s)])
                    S.dma("sp", lambda e, d=d: e.dma_start(out=KK[d][t], in_=kt[d][s][:]), reads=[("kt", d, s)])

            for n in range(-2, NTILES + 1):
                if ok(n + 1):
                    S2(n + 1)
                if ok(n - 1):
                    S6(n - 1)
                if ok(n):
                    S4(n)
                if ok(n + 2):
                    S1(n + 2)
                S3E(n + 1, n - 1)
                Estage(n + 1)
                if ok(n):
                    S5(n)
                TT(n - 1, 3)

            if dbg_cs is not None:
                for di, d in enumerate("fb"):
                    S.dma("sp", lambda e, d=d, di=di: e.dma_start(
                        out=dbg_cs[:, di * NTA * 24:di * NTA * 24 + NTA * 16], in_=CSE[d][:].rearrange("p a b -> p (a b)")),
                        reads=[("CSE", d, i) for i in range(NTILES)])
                    S.dma("sp", lambda e, d=d, di=di: e.dma_start(
                        out=dbg_cs[:, di * NTA * 24 + NTA * 16:(di + 1) * NTA * 24], in_=CSS[d][:].rearrange("p a b -> p (a b)")),
                        reads=[("CSS", d, i) for i in range(NTILES)])
            S.flush()
        pwin.close()
        if "stopA" in dbg:
            return nc
        arena2 = sb(es, "arena2", [128, 8 * T], BF16)

        X1D = dscr("X1D", [64, 128, 512], BF16)
        X1S = dscr("X1S", [T, D], F32)
        GG = dscr("GG", [NFB, 128, T], BF16)
        dbg_yt = dscr("dbg_yt", [128, 8 * T], BF16) if "dbg_yt" in dbg else None

        pbd = ExitStack()
        with pbd:
            YT = sb(pbd, "YT", [128, 8, T], BF16)
            wout = sb(pbd, "wout", [128, 8, 1024], BF16)
            with ExitStack() as pb:
                OT = arena2[:].bitcast(F32).rearrange("p (h t) -> p h t", h=4)
                QTt = {d: [sb(pb, "QTt%s%d" % (d, i), [128, 512], BF16) for i in range(2)] for d in "fb"}
                KTt = {d: [sb(pb, "KTt%s%d" % (d, i), [128, 512], BF16) for i in range(2)] for d in "fb"}
                Kc = {d: [sb(pb, "Kc%s%d" % (d, i), [128, 512], BF16) for i in range(2)] for d in "fb"}
                Vc = {d: [sb(pb, "Vc%s%d" % (d, i), [128, 512], BF16) for i in range(2)] for d in "fb"}
                ATm = {d: sb(pb, "ATm" + d, [128, 4, 128], BF16) for d in "fb"}
                St = {d: sb(pb, "St" + d, [128, 4, 128], F32) for d in "fb"}
                Sg = {d: sb(pb, "Sg" + d, [128, 4, 128], F32) for d in "fb"}
                Sp = {d: sb(pb, "Sp" + d, [128, 4, 128], BF16) for d in "fb"}
                mask = {d: sb(pb, "mask" + d, [128, 128], F32) for d in "fb"}
                ones_bf = sb(pb, "ones_bf", [128, 128], BF16)
                gnT = sb(pb, "gnT", [128, 1], F32)
                sq = [sb(pb, "sq%d" % i, [128, 512], BF16) for i in range(2)]
                rs = [sb(pb, "rs%d" % i, [128, 512], F32) for i in range(2)]
                rinv = [sb(pb, "rinv%d" % i, [128, 512], F32) for i in range(2)]
                t1 = [sb(pb, "t1_%d" % i, [128, 512], F32) for i in range(2)]
                GTt = [sb(pb, "GTt%d" % i, [128, 4, 128], BF16) for i in range(2)]
                atp = {d: ps(pb, "atp" + d, [128, 4, 128], F32) for d in "fb"}
                otp = {d: ps(pb, "otp" + d, [128, 4, 128], F32) for d in "fb"}
                upp = {d: ps(pb, "upp" + d, [128, 4, 128], F32) for d in "fb"}
                nps = [ps(pb, "nps%d" % i, [128, 512], F32) for i in range(2)]

                for d in "fb":
                    S.dma("sp", lambda e, d=d: e.dma_start(out=mask[d][:], in_=CI["mask_" + d]), writes=["mask" + d])
                    S.op("dve", lambda e, d=d: e.memset(St[d][:], 0.0), writes=[("St", d, h) for h in range(4)])
                    S.op("dve", lambda e, d=d: e.memset(Sp[d][:], 0.0), writes=[("Sp", d)])
                S.dma("sp", lambda e: e.dma_start(out=gnT[:], in_=SI["gnT"]), writes=["gnT"])
                S.op("dve", lambda e: e.memset(ones_bf[:], 1.0), writes=["ones_bf"])

                tseq = {"f": list(range(NTA)), "b": [1, 0] + list(range(NTA - 1, 1, -1))}
                cseq = {"f": (0, 1), "b": (1, 0)}
                NSTEP = int(os.environ.get("PB_STEPS", NTA))
                for step in range(NSTEP):
                    ts = step % 2
                    for d in "fb":
                        i = tseq[d][step]
                        isx = i >= 2
                        if isx:
                            S.dma("sp", lambda e, ts=ts, d=d, i=i: e.dma_start(out=QTt[d][ts][:], in_=QT[d][i]), writes=[("QTt", d, ts)])
                            S.dma("sp", lambda e, ts=ts, d=d, i=i: e.dma_start(out=KTt[d][ts][:], in_=KT[d][i]), writes=[("KTt", d, ts)])
                        S.dma("sp", lambda e, ts=ts, d=d, i=i: e.dma_start(out=Kc[d][ts][:], in_=KK[d][i]), writes=[("Kc", d, ts)])
                        S.dma("sp", lambda e, ts=ts, d=d, i=i: e.dma_start(out=Vc[d][ts][:], in_=VV[i]), writes=[("Vc", d, ts)])
                        if isx:
                            for h in range(4):
                                S.op("pe", lambda e, ts=ts, d=d, h=h: e.matmul(
                                    atp[d][:, h, :], lhsT=KTt[d][ts][:, h * 128:(h + 1) * 128],
                                    rhs=QTt[d][ts][:, h * 128:(h + 1) * 128], start=True, stop=True),
                                    reads=[("QTt", d, ts), ("KTt", d, ts)], writes=[("atp", d)])
                            S.op("dve", lambda e, ts=ts, d=d: e.tensor_tensor(
                                out=ATm[d][:], in0=atp[d][:], in1=mask[d][:].unsqueeze(1).to_broadcast([128, 4, 128]), op=ALU.mult),
                                reads=[("atp", d), "mask" + d], writes=[("ATm", d)])
                            for h in range(4):
                                S.op("pe", lambda e, ts=ts, d=d, h=h: e.matmul(
                                    otp[d][:, h, :], lhsT=Vc[d][ts][:, h * 128:(h + 1) * 128], rhs=ATm[d][:, h, :],
                                    start=(h == 0), stop=False, skip_group_check=True),
                                    reads=[("Vc", d, ts), ("ATm", d)], writes=[("otp", d)])
                    for ci in range(2):
                        for d in "fb":
                            i = tseq[d][step]
                            isx = i >= 2
                            c = cseq[d][ci]
                            for h in range(4):
                                S.op("pe", lambda e, ts=ts, d=d, h=h, c=c: e.matmul(
                                    upp[d][:, h, :], lhsT=Kc[d][ts][c * 64:(c + 1) * 64, h * 128:(h + 1) * 128],
                                    rhs=Vc[d][ts][c * 64:(c + 1) * 64, h * 128:(h + 1) * 128],
                                    start=True, stop=True), reads=[("Kc", d, ts), ("Vc", d, ts)], writes=[("upp", d)])
                            if isx:
                                for h in range(4):
                                    S.op("pe", lambda e, ts=ts, d=d, h=h, c=c, ci=ci: e.matmul(
                                        otp[d][:, h, c * 64:(c + 1) * 64], lhsT=Sp[d][:, h, :],
                                        rhs=QTt[d][ts][:, h * 128 + c * 64:h * 128 + c * 64 + 64],
                                        start=False, stop=(ci == 1 and h == 3), skip_group_check=True),
                                        reads=[("Sp", d), ("QTt", d, ts)], writes=[("otp", d)])
                            for h in range(4):
                                S.op("act", lambda e, ts=ts, d=d, h=h, i=i, c=c: e.activation(
                                    out=Sg[d][:, h, :], in_=upp[d][:, h, :], func=AF.Copy,
                                    scale=CSS[d][:, i, h * 2 + c:h * 2 + c + 1]),
                                    reads=[("upp", d)], writes=[("Sg", d, h)])
                            for h in range(4):
                                S.op("dve", lambda e, ts=ts, d=d, h=h, i=i, c=c: e.scalar_tensor_tensor(
                                    out=St[d][:, h, :], in0=St[d][:, h, :], scalar=CSE[d][:, i, h * 4 + 2 * c + 1:h * 4 + 2 * c + 2],
                                    in1=Sg[d][:, h, :], op0=ALU.mult, op1=ALU.add),
                                    reads=[("St", d, h), ("Sg", d, h)], writes=[("St", d, h)])
                            if ci == 0:
                                ni, ncn = i, cseq[d][1]
                            elif step + 1 < NTA:
                                ni, ncn = tseq[d][step + 1], cseq[d][0]
                            else:
                                ni = None
                            if ni is not None and ni >= 2:
                                S.op("pool", lambda e, ts=ts, d=d, ni=ni, ncn=ncn: e.tensor_tensor(
                                    out=Sp[d][:], in0=St[d][:],
                                    in1=CSE[d][:, ni, :].rearrange("p (h k) -> p h k", k=4)[:, :, 2 * ncn:2 * ncn + 1].to_broadcast([128, 4, 128]),
                                    op=ALU.mult), reads=[("St", d, h) for h in range(4)], writes=[("Sp", d)])
                    for d in "fb":
                        i = tseq[d][step]
                        if i >= 2:
                            jt = i - 2
                            tok0 = jt * 128
                            first = (d == "f" and jt < 16) or (d == "b" and jt >= 16)
                            okeys = [("OT", 2 * jt), ("OT", 2 * jt + 1)]
                            if first:
                                S.op("act", lambda e, ts=ts, d=d, tok0=tok0: e.activation(
                                    out=OT[:, :, tok0:tok0 + 128], in_=otp[d][:], func=AF.Copy),
                                    reads=[("otp", d)], writes=okeys)
                            else:
                                S.op("dve", lambda e, ts=ts, d=d, tok0=tok0: e.tensor_tensor(
                                    out=OT[:, :, tok0:tok0 + 128], in0=otp[d][:], in1=OT[:, :, tok0:tok0 + 128], op=ALU.add),
                                    reads=[("otp", d)] + okeys, writes=okeys)
                for bi in range(32 if NSTEP == NTA else 0):
                    h = bi // 8
                    tb = bi % 8
                    s2 = bi % 2
                    blk = OT[:, h, tb * 512:(tb + 1) * 512]
                    rk = [("OT", tb * 8 + q) for q in range(8)]
                    S.dma("sp", lambda e, s2=s2, h=h, tb=tb: e.dma_start(
                        out=GTt[s2][:], in_=GT[2 + 4 * tb:2 + 4 * tb + 4, :, h * 128:(h + 1) * 128].rearrange("t p k -> p t k")),
                        writes=[("GTt", s2)])
                    S.op("act", lambda e, s2=s2, blk=blk: e.activation(out=sq[s2][:], in_=blk, func=AF.Square),
                         reads=rk, writes=[("sq", s2)])
                    S.op("pe", lambda e, s2=s2: e.matmul(nps[s2][:], lhsT=ones_bf[:], rhs=sq[s2][:], start=True, stop=True),
                         reads=["ones_bf", ("sq", s2)], writes=[("nps", s2)])
                    S.op("act", lambda e, s2=s2: e.activation(out=rs[s2][:], in_=nps[s2][:], func=AF.Ln, scale=1.0 / 128, bias=EPS),
                         reads=[("nps", s2)], writes=[("rs", s2)])
                    S.op("act", lambda e, s2=s2: e.activation(out=rinv[s2][:], in_=rs[s2][:], func=AF.Exp, scale=-0.5),
                         reads=[("rs", s2)], writes=[("rinv", s2)])
                    S.op("dve", lambda e, s2=s2, blk=blk: e.tensor_tensor(out=t1[s2][:], in0=blk, in1=rinv[s2][:], op=ALU.mult),
                         reads=rk + [("rinv", s2)], writes=[("t1", s2)])
                    S.op("dve", lambda e, s2=s2, h=h, tb=tb: e.scalar_tensor_tensor(
                        out=YT[:, h, tb * 512:(tb + 1) * 512], in0=t1[s2][:], scalar=gnT[:, 0:1],
                        in1=GTt[s2][:].rearrange("p t k -> p (t k)"), op0=ALU.mult, op1=ALU.mult),
                        reads=[("t1", s2), ("GTt", s2), "gnT"], writes=[("YT", h, tb)])
                S.flush()
            if "stopB" in dbg:
                if dbg_yt is not None:
                    S.dma("sp", lambda e: e.dma_start(out=dbg_yt[:, 0:4 * T], in_=YT[:, 0:4, :].rearrange("p a b -> p (a b)")))
                    S.flush()
                return nc

            with ExitStack() as pc1:
                zus = arena2[0:64, :].rearrange("p (b c) -> p b c", c=512)
                w1 = sb(pc1, "w1", [64, 64, 128], BF16)
                x1sb = [sb(pc1, "x1sb%d" % i, [128, 512], BF16) for i in range(2)]
                x1ps = [ps(pc1, "x1ps%d" % i, [128, 512], F32) for i in range(2)]
                for q in range(4):
                    S.dma("sp", lambda e, q=q: e.dma_start(
                        out=zus[:, q * 16:(q + 1) * 16, :],
                        in_=ZU.rearrange("(a b) c -> a b c", b=64)[:, q * 16:(q + 1) * 16, :]), writes=[("zus", q)])
                S.dma("sp", lambda e: e.dma_start(out=w1[:], in_=CI["w1"]), writes=["w1"])
                for t2 in range(64):
                    s2 = t2 % 2
                    S.op("pe", lambda e, t2=t2, s2=s2: e.matmul(x1ps[s2][:], lhsT=w1[:, t2, :], rhs=zus[:, t2, :], start=True, stop=True),
                         reads=["w1", ("zus", t2 // 16)], writes=[("x1ps", s2)])
                    if s2 == 0:
                        S.op("act", lambda e, s2=s2: e.activation(out=x1sb[s2][:], in_=x1ps[s2][:], func=AF.Copy),
                             reads=[("x1ps", s2)], writes=[("x1sb", s2)])
                    else:
                        S.op("dve", lambda e, s2=s2: e.tensor_copy(out=x1sb[s2][:], in_=x1ps[s2][:]),
                             reads=[("x1ps", s2)], writes=[("x1sb", s2)])
                    S.dma("sp", lambda e, t2=t2, s2=s2: e.dma_start(out=X1D[t2], in_=x1sb[s2][:]), reads=[("x1sb", s2)])
                S.flush()
            with ExitStack() as pc2:
                FT = arena2[:].rearrange("p (g r k) -> p g r k", g=4, r=2)
                w2 = sb(pc2, "w2", [128, 128], BF16)
                c128 = sb(pc2, "c128", [128, 128], BF16)
                s128 = sb(pc2, "s128", [128, 128], BF16)
                dd = [sb(pc2, "dd%d" % i, [128, 512], BF16) for i in range(3)]
                fps = [ps(pc2, "fps%d" % i, [128, 4, 128], F32) for i in range(2)]
                yps = [ps(pc2, "yps%d" % i, [128, 512], F32) for i in range(2)]
                dmy2 = ps(pc2, "dmy2", [128, 512], F32)
                S.dma("sp", lambda e: e.dma_start(out=w2[:], in_=CI["w2"]), writes=["w2"])
                S.dma("sp", lambda e: e.dma_start(out=c128[:], in_=CI["c128"]), writes=["c128"])
                S.dma("sp", lambda e: e.dma_start(out=s128[:], in_=CI["s128"]), writes=["s128"])
                for kc in range(8):
                    S.dma("pool", lambda e, kc=kc: e.dma_start(out=wout[:, kc, :], in_=w_out[kc * 128:(kc + 1) * 128, :]),
                          writes=[("wout", kc)])
                for k1 in range(64):
                    s3 = k1 % 3
                    s2 = k1 % 2
                    S.dma("sp", lambda e, k1=k1, s3=s3: e.dma_start(out=dd[s3][0:64, :], in_=X1D[:, k1, :]), writes=[("dd", s3, 0)])
                    S.dma("sp", lambda e, k1=k1, s3=s3: e.dma_start(out=dd[s3][64:128, :], in_=X1D[:, 64 + k1, :]), writes=[("dd", s3, 1)])
                    for q in range(2):
                        S.op("pe", lambda e, q=q: e.matmul(dmy2[:], lhsT=c128[:], rhs=YT[:, q, 0:512], start=True, stop=True),
                             reads=["c128"], writes=["dmy2"])
                    for g in range(4):
                        S.op("pe", lambda e, g=g, s3=s3, s2=s2: e.matmul(
                            fps[s2][:, g, :], lhsT=dd[s3][:, g * 128:(g + 1) * 128], rhs=w2[:], start=True, stop=True),
                            reads=[("dd", s3, 0), ("dd", s3, 1), "w2"], writes=[("fps", s2)])
                    FTv = FT[:].rearrange("p g r (b a) -> p g r b a", a=64)[:, :, :, :, k1]
                    if s2 == 0:
                        S.op("act", lambda e, s2=s2, FTv=FTv: e.activation(
                            out=FTv, in_=fps[s2][:].rearrange("p g (r b) -> p g r b", r=2), func=AF.Copy),
                            reads=[("fps", s2)], writes=[("FT", k1)])
                    else:
                        S.op("dve", lambda e, s2=s2, FTv=FTv: e.tensor_copy(
                            out=FTv, in_=fps[s2][:].rearrange("p g (r b) -> p g r b", r=2)),
                            reads=[("fps", s2)], writes=[("FT", k1)])
                allft = [("FT", k1) for k1 in range(64)]
                for g in range(4):
                    for tb in range(8):
                        s2 = (g * 8 + tb) % 2
                        S.op("pe", lambda e, g=g, tb=tb, s2=s2: e.matmul(
                            yps[s2][:], lhsT=c128[:], rhs=FT[:, g, 0, tb * 512:(tb + 1) * 512], start=True, stop=False),
                            reads=allft + ["c128"], writes=[("yps", s2)])
                        S.op("pe", lambda e, g=g, tb=tb, s2=s2: e.matmul(
                            yps[s2][:], lhsT=s128[:], rhs=FT[:, g, 1, tb * 512:(tb + 1) * 512], start=False, stop=True),
                            reads=allft + ["s128"], writes=[("yps", s2)])
                        S.op("act", lambda e, g=g, tb=tb, s2=s2: e.activation(
                            out=YT[:, 4 + g, tb * 512:(tb + 1) * 512], in_=yps[s2][:], func=AF.Copy),
                            reads=[("yps", s2)], writes=[("YT", 4 + g, tb)])
                if dbg_yt is not None:
                    S.dma("sp", lambda e: e.dma_start(out=dbg_yt, in_=YT[:].rearrange("p a b -> p (a b)")),
                          reads=[("YT", a, b) for a in range(4, 8) for b in range(8)])
                S.flush()
            if "stopC" in dbg:
                return nc

            h2T = arena2[:].rearrange("p (k t) -> p k t", k=8)
            with ExitStack() as pd:
                xt = [sb(pd, "dxt%d" % i, [128, 1024], F32) for i in range(3)]
                tm = [sb(pd, "dtm%d" % i, [128, 1024], F32) for i in range(2)]
                junk = sb(pd, "djunk", [128, 1024], BF16)
                ssq = [sb(pd, "dssq%d" % i, [128, 1], F32) for i in range(2)]
                rst = [sb(pd, "drst%d" % i, [128, 1], F32) for i in range(2)]
                rstd = [sb(pd, "drstd%d" % i, [128, 1], F32) for i in range(2)]
                xn = [sb(pd, "dxn%d" % i, [128, 1024], BF16) for i in range(2)]
                aps = [ps(pd, "aps%d" % i, [128, 512], F32) for i in range(4)]
                psT = [ps(pd, "dpsT%d" % i, [128, 1024], BF16) for i in range(2)]
                dmy = ps(pd, "dmy", [128, 512], F32)
                def d_load(i):
                    s3 = i % 3
                    S.dma("pool", lambda e: e.dma_start(out=xt[s3][:], in_=x[i * 128:(i + 1) * 128, :]), writes=[("xt", s3)])

                def d_mm(i):
                    s = i % 2
                    s3 = i % 3
                    for half in range(2):
                        pb_ = aps[s * 2 + half]
                        for kc in range(8):
                            S.op("pe", lambda e, kc=kc, half=half, pb_=pb_: e.matmul(
                                pb_[:], lhsT=YT[:, kc, i * 128:(i + 1) * 128], rhs=wout[:, kc, half * 512:(half + 1) * 512],
                                start=(kc == 0), stop=(kc == 7)), writes=[("aps", s, half)])
                        S.op("dve", lambda e, half=half, pb_=pb_: e.tensor_tensor(
                            out=tm[s][:, half * 512:(half + 1) * 512], in0=pb_[:], in1=gts[:, 0, half * 512:(half + 1) * 512], op=ALU.mult),
                            reads=[("aps", s, half)], writes=[("tm", s, half)])
                        S.op("pool", lambda e, half=half: e.tensor_tensor(
                            out=xt[s3][:, half * 512:(half + 1) * 512], in0=tm[s][:, half * 512:(half + 1) * 512],
                            in1=xt[s3][:, half * 512:(half + 1) * 512], op=ALU.add),
                            reads=[("tm", s, half), ("xt", s3)], writes=[("xt", s3)])
                    S.dma("sp", lambda e: e.dma_start(out=X1S[i * 128:(i + 1) * 128, :], in_=xt[s3][:]), reads=[("xt", s3)])

                def d_post(i):
                    s = i % 2
                    s3 = i % 3
                    S.op("act", lambda e: e.activation(out=junk[:], in_=xt[s3][:], func=AF.Square, accum_out=ssq[s][:]),
                         reads=[("xt", s3)], writes=["junk", ("ssq", s)])
                    S.op("act", lambda e: e.activation(out=rst[s][:], in_=ssq[s][:], func=AF.Sqrt, scale=1.0 / D, bias=EPS),
                         reads=[("ssq", s)], writes=[("rst", s)])
                    S.op("dve", lambda e: e.reciprocal(out=rstd[s][:], in_=rst[s][:]), reads=[("rst", s)], writes=[("rstd", s)])
                    S.op("dve", lambda e: e.tensor_scalar(out=xn[s][:], in0=xt[s3][:], scalar1=rstd[s][:, 0:1], scalar2=None,
                                                         op0=ALU.mult), reads=[("xt", s3), ("rstd", s)], writes=[("xn", s)])
                    for q in range(8):
                        S.op("pe", lambda e, q=q: e.matmul(dmy[:], lhsT=YT[:, q, 0:128], rhs=wout[:, q, 0:512],
                                                          start=True, stop=True), writes=["dmy"])
                    for kc in range(8):
                        S.op("pe", lambda e, kc=kc: e.transpose(psT[s][:, kc * 128:(kc + 1) * 128],
                                                               xn[s][:, kc * 128:(kc + 1) * 128], ident[:]),
                             reads=[("xn", s)], writes=[("psT", s)])
                    for kc in range(8):
                        S.op("act", lambda e, kc=kc: e.activation(
                            out=h2T[:, kc, i * 128:(i + 1) * 128], in_=psT[s][:, kc * 128:(kc + 1) * 128], func=AF.Identity,
                            scale=A2[:, kc:kc + 1], bias=modT[:, 24 + kc, 0:1]),
                            reads=[("psT", s)], writes=[("h2T", i, kc)])

                d_load(0)
                d_load(1)
                d_mm(0)
                for i in range(NT):
                    if i + 2 < NT:
                        d_load(i + 2)
                    if i + 1 < NT:
                        d_mm(i + 1)
                    d_post(i)
                S.flush()
            pbd.close()
            if "stopD" in dbg:
                return nc

            pw = ExitStack()
            pw.__enter__()
            wdn = sb(pw, "wdn", [128, NFB, 1024], BF16)
            with ExitStack() as pe1:
                wa = [sb(pe1, "wa%d" % i, [128, 8, 128], BF16) for i in range(2)]
                wu = [sb(pe1, "wu%d" % i, [128, 8, 128], BF16) for i in range(2)]
                dg = [sb(pe1, "dg%d" % i, [128, 9, 128], BF16) for i in range(2)]
                apad = [sb(pe1, "apad%d" % i, [128, 66, 66], BF16) for i in range(2)]
                ggT = [sb(pe1, "ggT%d" % i, [128, T], BF16) for i in range(2)]
                ga = [sb(pe1, "ga%d" % i, [128, 512], F32) for i in range(2)]
                identf = sb(pe1, "identf", [128, 128], F32)
                wdwT = sb(pe1, "wdwT", [128, 22, 9], F32)
                bdwT = sb(pe1, "bdwT", [128, 22], F32)
                a_ps = [ps(pe1, "a_ps%d" % i, [128, 512], F32) for i in range(2)]
                c_ps = [ps(pe1, "c_ps%d" % i, [128, 8, 64], F32) for i in range(2)]
                u_ps = [ps(pe1, "u_ps%d" % i, [128, 512], F32) for i in range(2)]
                S.dma("sp", lambda e: e.dma_start(out=wdwT[:], in_=SI["wdwT"]), writes=["wdwT"])
                S.dma("sp", lambda e: e.dma_start(out=bdwT[:], in_=SI["bdwT"]), writes=["bdwT"])
                S.op("dve", lambda e: e.tensor_copy(out=identf[:], in_=ident[:]), writes=["identf"])
                for b2 in range(2):
                    S.op("pool", lambda e, b2=b2: e.memset(apad[b2][:], 0.0), writes=[("apad", b2, tb) for tb in range(8)])
                GELU = AF.Gelu_apprx_tanh
                for j in range(int(os.environ.get("PE_FB", NFB))):
                    s = j % 2
                    S.dma("pool", lambda e, s=s, j=j: e.dma_start(
                        out=wa[s][:], in_=w_up[:, j * 128:(j + 1) * 128].rearrange("(kc p) n -> p kc n", p=128)), writes=[("wa", s)])
                    S.dma("pool", lambda e, s=s, j=j: e.dma_start(
                        out=wu[s][:], in_=w_up[:, DFF + j * 128:DFF + (j + 1) * 128].rearrange("(kc p) n -> p kc n", p=128)), writes=[("wu", s)])
                    if j == 1:
                        for jj in range(NFB):
                            S.dma("pool", lambda e, jj=jj: e.dma_start(out=wdn[:, jj, :], in_=w_down[jj * 128:(jj + 1) * 128, :]),
                                  writes=[("wdn", jj)])
                    for tap in range(9):
                        S.op("act", lambda e, s=s, j=j, tap=tap: e.activation(
                            out=dg[s][:, tap, :], in_=identf[:], func=AF.Copy, scale=wdwT[:, j, tap:tap + 1]),
                            reads=["identf", "wdwT"], writes=[("dg", s)])
                    for tb in range(8):
                        p2 = tb % 2
                        for kc in range(8):
                            S.op("pe", lambda e, s=s, kc=kc, tb=tb, p2=p2: e.matmul(
                                a_ps[p2][:], lhsT=wa[s][:, kc, :], rhs=h2T[:, kc, tb * 512:(tb + 1) * 512],
                                start=(kc == 0), stop=(kc == 7)), reads=[("wa", s)], writes=[("a_ps", p2)])
                        S.op("act", lambda e, s=s, tb=tb, p2=p2: e.activation(
                            out=apad[s][:, 1 + 8 * tb:9 + 8 * tb, 1:65], in_=a_ps[p2][:].rearrange("p (r c) -> p r c", c=64), func=AF.Copy),
                            reads=[("a_ps", p2)], writes=[("apad", s, tb)])
                    for tb in range(8):
                        p2 = tb % 2
                        rk = [("apad", s, q) for q in (tb - 1, tb, tb + 1) if 0 <= q < 8]
                        for tap in range(9):
                            dr, dc = tap // 3, tap % 3
                            S.op("pe", lambda e, s=s, tb=tb, p2=p2, tap=tap, dr=dr, dc=dc: e.matmul(
                                c_ps[p2][:], lhsT=dg[s][:, tap, :], rhs=apad[s][:, 8 * tb + dr:8 * tb + dr + 8, dc:dc + 64],
                                start=(tap == 0), stop=(tap == 8)), reads=rk + [("dg", s)], writes=[("c_ps", p2)])
                        S.op("act", lambda e, s=s, j=j, p2=p2: e.activation(
                            out=ga[p2][:], in_=c_ps[p2][:].rearrange("p r c -> p (r c)"), func=GELU, bias=bdwT[:, j:j + 1]),
                            reads=[("c_ps", p2), "bdwT"], writes=[("ga", p2)])
                        for kc in range(8):
                            S.op("pe", lambda e, s=s, kc=kc, tb=tb, p2=p2: e.matmul(
                                u_ps[p2][:], lhsT=wu[s][:, kc, :], rhs=h2T[:, kc, tb * 512:(tb + 1) * 512],
                                start=(kc == 0), stop=(kc == 7)), reads=[("wu", s)], writes=[("u_ps", p2)])
                        S.op("dve", lambda e, s=s, tb=tb, p2=p2: e.tensor_tensor(
                            out=ggT[s][:, tb * 512:(tb + 1) * 512], in0=ga[p2][:], in1=u_ps[p2][:], op=ALU.mult),
                            reads=[("ga", p2), ("u_ps", p2)], writes=[("ggT", s, tb)])
                    S.dma("sp", lambda e, s=s, j=j: e.dma_start(out=GG[j], in_=ggT[s][:]), reads=[("ggT", s, tb) for tb in range(8)])
                S.flush()

            with ExitStack() as pe2:
                nf_bc = sb(pe2, "nf_bc", [128, 1024], F32)
                ggt = [sb(pe2, "ggt%d" % i, [128, NFB, 512], BF16) for i in range(2)]
                xt = [sb(pe2, "ext%d" % i, [128, 1024], F32) for i in range(2)]
                tm = [sb(pe2, "etm%d" % i, [128, 1024], F32) for i in range(2)]
                junk = sb(pe2, "ejunk", [128, 1024], BF16)
                ssq = [sb(pe2, "essq%d" % i, [128, 1], F32) for i in range(2)]
                rst = [sb(pe2, "erst%d" % i, [128, 1], F32) for i in range(2)]
                rstd = [sb(pe2, "erstd%d" % i, [128, 1], F32) for i in range(2)]
                ot = [sb(pe2, "eot%d" % i, [128, 1024], F32) for i in range(2)]
                dps = [ps(pe2, "dps%d" % i, [128, 512], F32) for i in range(4)]
                S.dma("sp", lambda e: e.dma_start(out=nf_bc[:], in_=SI["nf_bc"]), writes=["nf_bc"])
                def ggload(tb):
                    gs = tb % 2
                    for jq in range(2):
                        S.dma("pool", lambda e, gs=gs, tb=tb, jq=jq: e.dma_start(
                            out=ggt[gs][:, jq * 11:(jq + 1) * 11, :],
                            in_=GG[jq * 11:(jq + 1) * 11, :, tb * 512:(tb + 1) * 512].rearrange("j p t -> p j t")),
                            writes=[("ggt", gs, jq)])
                ggload(0)
                S.dma("pool", lambda e: e.dma_start(out=xt[0][:], in_=X1S[0:128, :]), writes=[("xt", 0)])
                for i in range(NT):
                    s = i % 2
                    tb = i // 4
                    gs = tb % 2
                    if i % 4 == 0 and tb + 1 < 8:
                        ggload(tb + 1)
                    if i + 1 < NT:
                        S.dma("pool", lambda e, s=s, i=i: e.dma_start(out=xt[1 - s][:], in_=X1S[(i + 1) * 128:(i + 2) * 128, :]),
                              writes=[("xt", 1 - s)])
                    to = (i % 4) * 128
                    for half in range(2):
                        pb_ = dps[s * 2 + half]
                        for j in range(NFB):
                            S.op("pe", lambda e, gs=gs, j=j, half=half, pb_=pb_, to=to: e.matmul(
                                pb_[:], lhsT=ggt[gs][:, j, to:to + 128], rhs=wdn[:, j, half * 512:(half + 1) * 512],
                                start=(j == 0), stop=(j == NFB - 1)),
                                reads=[("ggt", gs, j // 11)], writes=[("dps", s, half)])
                        S.op("dve", lambda e, s=s, half=half, pb_=pb_: e.tensor_tensor(
                            out=tm[s][:, half * 512:(half + 1) * 512], in0=pb_[:], in1=gts[:, 1, half * 512:(half + 1) * 512], op=ALU.mult),
                            reads=[("dps", s, half)], writes=[("tm", s, half)])
                        S.op("pool", lambda e, s=s, half=half: e.tensor_tensor(
                            out=xt[s][:, half * 512:(half + 1) * 512], in0=tm[s][:, half * 512:(half + 1) * 512],
                            in1=xt[s][:, half * 512:(half + 1) * 512], op=ALU.add),
                            reads=[("tm", s, half), ("xt", s)], writes=[("xt", s)])
                    S.op("act", lambda e, s=s: e.activation(out=junk[:], in_=xt[s][:], func=AF.Square, accum_out=ssq[s][:]),
                         reads=[("xt", s)], writes=["junk", ("ssq", s)])
                    S.op("act", lambda e, s=s: e.activation(out=rst[s][:], in_=ssq[s][:], func=AF.Sqrt, scale=1.0 / D, bias=EPS),
                         reads=[("ssq", s)], writes=[("rst", s)])
                    S.op("dve", lambda e, s=s: e.reciprocal(out=rstd[s][:], in_=rst[s][:]), reads=[("rst", s)], writes=[("rstd", s)])
                    S.op("dve", lambda e, s=s: e.scalar_tensor_tensor(
                        out=ot[s][:], in0=xt[s][:], scalar=rstd[s][:, 0:1], in1=nf_bc[:], op0=ALU.mult, op1=ALU.mult),
                        reads=[("xt", s), ("rstd", s), "nf_bc"], writes=[("ot", s)])
                    S.dma("sp", lambda e, s=s, i=i: e.dma_start(out=out[i * 128:(i + 1) * 128, :], in_=ot[s][:]), reads=[("ot", s)])
                S.flush()
            pw.close()
    return nc


def _host_inputs(inputs, b, consts):
    f = np.float32
    m = {}
    m["x"] = np.ascontiguousarray(inputs["x"][b], f)
    m["ctx"] = np.ascontiguousarray(inputs["ctx"][b], f)
    m["w_mod"] = np.ascontiguousarray(inputs["w_mod"][0], f)
    m["w_in"] = np.ascontiguousarray(inputs["w_in"][0], f)
    m["w_out"] = np.ascontiguousarray(inputs["w_out"][0], f)
    m["w_up"] = np.ascontiguousarray(inputs["w_up"][0], f)
    m["w_down"] = np.ascontiguousarray(inputs["w_down"][0], f)
    fm = lambda v: np.ascontiguousarray(np.asarray(v, f).reshape(-1, 128).T)
    m["cc"] = np.ascontiguousarray(np.stack([fm(inputs["c"][b]), fm(inputs["c_ctx"])], axis=-1))
    m["bmT"] = fm(inputs["b_mod"][0])
    bm = np.asarray(inputs["b_mod"][0], f)
    m["bm_bc"] = np.ascontiguousarray(np.broadcast_to(
        np.stack([bm[2048:3072], bm[5120:6144]])[None], (128, 2, 1024)))
    m["n1T"] = fm(inputs["norm1"][0]); m["n2T"] = fm(inputs["norm2"][0])
    m["nf_bc"] = np.ascontiguousarray(np.broadcast_to(np.asarray(inputs["norm_f"], f)[None], (128, 1024)))
    m["lbf_bc"] = np.ascontiguousarray(np.broadcast_to(np.asarray(inputs["lb_fwd"], f)[None], (128, 2, 512)))
    m["lbb_bc"] = np.ascontiguousarray(np.broadcast_to(np.asarray(inputs["lb_bwd"], f)[None], (128, 2, 512)))
    m["gnT"] = np.ascontiguousarray(np.asarray(inputs["hgrn_norm"][0], f).reshape(128, 1))
    wdw = np.asarray(inputs["w_dw"][0], f).reshape(9, NFB, 128)
    m["wdwT"] = np.ascontiguousarray(wdw.transpose(2, 1, 0))
    m["bdwT"] = fm(inputs["b_dw"][0])
    m.update(consts)
    return m


def kernel(**inputs):
    consts = _consts()
    nc = build()
    in_maps = [_host_inputs(inputs, b, consts) for b in range(8)]
    res = run_bass_kernel_spmd(nc, in_maps, core_ids=list(range(8)))
    return np.stack([np.asarray(r["out"], np.float32) for r in res.results], axis=0)
```
